# Optimizing a Trainium2 kernel written in Bass

```python
import jax, jax.numpy as jnp
from jax import lax
import numpy as np

D_MODEL = 1024
BATCH = 16
SEQ = 4096
DEPTH = 4

CTX_LEN = 256
GRID_W = 64
D_FF = 2816
N_MOD = 9
EPS = 1e-6
D_MIX = 1024
F_GROUPS = 4
F_GROUP_DIM = 64
F_W = F_GROUPS * F_GROUP_DIM
CONV_HEADS = 4
CONV_HEAD_DIM = 64
CONV_W = CONV_HEADS * CONV_HEAD_DIM
CONV_K = 3
MLA_HEADS = 8
QK_NOPE = 64
QK_ROPE = 32
V_DIM = 64
Q_LORA = 384
KV_LORA = 256
MLA_OUT = MLA_HEADS * V_DIM
ROPE_BASE = 10000.0
AXIS_ROPE = QK_ROPE // 2
Q_BLOCK = 128
ATTN_SCALE = (QK_NOPE + QK_ROPE) ** -0.5
OFF_F = 0
OFF_CB = OFF_F + F_W
OFF_CC = OFF_CB + CONV_W
OFF_CX = OFF_CC + CONV_W
OFF_Q = OFF_CX + CONV_W
OFF_KV = OFF_Q + Q_LORA
OFF_KR = OFF_KV + KV_LORA
IN_COLS = OFF_KR + QK_ROPE

kernel_name = "hybrid_fourier_conv_mla_macaron_dit"


def rmsnorm(x, g):
    xf = x.astype(jnp.float32)
    y = xf * lax.rsqrt(jnp.mean(xf * xf, axis=-1, keepdims=True) + EPS)
    return (y * g.astype(jnp.float32)).astype(x.dtype)


def modulation(cond, w_mod, b_mod):
    m = jax.nn.silu(cond) @ w_mod + b_mod
    return m.reshape(cond.shape[:-1] + (N_MOD, D_MODEL))


def modulate(h, m, i):
    return h * (1.0 + m[:, 3 * i + 1][:, None]) + m[:, 3 * i][:, None]


def gate(m, i):
    return m[:, 3 * i + 2][:, None]


def ffn_half_step(s, m, i, norm_g, w_in, w_out):
    h = modulate(rmsnorm(s, norm_g), m, i)
    gu = h @ w_in
    y = (jax.nn.silu(gu[..., :D_FF]) * gu[..., D_FF:]) @ w_out
    return s + 0.5 * gate(m, i) * y


def axial_rope_tables(rows, dtype):
    row = jnp.repeat(jnp.arange(rows), GRID_W).astype(jnp.float32)
    col = jnp.tile(jnp.arange(GRID_W), rows).astype(jnp.float32)
    inv = 1.0 / (ROPE_BASE ** (jnp.arange(0, AXIS_ROPE, 2, dtype=jnp.float32) / AXIS_ROPE))
    ang = jnp.stack([row[:, None] * inv, col[:, None] * inv], axis=1)
    return jnp.cos(ang).astype(dtype), jnp.sin(ang).astype(dtype)


def apply_axial_rope(x, cos, sin):
    shp = x.shape
    xr = x.reshape(shp[:-1] + (2, 2, AXIS_ROPE // 2))
    x1, x2 = xr[..., 0, :], xr[..., 1, :]
    if x.ndim == 4:
        cos, sin = cos[:, None], sin[:, None]
    out = jnp.stack([x1 * cos - x2 * sin, x2 * cos + x1 * sin], axis=-2)
    return out.reshape(shp)


def fourier_mix(u):
    b, n, _ = u.shape
    z = u.astype(jnp.float32).reshape(b, n, F_GROUPS, F_GROUP_DIM)
    z = jnp.fft.fft2(z, axes=(1, 3), norm="ortho").real
    return z.reshape(b, n, F_W).astype(u.dtype)


def conv_mix(gb, gc, v, conv_w):
    u = gc * v
    y = lax.conv_general_dilated(
        u, conv_w.astype(u.dtype)[:, None, :], window_strides=(1,),
        padding=((CONV_K // 2, CONV_K // 2),),
        dimension_numbers=("NWC", "WIO", "NWC"), feature_group_count=CONV_W)
    return gb * y


def mla_kv(p_kv, kv_norm, w_ukv, cos, sin):
    b, n = p_kv.shape[:2]
    kv_lat, k_rope = p_kv[..., :KV_LORA], p_kv[..., KV_LORA:]
    kv = (rmsnorm(kv_lat, kv_norm) @ w_ukv).reshape(b, n, MLA_HEADS, QK_NOPE + V_DIM)
    k_nope, v = kv[..., :QK_NOPE], kv[..., QK_NOPE:]
    if cos is not None:
        k_rope = apply_axial_rope(k_rope, cos, sin)
    return k_nope, k_rope, v


def mla_q(q_lat, q_norm, w_uq, cos, sin):
    b, n = q_lat.shape[:2]
    q = (rmsnorm(q_lat, q_norm) @ w_uq).reshape(b, n, MLA_HEADS, QK_NOPE + QK_ROPE)
    q_nope, q_rope = q[..., :QK_NOPE], q[..., QK_NOPE:]
    if cos is not None:
        q_rope = apply_axial_rope(q_rope, cos, sin)
    return q_nope, q_rope


def attend(q_nope, q_rope, k_nope, k_rope, v):
    s = (jnp.einsum("bqhd,bkhd->bhqk", q_nope, k_nope)
         + jnp.einsum("bqhr,bkr->bhqk", q_rope, k_rope)).astype(jnp.float32) * ATTN_SCALE
    p = jax.nn.softmax(s, axis=-1).astype(v.dtype)
    return jnp.einsum("bhqk,bkhd->bqhd", p, v)


def attend_blocked(q_nope, q_rope, k_nope, k_rope, v):
    b, n, h, _ = q_nope.shape
    nb = n // Q_BLOCK

    def to_blocks(t):
        return t.reshape((b, nb, Q_BLOCK) + t.shape[2:]).swapaxes(0, 1)

    out = lax.map(lambda qs: attend(qs[0], qs[1], k_nope, k_rope, v),
                  (to_blocks(q_nope), to_blocks(q_rope)))
    return out.swapaxes(0, 1).reshape(b, n, h * V_DIM)


def head_groups(proj, attn_out, conv_w):
    f = fourier_mix(proj[..., OFF_F:OFF_CB])
    cv = conv_mix(proj[..., OFF_CB:OFF_CC], proj[..., OFF_CC:OFF_CX], proj[..., OFF_CX:OFF_Q], conv_w)
    return jnp.concatenate([f, cv, attn_out], axis=-1)


def setup_inputs(seed: int = 0) -> dict:
    key = jax.random.key(seed)
    ks = jax.random.split(key, 24)
    L, D = DEPTH, D_MODEL

    def nrm(k, shape, scale):
        return jax.random.normal(k, shape, jnp.float32) * scale

    def gain(k, shape):
        return 1.0 + 0.02 * jax.random.normal(k, shape, jnp.float32)

    return {
        "x": nrm(ks[0], (BATCH, SEQ, D), 1.0),
        "c": nrm(ks[1], (BATCH, D), 1.0),
        "ctx": nrm(ks[2], (BATCH, CTX_LEN, D), 1.0),
        "c_ctx": nrm(ks[3], (D,), 1.0),
        "w_mod": nrm(ks[4], (L, D, N_MOD * D), 0.5 * D ** -0.5),
        "b_mod": nrm(ks[5], (L, N_MOD * D), 0.02),
        "ffn1_norm": gain(ks[6], (L, D)),
        "ffn1_w_in": nrm(ks[7], (L, D, 2 * D_FF), D ** -0.5),
        "ffn1_w_out": nrm(ks[8], (L, D_FF, D), D_FF ** -0.5),
        "mix_norm": gain(ks[9], (L, D)),
        "w_in": nrm(ks[10], (L, D, IN_COLS), D ** -0.5),
        "conv_w": nrm(ks[11], (L, CONV_K, CONV_W), CONV_K ** -0.5),
        "q_norm": gain(ks[12], (L, Q_LORA)),
        "w_uq": nrm(ks[13], (L, Q_LORA, MLA_HEADS * (QK_NOPE + QK_ROPE)), Q_LORA ** -0.5),
        "kv_norm": gain(ks[14], (L, KV_LORA)),
        "w_ukv": nrm(ks[15], (L, KV_LORA, MLA_HEADS * (QK_NOPE + V_DIM)), KV_LORA ** -0.5),
        "w_out": nrm(ks[16], (L, D_MIX, D), D_MIX ** -0.5),
        "ffn2_norm": gain(ks[17], (L, D)),
        "ffn2_w_in": nrm(ks[18], (L, D, 2 * D_FF), D ** -0.5),
        "ffn2_w_out": nrm(ks[19], (L, D_FF, D), D_FF ** -0.5),
        "final_norm": gain(ks[20], (D,)),
    }


def reference(x, c, ctx, c_ctx, w_mod, b_mod, ffn1_norm, ffn1_w_in, ffn1_w_out,
              mix_norm, w_in, conv_w, q_norm, w_uq, kv_norm, w_ukv, w_out,
              ffn2_norm, ffn2_w_in, ffn2_w_out, final_norm):
    n_lat = x.shape[1]
    rows = n_lat // GRID_W
    cos, sin = axial_rope_tables(rows, x.dtype)
    xs, cs = x, ctx
    for l in range(DEPTH):
        last = l == DEPTH - 1
        m_x = modulation(c, w_mod[l], b_mod[l])
        m_c = modulation(c_ctx, w_mod[l], b_mod[l])[None]

        xs = ffn_half_step(xs, m_x, 0, ffn1_norm[l], ffn1_w_in[l], ffn1_w_out[l])
        cs = ffn_half_step(cs, m_c, 0, ffn1_norm[l], ffn1_w_in[l], ffn1_w_out[l])

        hx = modulate(rmsnorm(xs, mix_norm[l]), m_x, 1)
        hc = modulate(rmsnorm(cs, mix_norm[l]), m_c, 1)
        px = hx @ w_in[l]
        if last:
            pc_kv = hc @ w_in[l][:, OFF_KV:]
        else:
            pc = hc @ w_in[l]
            pc_kv = pc[..., OFF_KV:]
        kn_c, kr_c, v_c = mla_kv(pc_kv, kv_norm[l], w_ukv[l], None, None)
        kn_x, kr_x, v_x = mla_kv(px[..., OFF_KV:], kv_norm[l], w_ukv[l], cos, sin)
        qn_x, qr_x = mla_q(px[..., OFF_Q:OFF_KV], q_norm[l], w_uq[l], cos, sin)
        att_x = attend_blocked(qn_x, qr_x,
                               jnp.concatenate([kn_x, kn_c], axis=1),
                               jnp.concatenate([kr_x, kr_c], axis=1),
                               jnp.concatenate([v_x, v_c], axis=1))
        mix_x = head_groups(px, att_x, conv_w[l]) @ w_out[l]
        xs = xs + gate(m_x, 1) * mix_x
        xs = ffn_half_step(xs, m_x, 2, ffn2_norm[l], ffn2_w_in[l], ffn2_w_out[l])

        if not last:
            qn_c, qr_c = mla_q(pc[..., OFF_Q:OFF_KV], q_norm[l], w_uq[l], None, None)
            att_c = attend(qn_c, qr_c, kn_c, kr_c, v_c).reshape(cs.shape[0], cs.shape[1], MLA_OUT)
            mix_c = head_groups(pc, att_c, conv_w[l]) @ w_out[l]
            cs = cs + gate(m_c, 1) * mix_c
            cs = ffn_half_step(cs, m_c, 2, ffn2_norm[l], ffn2_w_in[l], ffn2_w_out[l])
    return rmsnorm(xs, final_norm)
```

```python
import numpy as np
from contextlib import ExitStack
import concourse.bass as bass
import concourse.mybir as mybir
from concourse.bass_utils import run_bass_kernel_spmd

F32 = mybir.dt.float32
BF16 = mybir.dt.bfloat16
AF = mybir.ActivationFunctionType
ALU = mybir.AluOpType

D = 1024
DFF = 2816
NCH = 8
FCH = 22
T = 256
H = 8
OFF_CB, OFF_CC, OFF_CX, OFF_Q, OFF_KV, OFF_KR, IN_COLS = 256, 512, 768, 1024, 1408, 1664, 1696
EPS = 1e-6
ATTN_SCALE = 96.0 ** -0.5
DBG_CTX = False


class Op:
    __slots__ = ("eng", "fn", "deps", "dma", "val", "sem", "needs_inc", "idx")

    def __init__(self, eng, fn, dma):
        self.eng = eng
        self.fn = fn
        self.dma = dma
        self.deps = []
        self.val = 0
        self.sem = None
        self.needs_inc = False


class Sched:
    ENGS = ("pe", "act", "dve", "pool", "sp")

    def __init__(self):
        self.ops = {e: [] for e in self.ENGS}
        self.last_w = {}
        self.readers = {}
        self.last_dma = {}
        self.all = []
        self.n = 0

    def add(self, eng, fn, r=(), w=(), dma=None):
        op = Op(eng, fn, dma)
        op.idx = self.n
        self.n += 1
        deps = {}
        for k in r:
            p = self.last_w.get(k)
            if p is not None:
                deps[p.idx] = (p, "raw")
        for k in w:
            p = self.last_w.get(k)
            if p is not None and p.idx not in deps:
                deps[p.idx] = (p, "waw")
            for q in self.readers.get(k, ()):
                if q.idx not in deps:
                    deps[q.idx] = (q, "war")
        if dma is not None:
            p = self.last_dma.get(dma)
            if p is not None:
                deps[p.idx] = (p, "raw")
            self.last_dma[dma] = op
        for p, kind in deps.values():
            if p.dma is None and dma is None and p.eng == eng:
                if eng == "pe" or kind != "raw":
                    continue
            op.deps.append(p)
            if p.dma is None:
                p.needs_inc = True
        for k in r:
            self.readers.setdefault(k, []).append(op)
        for k in w:
            self.last_w[k] = op
            self.readers[k] = []
        self.ops[eng].append(op)
        self.all.append(op)
        return op

    def fence(self, eng, keys):
        return self.add(eng, None, r=keys)

    def barrier(self):
        lasts = []
        for e in self.ENGS:
            for op in reversed(self.ops[e]):
                if op.fn is not None and op.dma is None:
                    lasts.append(op)
                    break
        dmas = list(self.last_dma.values())
        for e in self.ENGS:
            op = Op(e, None, None)
            op.idx = self.n
            self.n += 1
            for p in lasts:
                if p.eng != e:
                    op.deps.append(p)
                    p.needs_inc = True
            op.deps.extend(dmas)
            self.ops[e].append(op)
        self.last_w = {}
        self.readers = {}

    def emit(self, nc, stack):
        esem = {e: stack.enter_context(nc.semaphore("s_" + e)) for e in self.ENGS}
        dsem = {}
        cnt = {e: 0 for e in self.ENGS}
        for op in self.all:
            if op.dma is not None:
                if op.dma not in dsem:
                    dsem[op.dma] = [stack.enter_context(nc.semaphore("d_%d" % len(dsem))), 0]
                ent = dsem[op.dma]
                ent[1] += 16
                op.sem, op.val = ent[0], ent[1]
            elif op.needs_inc:
                cnt[op.eng] += 1
                op.sem, op.val = esem[op.eng], cnt[op.eng]
        self.n_sems = len(dsem) + 5
        block = stack.enter_context(nc.Block())

        def run(e, eng):
            waited = {}
            for op in self.ops[e]:
                need = {}
                for p in op.deps:
                    key = id(p.sem)
                    if waited.get(key, 0) >= p.val:
                        continue
                    if key not in need or need[key][1] < p.val:
                        need[key] = (p.sem, p.val)
                for key, (s, v) in need.items():
                    eng.wait_ge(s, v)
                    waited[key] = v
                if op.fn is None:
                    continue
                ins = op.fn(eng)
                if op.dma is not None:
                    ins.then_inc(op.sem, 16)
                elif op.needs_inc:
                    ins.then_inc(op.sem, 1)

        block.tensor(lambda eng: run("pe", eng))
        block.scalar(lambda eng: run("act", eng))
        block.vector(lambda eng: run("dve", eng))
        block.gpsimd(lambda eng: run("pool", eng))
        block.sync(lambda eng: run("sp", eng))


class Arena:
    def __init__(self, nc, nbytes):
        self.t = nc.alloc_sbuf_tensor("arena", [128, nbytes // 4], F32)
        self.off = 0
        self.cap = nbytes
        self.n = 0

    def alloc(self, shape, dtype, key=None):
        esz = 2 if dtype == BF16 else 4
        n = 1
        for s in shape[1:]:
            n *= s
        nb = (n * esz + 63) // 64 * 64
        assert self.off + nb <= self.cap, ("SBUF arena overflow", self.off, nb, self.cap)
        ap = self.t[:, self.off // 4:(self.off + nb) // 4]
        if dtype == BF16:
            ap = ap.bitcast(BF16)
        ap = ap[:, 0:n]
        if len(shape) == 3:
            ap = ap.rearrange("p (a b) -> p a b", a=shape[1])
        elif len(shape) == 4:
            ap = ap.rearrange("p (a b c) -> p a b c", a=shape[1], b=shape[2])
        if shape[0] < 128:
            ap = ap[0:shape[0]]
        self.off += nb
        self.n += 1
        return ap, (key or "t%d" % self.n)


def build(SEQ, CTX, NB, L, debug=False):
    NT = SEQ + CTX
    NTL = NT // T
    NR = NB + 1
    segs = [(0, SEQ, SEQ // 64), (SEQ, CTX, CTX // 64)]
    nc = bass.Bass("TRN2", target_bir_lowering=False)
    S = Sched()
    add = S.add

    def din(name, shape):
        return nc.dram_tensor(name, list(shape), F32, kind="ExternalInput").ap()

    x_in = din("x", [NB, SEQ, D])
    ctx_in = din("ctx", [NB, CTX, D])
    c3 = din("c3", [NR, D])
    w_mod = din("w_mod", [L, D, 9 * D])
    b_mod = din("b_mod", [L, 9 * D])
    ffn_norm = [din("ffn1_norm", [L, D]), din("mix_norm", [L, D]), din("ffn2_norm", [L, D])]
    ffn_win = [din("ffn1_w_in", [L, D, 2 * DFF]), din("ffn2_w_in", [L, D, 2 * DFF])]
    ffn_wout = [din("ffn1_w_out", [L, DFF, D]), din("ffn2_w_out", [L, DFF, D])]
    w_in = din("w_in", [L, D, IN_COLS])
    conv_w = din("conv_w", [L, 3, 256])
    q_norm = din("q_norm", [L, 384])
    w_uq = din("w_uq", [L, 384, 768])
    kv_norm = din("kv_norm", [L, 256])
    w_ukv = din("w_ukv", [L, 256, 1024])
    w_out = din("w_out", [L, D, D])
    final_norm = din("final_norm", [D])
    ident_in = din("ident", [128, 128])
    cs_in = din("cs", [256, 512])
    m1_in = [din("m1x", [2 * segs[0][2]] * 2), din("m1c", [2 * segs[1][2]] * 2)]
    g2_in = [din("g2x", [128, SEQ]), din("g2c", [128, CTX])]
    cos_in = din("cosT", [96, NT])
    sin_in = din("sinT", [96, NT])
    y_out = nc.dram_tensor("y", [NB, SEQ, D], F32, kind="ExternalOutput").ap()

    dk = "ExternalOutput" if debug else "Internal"
    sT = nc.dram_tensor("sT", [NB, D, NT], F32, kind=dk).ap()
    Ud = nc.dram_tensor("Ud", [NB, 2, NT, 256], BF16, kind=dk).ap()
    Yd = nc.dram_tensor("Yd", [NB, 2, NT, 256], BF16, kind=dk).ap()
    CVd = nc.dram_tensor("CVd", [NB, 2, 256, NT], F32, kind=dk).ap()
    QTd = nc.dram_tensor("QTd", [NB, H, 96, NT], BF16, kind=dk).ap()
    KTd = nc.dram_tensor("KTd", [NB, H, 96, NT], BF16, kind=dk).ap()
    Vd = nc.dram_tensor("Vd", [NB, NT, H * 65], BF16, kind=dk).ap()
    MIXd = nc.dram_tensor("MIXd", [NB, D, NT], BF16, kind=dk).ap()

    with ExitStack() as st:
        A = Arena(nc, 207 * 1024)
        ps = nc.alloc_psum_tensor("ps", [128, 8, 512], F32)
        nc_allow = st.enter_context(nc.allow_non_contiguous_dma("small per-partition vectors"))

        def PK(b):
            return ("ps", b)

        ident, k_ident = A.alloc([128, 128], F32, "ident")
        ones_d, k_ones = A.alloc([128, 3, 128], BF16, "ones")
        onesf, k_onesf = A.alloc([128, 64], F32, "onesf")
        epst, k_eps = A.alloc([128, 1], F32, "eps")
        scT, k_scT = A.alloc([128, NCH, NR], F32, "scT")
        normv, k_normv = A.alloc([128, 3, L, NCH], F32, "normv")
        fnormv, k_fnormv = A.alloc([128, NCH], F32, "fnormv")
        qnv, k_qnv = A.alloc([128, L, 3], F32, "qnv")
        kvnv, k_kvnv = A.alloc([128, L, 2], F32, "kvnv")
        cwv, k_cwv = A.alloc([128, L, 3, 2], F32, "cwv")
        modT, k_modT = A.alloc([128, 72, NR], F32, "modT")
        Gv, k_Gv = A.alloc([128, 3, NCH, NR], F32, "Gv")
        gatev, k_gatev = A.alloc([128, 3, NCH, NR], F32, "gatev")
        rstd_l = [A.alloc([128, T], F32, "rstd0")] * 2
        sqb_l = [A.alloc([128, NCH, T], BF16, "sqb0")] * 2
        tn_l = [A.alloc([128, NCH, T], F32, "tn0")] * 2
        hT_l = [A.alloc([128, NCH, T], BF16, "hT%d" % i) for i in range(2)]
        hT, k_hT = hT_l[0]
        stile = [A.alloc([128, NCH, T], F32, "stile%d" % i) for i in range(2)]
        base_mark = A.off

        add("sp", lambda e: e.dma_start(out=ident, in_=ident_in), w=[k_ident], dma="c0")
        for i, v in enumerate((1.0 / 1024, 1.0 / 384, 1.0 / 256)):
            add("pool", lambda e, i=i, v=v: e.memset(ones_d[:, i, :], v), w=[k_ones])
        add("pool", lambda e: e.memset(onesf, 1.0), w=[k_onesf])
        add("pool", lambda e: e.memset(epst, EPS), w=[k_eps])
        for i in range(3):
            for l in range(L):
                add("sp", lambda e, i=i, l=l: e.dma_start(
                    out=normv[:, i, l, :], in_=ffn_norm[i][l].rearrange("(c p) -> p c", p=128)),
                    w=[k_normv], dma="c1")
        add("sp", lambda e: e.dma_start(out=fnormv, in_=final_norm.rearrange("(c p) -> p c", p=128)),
            w=[k_fnormv], dma="c2")
        for l in range(L):
            add("sp", lambda e, l=l: e.dma_start(out=qnv[:, l, :], in_=q_norm[l].rearrange("(c p) -> p c", p=128)),
                w=[k_qnv], dma="c3")
            add("sp", lambda e, l=l: e.dma_start(out=kvnv[:, l, :], in_=kv_norm[l].rearrange("(c p) -> p c", p=128)),
                w=[k_kvnv], dma="c4")
            for k in range(3):
                add("sp", lambda e, l=l, k=k: e.dma_start(
                    out=cwv[:, l, k, :], in_=conv_w[l, k].rearrange("(c p) -> p c", p=128)),
                    w=[k_cwv], dma="c5")

        xin = [A.alloc([128, D], F32, "xin%d" % i) for i in range(2)]
        xo = [A.alloc([128, NCH, 128], F32, "xo%d" % i) for i in range(2)]
        it = 0
        for b in range(NB):
            for (t0, n, _), src in zip(segs, (x_in, ctx_in)):
                for blk in range(n // 128):
                    xi, kxi = xin[it % 2]
                    xoo, kxo = xo[it % 2]
                    add("sp", lambda e, xi=xi, src=src, b=b, blk=blk: e.dma_start(
                        out=xi, in_=src[b, blk * 128:(blk + 1) * 128, :]), w=[kxi], dma="xin%d" % (it % 2))
                    for half in range(2):
                        bank = 2 * (it % 2) + half
                        for q in range(4):
                            c = half * 4 + q
                            add("pe", lambda e, xi=xi, bank=bank, q=q, c=c: e.transpose(
                                ps[:, bank, q * 128:(q + 1) * 128], xi[:, c * 128:(c + 1) * 128], ident),
                                r=[kxi, k_ident], w=[PK(bank)])
                        eng = "act" if half == 0 else "dve"
                        if eng == "act":
                            add("act", lambda e, xoo=xoo, bank=bank, half=half: e.copy(
                                out=xoo[:, half * 4:(half + 1) * 4, :],
                                in_=ps[:, bank, :].rearrange("p (a b) -> p a b", a=4)),
                                r=[PK(bank)], w=[kxo + "h%d" % half])
                        else:
                            add("dve", lambda e, xoo=xoo, bank=bank, half=half: e.tensor_copy(
                                out=xoo[:, half * 4:(half + 1) * 4, :],
                                in_=ps[:, bank, :].rearrange("p (a b) -> p a b", a=4)),
                                r=[PK(bank)], w=[kxo + "h%d" % half])
                    tok = t0 + blk * 128
                    add("sp", lambda e, xoo=xoo, b=b, tok=tok: e.dma_start(
                        out=sT[b].rearrange("(c p) t -> p c t", p=128)[:, :, tok:tok + 128], in_=xoo),
                        r=[kxo + "h0", kxo + "h1"], w=[("sT", b, tok // T)], dma="xo%d" % (it % 2))
                    it += 1
        crow, k_crow = A.alloc([NR, D], F32, "crow")
        add("sp", lambda e: e.dma_start(out=crow, in_=c3), w=[k_crow], dma="c6")
        add("act", lambda e: e.activation(out=crow, in_=crow, func=AF.Silu), r=[k_crow], w=[k_crow])
        for c in range(NCH):
            add("pe", lambda e, c=c: e.transpose(ps[:, 7, c * NR:(c + 1) * NR], crow[:, c * 128:(c + 1) * 128],
                                                  ident[0:NR, 0:NR]), r=[k_crow, k_ident], w=[PK(7)])
        add("dve", lambda e: e.tensor_copy(out=scT, in_=ps[:, 7, 0:NCH * NR].rearrange("p (a b) -> p a b", a=NCH)),
            r=[PK(7)], w=[k_scT])
        S.barrier()

        def norm_mod_g(stl, kst, i, r, dst, kdst, nchunk=NCH, ones_i=0, gain=None, bias=None, bank=7, bs=0, scratch=None):
            (rstd, k_rstd), (sqb, k_sqb), (tn, k_tn) = scratch if scratch is not None else (rstd_l[bs], sqb_l[bs], tn_l[bs])
            add("act", lambda e: e.activation(out=sqb[:, 0:nchunk, :], in_=stl[:, 0:nchunk, :], func=AF.Square),
                r=[kst], w=[k_sqb])
            yield
            for c in range(nchunk):
                add("pe", lambda e, c=c: e.matmul(ps[:, bank, 0:T], ones_d[:, ones_i, :], sqb[:, c, :],
                                                  start=(c == 0), stop=(c == nchunk - 1)),
                    r=[k_sqb, k_ones], w=[PK(bank)])
            yield
            add("act", lambda e: e.activation(out=rstd, in_=ps[:, bank, 0:T], func=AF.Sqrt, bias=epst[:, 0:1], scale=1.0),
                r=[PK(bank), k_eps], w=[k_rstd])
            yield
            add("dve", lambda e: e.reciprocal(out=rstd, in_=rstd), r=[k_rstd], w=[k_rstd])
            yield
            for c in range(nchunk):
                add("dve", lambda e, c=c: e.tensor_tensor(out=tn[:, c, :], in0=stl[:, c, :], in1=rstd, op=ALU.mult),
                    r=[kst, k_rstd], w=[k_tn + str(c)])
                g = gain(c)
                bb = bias(c) if bias is not None else 0.0
                add("act", lambda e, c=c, g=g, bb=bb: e.activation(out=dst[:, c, :], in_=tn[:, c, :], func=AF.Identity,
                                                                     scale=g, bias=bb),
                    r=[k_tn + str(c), k_Gv, k_modT, k_qnv, k_kvnv], w=[kdst])
                if c % 2 == 1:
                    yield

        def norm_mod(*a_, **k_):
            for _ in norm_mod_g(*a_, **k_):
                pass

        def load_ffn_weights(l, f, wi, kwi, wo, kwo):
            n = 0
            for q in (0, 2, 1, 3):
                for c in range(NCH):
                    add("pool", lambda e, q=q, c=c: e.dma_start(
                        out=wi[:, c, q * 1408:(q + 1) * 1408],
                        in_=ffn_win[f][l, c * 128:(c + 1) * 128, q * 1408:(q + 1) * 1408]),
                        w=[(kwi, c, q)], dma="w%d" % (n % 6))
                    n += 1
            for j in range(FCH):
                add("pool", lambda e, j=j: e.dma_start(out=wo[:, j, :], in_=ffn_wout[f][l, j * 128:(j + 1) * 128, :]),
                    w=[(kwo, j)], dma="w%d" % (n % 6))
                n += 1

        def ffn_pro(stl, kst, i, r, bs):
            hb, khb = hT_l[bs]
            norm_mod(stl, kst, i, r, hb, khb, bs=bs, bank=7,
                     gain=lambda c: Gv[:, i, c, r:r + 1], bias=lambda c: modT[:, (3 * i) * NCH + c, r:r + 1])

        def ffn_main(stl, kst, i, r, bs, wi, kwi, wo, kwo, actb, hook=None):
            hb, khb = hT_l[bs]

            def gu(j):
                bank = 4 + j % 3
                for half, col0 in ((0, j * 128), (1, DFF + j * 128)):
                    q = col0 // 1408
                    q2 = (col0 + 127) // 1408
                    for c in range(NCH):
                        add("pe", lambda e, c=c, half=half, col0=col0, bank=bank: e.matmul(
                            ps[:, bank, half * T:(half + 1) * T], wi[:, c, col0:col0 + 128], hb[:, c, :],
                            start=(c == 0), stop=(c == NCH - 1)),
                            r=[(kwi, c, q), (kwi, c, q2), khb], w=[PK(bank)])
                ab, kab = actb[j % 4]
                sg, ksg = actb[4 + j % 4]
                add("act", lambda e, bank=bank, sg=sg: e.activation(out=sg, in_=ps[:, bank, 0:T], func=AF.Silu),
                    r=[PK(bank)], w=[ksg])
                add("dve", lambda e, bank=bank, sg=sg, ab=ab: e.tensor_tensor(out=ab, in0=sg, in1=ps[:, bank, T:2 * T],
                                                                               op=ALU.mult),
                    r=[PK(bank), ksg], w=[kab])

            def yacc(j):
                ab, kab = actb[j % 4]
                for c in range(NCH):
                    add("pe", lambda e, c=c, j=j, ab=ab: e.matmul(
                        ps[:, c // 2, (c % 2) * T:(c % 2 + 1) * T], wo[:, j, c * 128:(c + 1) * 128], ab,
                        start=(j == 0 and c % 2 == 0), stop=(j == FCH - 1), skip_group_check=True),
                        r=[(kwo, j), kab], w=[PK(c // 2)])

            gu(0)
            gu(1)
            for j in range(FCH):
                yacc(j)
                if j + 2 < FCH:
                    gu(j + 2)
                if j == 5 and hook is not None:
                    hook()
            for c in range(NCH):
                add("dve", lambda e, c=c: e.scalar_tensor_tensor(
                    out=stl[:, c, :], in0=ps[:, c // 2, (c % 2) * T:(c % 2 + 1) * T], scalar=gatev[:, i, c, r:r + 1],
                    in1=stl[:, c, :], op0=ALU.mult, op1=ALU.add),
                    r=[PK(c // 2), k_gatev, kst], w=[kst])

        def sT_tile(b, t):
            return sT[b].rearrange("(c p) t -> p c t", p=128)[:, :, t * T:(t + 1) * T]

        for l in range(L):
            last = l == L - 1
            A.off = base_mark
            wm = [A.alloc([128, NCH, 512], F32, "wm%d" % i) for i in range(2)]
            bm = [A.alloc([1, 512], F32, "bm%d" % i) for i in range(2)]
            mrow, k_mrow = A.alloc([NR, 9 * D], F32, "mrow")
            for ct in range(18):
                wmt, kwm = wm[ct % 2]
                bmt, kbm = bm[ct % 2]
                add("sp", lambda e, wmt=wmt, ct=ct, l=l: e.dma_start(
                    out=wmt, in_=w_mod[l].rearrange("(c p) n -> p c n", p=128)[:, :, ct * 512:(ct + 1) * 512]),
                    w=[kwm], dma="wm%d" % (ct % 2))
                add("sp", lambda e, bmt=bmt, ct=ct, l=l: e.dma_start(out=bmt, in_=b_mod[l:l + 1, ct * 512:(ct + 1) * 512]),
                    w=[kbm], dma="bm%d" % (ct % 2))
                bank = ct % 2
                for c in range(NCH):
                    add("pe", lambda e, c=c, wmt=wmt, bank=bank: e.matmul(
                        ps[0:NR, bank, :], scT[:, c, :], wmt[:, c, :], start=(c == 0), stop=False),
                        r=[kwm, k_scT], w=[PK(bank)])
                add("pe", lambda e, bmt=bmt, bank=bank: e.matmul(ps[0:NR, bank, :], onesf[0:1, 0:NR], bmt,
                                                                    start=False, stop=True),
                    r=[kbm, k_onesf], w=[PK(bank)])
                add("act", lambda e, ct=ct, bank=bank: e.copy(out=mrow[:, ct * 512:(ct + 1) * 512], in_=ps[0:NR, bank, :]),
                    r=[PK(bank)], w=[k_mrow])
            for ch in range(72):
                add("pe", lambda e, ch=ch: e.transpose(ps[:, 2, ch * NR:(ch + 1) * NR], mrow[:, ch * 128:(ch + 1) * 128],
                                                        ident[0:NR, 0:NR]), r=[k_mrow, k_ident], w=[PK(2)])
            add("dve", lambda e: e.tensor_copy(out=modT, in_=ps[:, 2, 0:72 * NR].rearrange("p (a b) -> p a b", a=72)),
                r=[PK(2)], w=[k_modT])
            for i in range(3):
                add("dve", lambda e, i=i: e.tensor_scalar_add(out=Gv[:, i, :, :],
                                                             in0=modT[:, (3 * i + 1) * NCH:(3 * i + 2) * NCH, :], scalar1=1.0),
                    r=[k_modT], w=[k_Gv])
                for r in range(NR):
                    add("dve", lambda e, i=i, r=r, l=l: e.tensor_tensor(out=Gv[:, i, :, r], in0=Gv[:, i, :, r],
                                                                  in1=normv[:, i, l, :], op=ALU.mult),
                        r=[k_Gv, k_normv], w=[k_Gv])
                add("dve", lambda e, i=i: e.tensor_scalar_mul(out=gatev[:, i, :, :],
                                                             in0=modT[:, (3 * i + 2) * NCH:(3 * i + 3) * NCH, :],
                                                             scalar1=(1.0 if i == 1 else 0.5)),
                    r=[k_modT], w=[k_gatev])
            S.barrier()

            A.off = base_mark
            wi, kwi = A.alloc([128, NCH, 2 * DFF], BF16, "wi")
            wo, kwo = A.alloc([128, FCH, D], BF16, "wo")
            actb = [A.alloc([128, T], BF16, "actb%d" % i) for i in range(8)]
            ffn_mark = A.off
            load_ffn_weights(l, 0, wi, kwi, wo, kwo)
            tiles = [(b, t) for b in range(NB) for t in range(NTL)]

            def ffn_phase(i, tl, epilogue):
                def pro(n):
                    b, t = tl[n]
                    stl, kst = stile[n % 2]
                    r = NB if t == NTL - 1 else b
                    add("sp", lambda e, stl=stl, b=b, t=t: e.dma_start(out=stl, in_=sT_tile(b, t)),
                        r=[("sT", b, t)], w=[kst], dma="ld_s%d" % (n % 2))
                    ffn_pro(stl, kst, i, r, n % 2)
                pro(0)
                for n, (b, t) in enumerate(tl):
                    stl, kst = stile[n % 2]
                    r = NB if t == NTL - 1 else b
                    ffn_main(stl, kst, i, r, n % 2, wi, kwi, wo, kwo, actb,
                             hook=(lambda n=n: pro(n + 1)) if n + 1 < len(tl) else None)
                    epilogue(n, b, t, stl, kst)

            def store_s(n, b, t, stl, kst):
                add("sp", lambda e, stl=stl, b=b, t=t: e.dma_start(out=sT_tile(b, t), in_=stl),
                    r=[kst], w=[("sT", b, t)], dma="st_s%d" % (n % 2))

            ffn_phase(0, tiles, store_s)
            S.barrier()

            p2(nc, S, A, base_mark, l, locals())
            S.barrier()
            p3(nc, S, A, base_mark, l, locals())
            S.barrier()
            p4(nc, S, A, base_mark, l, locals())
            S.barrier()

            A.off = ffn_mark
            wom, kwom = A.alloc([128, NCH, D], BF16, "wom")
            if last:
                _m = A.off
                yo = [A.alloc([128, D], F32, "yo0")] * 2
                A.off = _m
            mixt = [A.alloc([128, NCH, T], BF16, "mixt%d" % i) for i in range(2)]
            for c in range(NCH):
                add("pool", lambda e, c=c, l=l: e.dma_start(out=wom[:, c, :], in_=w_out[l, c * 128:(c + 1) * 128, :]),
                    w=[(kwom, c)], dma="w%d" % (c % 6))
            load_ffn_weights(l, 1, wi, kwi, wo, kwo)
            tl5 = [(b, t) for (b, t) in tiles if not (last and t == NTL - 1)]
            def loads5(n, b, t):
                stl, kst = stile[n % 2]
                mx, kmx = mixt[n % 2]
                add("sp", lambda e: e.dma_start(out=stl, in_=sT_tile(b, t)),
                    r=[("sT", b, t)], w=[kst], dma="ld_s%d" % (n % 2))
                add("sp", lambda e: e.dma_start(
                    out=mx, in_=MIXd[b].rearrange("(c p) t -> p c t", p=128)[:, :, t * T:(t + 1) * T]),
                    r=[("MIX", b)], w=[kmx], dma="ld_m%d" % (n % 2))

            loads5(0, *tl5[0])
            for n, (b, t) in enumerate(tl5):
                stl, kst = stile[n % 2]
                mx, kmx = mixt[n % 2]
                r = NB if t == NTL - 1 else b
                bk0 = 4 * (n % 2)
                if n + 1 < len(tl5):
                    loads5(n + 1, *tl5[n + 1])
                for c in range(NCH):
                    for kc in range(NCH):
                        add("pe", lambda e, c=c, kc=kc, mx=mx, bk0=bk0: e.matmul(
                            ps[:, bk0 + c // 2, (c % 2) * T:(c % 2 + 1) * T], wom[:, kc, c * 128:(c + 1) * 128], mx[:, kc, :],
                            start=(kc == 0), stop=(kc == NCH - 1), skip_group_check=True),
                            r=[(kwom, kc), kmx], w=[PK(bk0 + c // 2)])
                for c in range(NCH):
                    add("dve", lambda e, c=c, stl=stl, r=r, bk0=bk0: e.scalar_tensor_tensor(
                        out=stl[:, c, :], in0=ps[:, bk0 + c // 2, (c % 2) * T:(c % 2 + 1) * T], scalar=gatev[:, 1, c, r:r + 1],
                        in1=stl[:, c, :], op0=ALU.mult, op1=ALU.add),
                        r=[PK(bk0 + c // 2), k_gatev, kst], w=[kst])
                add("sp", lambda e, stl=stl, b=b, t=t: e.dma_start(out=sT_tile(b, t), in_=stl),
                    r=[kst], w=[("sT", b, t)], dma="st_s%d" % (n % 2))

            if last:
                S.barrier()

            def final_out(n, b, t, stl, kst):
                rstd, k_rstd = rstd_l[n % 2]
                sqb, k_sqb = sqb_l[n % 2]
                tn, k_tn = tn_l[n % 2]
                add("act", lambda e: e.activation(out=sqb, in_=stl, func=AF.Square), r=[kst], w=[k_sqb])
                for c in range(NCH):
                    add("pe", lambda e, c=c: e.matmul(ps[:, 7, 0:T], ones_d[:, 0, :], sqb[:, c, :],
                                                      start=(c == 0), stop=(c == NCH - 1)),
                        r=[k_sqb, k_ones], w=[PK(7)])
                add("act", lambda e: e.activation(out=rstd, in_=ps[:, 7, 0:T], func=AF.Sqrt, bias=epst[:, 0:1], scale=1.0),
                    r=[PK(7), k_eps], w=[k_rstd])
                add("dve", lambda e: e.reciprocal(out=rstd, in_=rstd), r=[k_rstd], w=[k_rstd])
                for c in range(NCH):
                    add("dve", lambda e, c=c: e.scalar_tensor_tensor(
                        out=tn[:, c, :], in0=stl[:, c, :], scalar=fnormv[:, c:c + 1], in1=rstd,
                        op0=ALU.mult, op1=ALU.mult), r=[kst, k_rstd, k_fnormv], w=[k_tn + str(c)])
                for sub in range(T // 128):
                    yt, kyt = yo[sub]
                    for half in range(2):
                        bank = 5 + half
                        for q in range(4):
                            c = half * 4 + q
                            add("pe", lambda e, c=c, q=q, bank=bank, sub=sub: e.transpose(
                                ps[:, bank, q * 128:(q + 1) * 128], tn[:, c, sub * 128:(sub + 1) * 128], ident),
                                r=[k_tn + str(c), k_ident], w=[PK(bank)])
                        if half == 0:
                            add("act", lambda e, yt=yt, bank=bank: e.copy(out=yt[:, 0:512], in_=ps[:, bank, :]),
                                r=[PK(bank)], w=[kyt + "a"])
                        else:
                            add("dve", lambda e, yt=yt, bank=bank: e.tensor_copy(out=yt[:, 512:1024], in_=ps[:, bank, :]),
                                r=[PK(bank)], w=[kyt + "b"])
                    tok = t * T + sub * 128
                    add("sp", lambda e, yt=yt, b=b, tok=tok: e.dma_start(out=y_out[b, tok:tok + 128, :], in_=yt),
                        r=[kyt + "a", kyt + "b"], w=[("y", b, tok)], dma="yo%d" % sub)

            ffn_phase(2, tl5, final_out if last else store_s)
            S.barrier()
        S.fence("sp", [("y", b, tok) for b in range(NB) for tok in range(0, SEQ, 128)])
        S.emit(nc, st)
    return nc


def p2(nc, S, A, base_mark, l, env):
    g = env
    add = S.add
    NB, NT, NTL, NR, L = g["NB"], g["NT"], g["NTL"], g["NR"], g["L"]
    ps, PK = g["ps"], g["PK"]
    w_in, w_uq, w_ukv = g["w_in"], g["w_uq"], g["w_ukv"]
    stile, hT, k_hT = g["stile"], g["hT"], g["k_hT"]
    norm_mod, norm_mod_g, sT_tile = g["norm_mod"], g["norm_mod_g"], g["sT_tile"]
    Gv, modT, qnv, kvnv = g["Gv"], g["modT"], g["qnv"], g["kvnv"]
    A.off = base_mark
    wmx, kwmx = A.alloc([128, NCH, IN_COLS], BF16, "wmx")
    wkr, kwkr = A.alloc([128, NCH, 2, 96], BF16, "wkr")
    wuq, kwuq = A.alloc([128, 3, 768], BF16, "wuq")
    wuqr, kwuqr = A.alloc([128, 3, H, 96], BF16, "wuqr")
    wukv, kwukv = A.alloc([128, 2, 1024], BF16, "wukv")
    csm, kcsm = A.alloc([128, 2, 512], BF16, "csm")
    cosS = [A.alloc([96, T], F32, "cosS%d" % i) for i in range(2)]
    sinS = [A.alloc([96, T], F32, "sinS%d" % i) for i in range(2)]
    def alloc_set(i):
        d = {}
        d['zT'] = A.alloc([128, 2, T], BF16, "zT%d" % i)
        d['Usb'] = A.alloc([128, 2, 512], BF16, "Usb%d" % i)
        d['cb_sb'] = A.alloc([128, 2, T], F32, "cb_sb%d" % i)
        d['cc_sb'] = A.alloc([128, 2, T], F32, "cc_sb%d" % i)
        d['u_sb'] = A.alloc([128, 2, T], F32, "u_sb%d" % i)
        d['ql'] = A.alloc([128, 3, T], F32, "ql%d" % i)
        d['qn'] = A.alloc([128, 3, T], BF16, "qn%d" % i)
        d['kvl'] = A.alloc([128, 2, T], F32, "kvl%d" % i)
        d['kvn'] = A.alloc([128, 2, T], BF16, "kvn%d" % i)
        d['t1'] = A.alloc([96, 2, T], F32, "t1%d" % i)
        d['t2'] = A.alloc([96, 2, T], F32, "t2%d" % i)
        d['krf'] = A.alloc([96, T], BF16, "krf%d" % i)
        d['Qsb'] = A.alloc([96, H, T], BF16, "Qsb%d" % i)
        d['Ksb'] = A.alloc([96, H, T], BF16, "Ksb%d" % i)
        d['Vsb'] = A.alloc([128, 2, H, 65], BF16, "Vsb%d" % i)
        return d
    sets = [alloc_set(0), alloc_set(1)]
    scr = [(A.alloc([128, T], F32, "p2rstd%d" % i), A.alloc([128, NCH, T], BF16, "p2sqb%d" % i),
            A.alloc([128, NCH, T], F32, "p2tn%d" % i)) for i in range(2)]

    for c in range(NCH):
        add("pool", lambda e, c=c: e.dma_start(out=wmx[:, c, :], in_=w_in[l, c * 128:(c + 1) * 128, :]),
            w=[kwmx], dma="w%d" % (c % 6))
    add("pool", lambda e: e.memset(wkr, 0.0), w=[kwkr])
    add("pool", lambda e: e.memset(wuqr, 0.0), w=[kwuqr])
    for _d in sets:
        add("pool", lambda e, _d=_d: e.memset(_d["Vsb"][0], 1.0), w=[_d["Vsb"][1]])
    win_v = w_in[l].rearrange("(c p) n -> p c n", p=128)
    for c in range(NCH):
        add("pool", lambda e, c=c: e.dma_start(out=wkr[:, c, 0, 64:96], in_=win_v[:, c, OFF_KR:OFF_KR + 32]),
            w=[kwkr], dma="w0")
        for ax in range(2):
            for half in range(2):
                d0 = ax * 16 + half * 8
                p0 = ax * 16 + (1 - half) * 8
                add("pool", lambda e, c=c, d0=d0, p0=p0: e.dma_start(
                    out=wkr[:, c, 1, 64 + d0:64 + d0 + 8], in_=win_v[:, c, OFF_KR + p0:OFF_KR + p0 + 8]),
                    w=[kwkr], dma="w1")
    wuq_v = w_uq[l].rearrange("(c p) n -> p c n", p=128)
    for c in range(3):
        add("pool", lambda e, c=c: e.dma_start(out=wuq[:, c, :], in_=wuq_v[:, c, :]), w=[kwuq], dma="w2")
        for ax in range(2):
            for half in range(2):
                d0 = ax * 16 + half * 8
                p0 = ax * 16 + (1 - half) * 8
                add("pool", lambda e, c=c, d0=d0, p0=p0: e.dma_start(
                    out=wuqr[:, c, :, 64 + d0:64 + d0 + 8],
                    in_=wuq_v[:, c, :].rearrange("p (h d) -> p h d", h=H)[:, :, 64 + p0:64 + p0 + 8]),
                    w=[kwuqr], dma="w3")
    for c in range(2):
        add("pool", lambda e, c=c: e.dma_start(out=wukv[:, c, :], in_=w_ukv[l, c * 128:(c + 1) * 128, :]),
            w=[kwukv], dma="w4")
        add("pool", lambda e, c=c: e.dma_start(out=csm[:, c, :], in_=g["cs_in"][c * 128:(c + 1) * 128, :]),
            w=[kcsm], dma="w5")

    def do_tile(n, b, t, BS):
        zT, kzT = BS['zT']
        Usb, kUsb = BS['Usb']
        cb_sb, kcb = BS['cb_sb']
        cc_sb, kcc = BS['cc_sb']
        u_sb, ku = BS['u_sb']
        ql, kql = BS['ql']
        qn, kqn = BS['qn']
        kvl, kkvl = BS['kvl']
        kvn, kkvn = BS['kvn']
        t1, kt1 = BS['t1']
        t2, kt2 = BS['t2']
        krf, kkrf = BS['krf']
        Qsb, kQ = BS['Qsb']
        Ksb, kK = BS['Ksb']
        Vsb, kV = BS['Vsb']
        hT, k_hT = g["hT_l"][n % 2]
        SC = scr[n % 2]

        def BK(k):
            return (k + 4 * (n % 2)) % 8

        def proj(lhs_fn, nout, bank, M=128, keys=()):
            for oc in range(nout):
                for kc in range(NCH):
                    add("pe", lambda e, oc=oc, kc=kc: e.matmul(ps[0:M, bank, oc * T:(oc + 1) * T], lhs_fn(kc, oc), hT[:, kc, :],
                                                               start=(kc == 0), stop=(kc == NCH - 1), skip_group_check=True),
                        r=[k_hT] + list(keys), w=[PK(bank)])


        stl, kst = stile[n % 2]
        cs_t, kcs = cosS[n % 2]
        sn_t, ksn = sinS[n % 2]
        r = NB if t == NTL - 1 else b
        tok = t * T
        yield from norm_mod_g(stl, kst, 1, r, hT, k_hT, gain=lambda c: Gv[:, 1, c, r:r + 1],
                               bias=lambda c: modT[:, 3 * NCH + c, r:r + 1], bank=BK(7), scratch=SC)
        proj(lambda kc, oc: wmx[:, kc, oc * 128:(oc + 1) * 128], 2, BK(0), keys=[kwmx])
        add("act", lambda e: e.copy(out=zT, in_=ps[:, BK(0), :].rearrange("p (a b) -> p a b", a=2)), r=[PK(BK(0))], w=[kzT])
        yield
        for sub in range(2):
            for kc in range(2):
                add("pe", lambda e, sub=sub, kc=kc: e.matmul(ps[:, BK(1 + sub), :], zT[:, kc, sub * 128:(sub + 1) * 128],
                                                             csm[:, kc, :], start=(kc == 0), stop=(kc == 1)),
                    r=[kzT, kcsm], w=[PK(BK(1 + sub))])
            if sub == 0:
                add("act", lambda e: e.copy(out=Usb[:, 0, :], in_=ps[:, BK(1), :]), r=[PK(BK(1))], w=[kUsb + "0"])
            else:
                add("dve", lambda e: e.tensor_copy(out=Usb[:, 1, :], in_=ps[:, BK(2), :]), r=[PK(BK(2))], w=[kUsb + "1"])
            add("sp", lambda e, sub=sub, b=b, tok=tok: e.dma_start(
                out=g["Ud"][b, :, tok + sub * 128:tok + (sub + 1) * 128, :].rearrange("r t c -> t r c"),
                in_=Usb[:, sub, :].rearrange("p (r c) -> p r c", r=2)),
                r=[kUsb + str(sub)], w=[("Ud", b)], dma="st_u%d" % sub)
        yield
        proj(lambda kc, oc: wmx[:, kc, OFF_CB + oc * 128:OFF_CB + (oc + 1) * 128], 2, BK(3), keys=[kwmx])
        add("act", lambda e: e.copy(out=cb_sb, in_=ps[:, BK(3), :].rearrange("p (a b) -> p a b", a=2)), r=[PK(BK(3))], w=[kcb])
        add("sp", lambda e, b=b, tok=tok: e.dma_start(
            out=g["CVd"][b, 0].rearrange("(c p) t -> p c t", p=128)[:, :, tok:tok + T], in_=cb_sb),
            r=[kcb], w=[("CV", b)], dma="st_cb")
        proj(lambda kc, oc: wmx[:, kc, OFF_CC + oc * 128:OFF_CC + (oc + 1) * 128], 2, BK(4), keys=[kwmx])
        add("act", lambda e: e.copy(out=cc_sb, in_=ps[:, BK(4), :].rearrange("p (a b) -> p a b", a=2)), r=[PK(BK(4))], w=[kcc])
        yield
        proj(lambda kc, oc: wmx[:, kc, OFF_CX + oc * 128:OFF_CX + (oc + 1) * 128], 2, BK(5), keys=[kwmx])
        add("dve", lambda e: e.tensor_tensor(out=u_sb, in0=cc_sb, in1=ps[:, BK(5), :].rearrange("p (a b) -> p a b", a=2),
                                             op=ALU.mult), r=[PK(BK(5)), kcc], w=[ku])
        add("sp", lambda e, b=b, tok=tok: e.dma_start(
            out=g["CVd"][b, 1].rearrange("(c p) t -> p c t", p=128)[:, :, tok:tok + T], in_=u_sb),
            r=[ku], w=[("CV", b)], dma="st_u")
        yield
        proj(lambda kc, oc: wmx[:, kc, OFF_Q + oc * 128:OFF_Q + (oc + 1) * 128], 2, BK(6), keys=[kwmx])
        add("act", lambda e: e.copy(out=ql[:, 0:2, :], in_=ps[:, BK(6), :].rearrange("p (a b) -> p a b", a=2)),
            r=[PK(BK(6))], w=[kql])
        proj(lambda kc, oc: wmx[:, kc, OFF_Q + 256:OFF_Q + 384], 1, BK(7), keys=[kwmx])
        add("act", lambda e: e.copy(out=ql[:, 2, :], in_=ps[:, BK(7), 0:T]), r=[PK(BK(7))], w=[kql])
        yield
        proj(lambda kc, oc: wmx[:, kc, OFF_KV + oc * 128:OFF_KV + (oc + 1) * 128], 2, BK(0), keys=[kwmx])
        add("act", lambda e: e.copy(out=kvl, in_=ps[:, BK(0), :].rearrange("p (a b) -> p a b", a=2)), r=[PK(BK(0))], w=[kkvl])
        proj(lambda kc, oc: wkr[:, kc, oc, :], 2, BK(1), M=96, keys=[kwkr])
        add("dve", lambda e, cs_t=cs_t: e.tensor_tensor(out=t1[:, 0, :], in0=ps[0:96, BK(1), 0:T], in1=cs_t, op=ALU.mult),
            r=[PK(BK(1)), kcs], w=[kt1])
        add("dve", lambda e, sn_t=sn_t: e.tensor_tensor(out=t2[:, 0, :], in0=ps[0:96, BK(1), T:2 * T], in1=sn_t, op=ALU.mult),
            r=[PK(BK(1)), ksn], w=[kt2])
        add("pool", lambda e: e.tensor_tensor(out=krf, in0=t1[:, 0, :], in1=t2[:, 0, :], op=ALU.add),
            r=[kt1, kt2], w=[kkrf])
        yield
        yield from norm_mod_g(ql, kql, 1, r, qn, kqn, nchunk=3, ones_i=1, gain=lambda c: qnv[:, l, c:c + 1], bank=BK(2), scratch=SC)
        for hp in range(4):
            ba, bb = BK(3 + 2 * (hp % 2)), BK(4 + 2 * (hp % 2))
            for hh in range(2):
                h = 2 * hp + hh
                for kc in range(3):
                    add("pe", lambda e, h=h, hh=hh, kc=kc, ba=ba: e.matmul(
                        ps[0:96, ba, hh * T:(hh + 1) * T], wuq[:, kc, h * 96:(h + 1) * 96], qn[:, kc, :],
                        start=(kc == 0), stop=(kc == 2), skip_group_check=True), r=[kwuq, kqn], w=[PK(ba)])
                for kc in range(3):
                    add("pe", lambda e, h=h, hh=hh, kc=kc, bb=bb: e.matmul(
                        ps[0:96, bb, hh * T:(hh + 1) * T], wuqr[:, kc, h, :], qn[:, kc, :],
                        start=(kc == 0), stop=(kc == 2), skip_group_check=True), r=[kwuqr, kqn], w=[PK(bb)])
            add("dve", lambda e, ba=ba, cs_t=cs_t: e.tensor_tensor(
                out=t1, in0=ps[0:96, ba, :].rearrange("p (a b) -> p a b", a=2),
                in1=cs_t.unsqueeze(1).to_broadcast([96, 2, T]), op=ALU.mult), r=[PK(ba), kcs], w=[kt1])
            add("dve", lambda e, bb=bb, sn_t=sn_t: e.tensor_tensor(
                out=t2, in0=ps[0:96, bb, :].rearrange("p (a b) -> p a b", a=2),
                in1=sn_t.unsqueeze(1).to_broadcast([96, 2, T]), op=ALU.mult), r=[PK(bb), ksn], w=[kt2])
            add("pool", lambda e, hp=hp: e.tensor_tensor(out=Qsb[:, 2 * hp:2 * hp + 2, :], in0=t1, in1=t2, op=ALU.add),
                r=[kt1, kt2], w=[kQ])
            yield
        add("sp", lambda e, b=b, tok=tok: e.dma_start(
            out=g["QTd"][b, :, :, tok:tok + T].rearrange("h d t -> d h t"), in_=Qsb),
            r=[kQ], w=[("QT", b)], dma="st_q")
        yield from norm_mod_g(kvl, kkvl, 1, r, kvn, kkvn, nchunk=2, ones_i=2, gain=lambda c: kvnv[:, l, c:c + 1], bank=BK(2), scratch=SC)
        for hp in range(4):
            bank = BK(3 + (hp % 2) * 2)
            for hh in range(2):
                h = 2 * hp + hh
                for kc in range(2):
                    add("pe", lambda e, h=h, hh=hh, kc=kc, bank=bank: e.matmul(
                        ps[0:64, bank, hh * T:(hh + 1) * T], wukv[:, kc, h * 128:h * 128 + 64], kvn[:, kc, :],
                        start=(kc == 0), stop=(kc == 1), skip_group_check=True), r=[kwukv, kkvn], w=[PK(bank)])
            add("act", lambda e, hp=hp, bank=bank: e.copy(
                out=Ksb[0:64, 2 * hp:2 * hp + 2, :], in_=ps[0:64, bank, :].rearrange("p (a b) -> p a b", a=2)),
                r=[PK(bank)], w=[kK + "n"])
            yield
        add("pool", lambda e: e.tensor_copy(out=Ksb[64:96, :, :], in_=krf[64:96, :].unsqueeze(1).to_broadcast([32, H, T])),
            r=[kkrf], w=[kK + "r"])
        add("sp", lambda e, b=b, tok=tok: e.dma_start(
            out=g["KTd"][b, :, :, tok:tok + T].rearrange("h d t -> d h t"), in_=Ksb),
            r=[kK + "n", kK + "r"], w=[("KT", b)], dma="st_k")
        for sub in range(2):
            bank = BK(6 + sub)
            for kc in range(2):
                add("pe", lambda e, sub=sub, kc=kc, bank=bank: e.matmul(
                    ps[:, bank, :], kvn[:, kc, sub * 128:(sub + 1) * 128],
                    wukv[:, kc, :].rearrange("p (h d) -> p h d", h=H)[:, :, 64:128],
                    start=(kc == 0), stop=(kc == 1)), r=[kwukv, kkvn], w=[PK(bank)])
            add("act", lambda e, sub=sub, bank=bank: e.copy(
                out=Vsb[:, sub, :, 0:64], in_=ps[:, bank, :].rearrange("p (h d) -> p h d", h=H)),
                r=[PK(bank)], w=[kV + str(sub)])
            add("sp", lambda e, sub=sub, b=b, tok=tok: e.dma_start(
                out=g["Vd"][b, tok + sub * 128:tok + (sub + 1) * 128, :],
                in_=Vsb[:, sub, :, :].rearrange("p h d -> p (h d)")),
                r=[kV, kV + str(sub)], w=[("V", b)], dma="st_v%d" % sub)


    def loads(n, b, t, what="sc"):
        stl, kst = stile[n % 2]
        cs_t, kcs = cosS[n % 2]
        sn_t, ksn = sinS[n % 2]
        tok = t * T
        if "s" in what:
            add("sp", lambda e: e.dma_start(out=stl, in_=sT_tile(b, t)),
                r=[("sT", b, t)], w=[kst], dma="ld_s%d" % (n % 2))
        if "c" in what:
            add("sp", lambda e: e.dma_start(out=cs_t, in_=g["cos_in"][:, tok:tok + T]),
                w=[kcs], dma="ld_c%d" % (n % 2))
            add("sp", lambda e: e.dma_start(out=sn_t, in_=g["sin_in"][:, tok:tok + T]),
                w=[ksn], dma="ld_n%d" % (n % 2))

    tl = [(b, t) for b in range(NB) for t in range(NTL)]
    loads(0, *tl[0])
    if len(tl) > 1:
        loads(1, *tl[1])
    for n0 in range(0, len(tl), 2):
        pair = list(range(n0, min(n0 + 2, len(tl))))
        gens = [do_tile(n, tl[n][0], tl[n][1], sets[n % 2]) for n in pair]
        rounds = 0
        while gens:
            for gi_ in list(gens):
                try:
                    next(gi_)
                except StopIteration:
                    gens.remove(gi_)
            rounds += 1
            if rounds == 10:
                for n in (n0 + 2, n0 + 3):
                    if n < len(tl):
                        loads(n, *tl[n], what="s")
        for n in (n0 + 2, n0 + 3):
            if n < len(tl):
                loads(n, *tl[n], what="c")


def p3(nc, S, A, base_mark, l, env):
    g = env
    add = S.add
    NB, NT, NTL, SEQ, CTX = g["NB"], g["NT"], g["NTL"], g["SEQ"], g["CTX"]
    ps, PK, onesf, k_onesf = g["ps"], g["PK"], g["onesf"], g["k_onesf"]
    A.off = base_mark
    NKC = NT // 128
    TQ = 512 if SEQ % 512 == 0 else 256
    Kall, kKall = A.alloc([96, H, NT], BF16, "Kall")
    Vall, kVall = A.alloc([128, NKC, H * 65], BF16, "Vall")
    Qt = [A.alloc([96, H, TQ], BF16, "Qt%d" % i) for i in range(2)]
    Pb = [A.alloc([128, 2, TQ], BF16, "Pb%d" % i) for i in range(2)]
    rd, krd = A.alloc([128, TQ], F32, "rd")
    bcs = [A.alloc([64, TQ], F32, "bcs%d" % i) for i in range(2)]
    Osb = [A.alloc([64, H, TQ], BF16, "Osb%d" % i) for i in range(2)]
    n = 0
    hc = 0
    gcount = 0
    for b in range(NB):
        for h in range(H):
            add("sp", lambda e, b=b, h=h: e.dma_start(out=Kall[:, h, :], in_=g["KTd"][b, h]),
                r=[("KT", b)], w=[kKall], dma="ld_k%d" % (h % 4))
        for c0 in range(0, NKC, 6):
            c1 = min(NKC, c0 + 6)
            add("sp", lambda e, b=b, c0=c0, c1=c1: e.dma_start(
                out=Vall[:, c0:c1, :], in_=g["Vd"][b].rearrange("(c p) n -> p c n", p=128)[:, c0:c1, :]),
                r=[("V", b)], w=[kVall], dma="ld_v")
        qtiles = [(tok, TQ, list(range(NKC))) for tok in range(0, SEQ, TQ)]
        if not (g["last"] and not DBG_CTX):
            qtiles.append((SEQ, CTX, list(range(SEQ // 128, NKC))))
        def loadq(nn, tok, W, b=b):
            qt, kqt = Qt[nn % 2]
            add("sp", lambda e: e.dma_start(
                out=qt[:, :, 0:W], in_=g["QTd"][b, :, :, tok:tok + W].rearrange("h d t -> d h t")),
                r=[("QT", b)], w=[kqt], dma="ld_q%d" % (nn % 2))

        loadq(n, qtiles[0][0], qtiles[0][1])
        for qi, (tok, W, kcs) in enumerate(qtiles):
            qt, kqt = Qt[n % 2]
            ob, kob = Osb[n % 2]
            if qi + 1 < len(qtiles):
                loadq(n + 1, qtiles[qi + 1][0], qtiles[qi + 1][1])
            groups = [kcs[i:i + 2] for i in range(0, len(kcs), 2)]
            work = [(h, gi) for h in range(H) for gi in range(len(groups))]
            slot_of = {}
            deferred = []

            def emitS(idx):
                nonlocal gcount
                h, gi = work[idx]
                slot = gcount % 2
                gcount += 1
                slot_of[idx] = slot
                for q, kc in enumerate(groups[gi]):
                    add("pe", lambda e, h=h, kc=kc, slot=slot, q=q, qt=qt, W=W: e.matmul(
                        ps[:, 2 * slot + q, 0:W], Kall[:, h, kc * 128:(kc + 1) * 128], qt[:, h, 0:W],
                        start=True, stop=True), r=[kKall, kqt], w=[PK(2 * slot + q)])

            def emitExpPV(idx):
                h, gi = work[idx]
                slot = slot_of[idx]
                ng = len(groups[gi])
                pb, kpb = Pb[slot]
                obank = 4 + (hc + h) % 2
                add("act", lambda e, slot=slot, pb=pb, ng=ng, W=W: e.activation(
                    out=pb[:, 0:ng, 0:W], in_=ps[:, 2 * slot:2 * slot + ng, 0:W], func=AF.Exp, scale=ATTN_SCALE),
                    r=[PK(2 * slot + q) for q in range(ng)], w=[kpb])
                for q, kc in enumerate(groups[gi]):
                    first = gi == 0 and q == 0
                    lastmm = gi == len(groups) - 1 and q == ng - 1
                    add("pe", lambda e, h=h, kc=kc, pb=pb, q=q, obank=obank, first=first, lastmm=lastmm, W=W: e.matmul(
                        ps[0:65, obank, 0:W], Vall[:, kc, h * 65:(h + 1) * 65], pb[:, q, 0:W], start=first, stop=lastmm),
                        r=[kVall, kpb], w=[PK(obank)])

            def epilogue_a(h):
                obank = 4 + (hc + h) % 2
                add("dve", lambda e, obank=obank, W=W: e.reciprocal(out=rd[64:65, 0:W], in_=ps[64:65, obank, 0:W]),
                    r=[PK(obank)], w=[krd])

            def epilogue_b(h):
                obank = 4 + (hc + h) % 2
                bb = 6 + (hc + h) % 2
                bc, kbc = bcs[(hc + h) % 2]
                add("pe", lambda e, bb=bb, W=W: e.matmul(ps[0:64, bb, 0:W], onesf[64:65, 0:64], rd[64:65, 0:W],
                                                         start=True, stop=True), r=[krd, k_onesf], w=[PK(bb)])
                add("dve", lambda e, bb=bb, bc=bc, W=W: e.tensor_copy(out=bc[:, 0:W], in_=ps[0:64, bb, 0:W]),
                    r=[PK(bb)], w=[kbc])
                add("dve", lambda e, h=h, ob=ob, obank=obank, bc=bc, W=W: e.tensor_tensor(
                    out=ob[:, h, 0:W], in0=ps[0:64, obank, 0:W], in1=bc[:, 0:W], op=ALU.mult),
                    r=[PK(obank), kbc], w=[kob])

            emitS(0)
            for idx in range(len(work)):
                if idx + 1 < len(work):
                    emitS(idx + 1)
                emitExpPV(idx)
                h, gi = work[idx]
                if deferred and gi == min(5, len(groups) - 1):
                    epilogue_b(deferred.pop(0))
                if gi == len(groups) - 1:
                    epilogue_a(h)
                    deferred.append(h)
            while deferred:
                epilogue_b(deferred.pop(0))
            hc += H
            add("sp", lambda e, ob=ob, b=b, tok=tok, W=W: e.dma_start(
                out=g["MIXd"][b, 512:1024, tok:tok + W].rearrange("(h d) t -> d h t", h=H), in_=ob[:, :, 0:W]),
                r=[kob], w=[("MIX", b)], dma="st_o%d" % (n % 2))
            n += 1


def p4(nc, S, A, base_mark, l, env):
    g = env
    add = S.add
    NB, NT, NTL = g["NB"], g["NT"], g["NTL"]
    ps, PK, cwv, k_cwv = g["ps"], g["PK"], g["cwv"], g["k_cwv"]
    it = 0
    for si, (t0, N, R) in enumerate(g["segs"]):
        if g["last"] and not DBG_CTX and si == 1:
            continue
        if si > 0:
            S.barrier()
        A.off = base_mark
        m1, km1 = A.alloc([2 * R, 2 * R], BF16, "m1")
        G2, kG2 = A.alloc([128, N], BF16, "G2")
        D2, kD2 = A.alloc([2 * R, 64 * 256], BF16, "D2")
        Y1, kY1 = A.alloc([2 * R, 64 * 256], BF16, "Y1")
        Y2, kY2 = A.alloc([128, R, 256], BF16, "Y2")
        Fo, kFo = A.alloc([128, 2, N], BF16, "Fo")
        add("pool", lambda e, m1=m1, si=si: e.dma_start(out=m1, in_=g["m1_in"][si]), w=[km1], dma="w0")
        add("pool", lambda e, G2=G2, si=si: e.dma_start(out=G2, in_=g["g2_in"][si]), w=[kG2], dma="w1")
        for b in range(NB):
            for ri in range(2):
                add("sp", lambda e, b=b, ri=ri, D2=D2, R=R, t0=t0, N=N: e.dma_start(
                    out=D2[ri * R:(ri + 1) * R, :],
                    in_=g["Ud"][b, ri, t0:t0 + N, :].rearrange("(a n) c -> a (n c)", a=R)),
                    r=[("Ud", b)], w=[kD2], dma="ld_d%d" % ri)
            for j in range(32):
                bank = j % 4
                add("pe", lambda e, j=j, bank=bank, m1=m1, D2=D2, R=R: e.matmul(
                    ps[0:2 * R, bank, :], m1, D2[:, j * 512:(j + 1) * 512], start=True, stop=True),
                    r=[km1, kD2], w=[PK(bank)])
                if j % 2 == 0:
                    add("act", lambda e, j=j, bank=bank, Y1=Y1, R=R: e.copy(out=Y1[:, j * 512:(j + 1) * 512],
                                                                            in_=ps[0:2 * R, bank, :]),
                        r=[PK(bank)], w=[kY1 + "a"])
                else:
                    add("dve", lambda e, j=j, bank=bank, Y1=Y1, R=R: e.tensor_copy(out=Y1[:, j * 512:(j + 1) * 512],
                                                                                   in_=ps[0:2 * R, bank, :]),
                        r=[PK(bank)], w=[kY1 + "d"])
            for ri in range(2):
                add("sp", lambda e, b=b, ri=ri, Y1=Y1, R=R, t0=t0, N=N: e.dma_start(
                    out=g["Yd"][b, ri, t0:t0 + N, :].rearrange("(a n) c -> a (n c)", a=R),
                    in_=Y1[ri * R:(ri + 1) * R, :]),
                    r=[kY1 + "a", kY1 + "d"], w=[("Yd", b, si)], dma="st_y%d" % ri)
            for ri in range(2):
                for a0 in range(0, R, 16):
                    a1 = min(R, a0 + 16)
                    add("sp", lambda e, b=b, ri=ri, Y2=Y2, R=R, t0=t0, N=N, a0=a0, a1=a1: e.dma_start(
                        out=Y2[ri * 64:(ri + 1) * 64, a0:a1, :],
                        in_=g["Yd"][b, ri, t0:t0 + N, :].rearrange("(a n) c -> n a c", a=R)[:, a0:a1, :]),
                        r=[("Yd", b, si)], w=[kY2], dma="ld_y%d" % ri)
            ng = (R + 7) // 8
            for cc in range(2):
                for gi in range(ng):
                    nk = min(8, R - gi * 8)
                    bank = 4 + (it % 4)
                    it += 1
                    for kl in range(nk):
                        k1 = gi * 8 + kl
                        add("pe", lambda e, k1=k1, kl=kl, cc=cc, bank=bank, Y2=Y2, G2=G2: e.matmul(
                            ps[:, bank, kl * 64:(kl + 1) * 64], Y2[:, k1, cc * 128:(cc + 1) * 128],
                            G2[:, k1 * 64:(k1 + 1) * 64], start=True, stop=True, skip_group_check=True),
                            r=[kY2, kG2], w=[PK(bank)])
                    add("dve", lambda e, cc=cc, gi=gi, nk=nk, bank=bank, Fo=Fo, R=R: e.tensor_copy(
                        out=Fo[:, cc, :].rearrange("p (k2 k1) -> p k1 k2", k1=R)[:, gi * 8:gi * 8 + nk, :],
                        in_=ps[:, bank, 0:nk * 64].rearrange("p (a b) -> p a b", a=nk)),
                        r=[PK(bank)], w=[kFo])
            add("sp", lambda e, b=b, Fo=Fo, t0=t0, N=N: e.dma_start(
                out=g["MIXd"][b, 0:256, t0:t0 + N].rearrange("(c p) t -> p c t", p=128), in_=Fo),
                r=[kFo], w=[("MIX", b)], dma="st_f")
    S.barrier()
    A.off = base_mark
    CW = 1024
    cu = [A.alloc([128, 2, CW + 2], F32, "cu%d" % i) for i in range(2)]
    cg = [A.alloc([128, 2, CW], F32, "cg%d" % i) for i in range(2)]
    cy = [A.alloc([128, 2, CW], F32, "cy%d" % i) for i in range(2)]
    co = [A.alloc([128, 2, CW], BF16, "co%d" % i) for i in range(2)]
    ctl = []
    for b in range(NB):
        for si, (t0, N, R) in enumerate(g["segs"]):
            if g["last"] and not DBG_CTX and si == 1:
                continue
            w = min(CW, N)
            for a in range(t0, t0 + N, w):
                ctl.append((b, t0, N, a, w))

    def cloads(n, b, t0, N, a, w):
        u, kcu = cu[n % 2]
        gb, kcg = cg[n % 2]
        cvu = g["CVd"][b, 1].rearrange("(c p) t -> p c t", p=128)
        cvg = g["CVd"][b, 0].rearrange("(c p) t -> p c t", p=128)
        lo = a - 1 if a > t0 else a
        hi = a + w + 1 if a + w < t0 + N else a + w
        if a == t0:
            add("pool", lambda e: e.memset(u[:, :, 0:1], 0.0), w=[kcu])
        if a + w == t0 + N:
            add("pool", lambda e: e.memset(u[:, :, w + 1:w + 2], 0.0), w=[kcu])
        add("sp", lambda e: e.dma_start(out=u[:, :, 1 - (a - lo):1 + (hi - a)], in_=cvu[:, :, lo:hi]),
            r=[("CV", b)], w=[kcu], dma="ld_cu%d" % (n % 2))
        add("sp", lambda e: e.dma_start(out=gb[:, :, 0:w], in_=cvg[:, :, a:a + w]),
            r=[("CV", b)], w=[kcg], dma="ld_cg%d" % (n % 2))

    def ccompute(n, b, t0, N, a, w):
        u, kcu = cu[n % 2]
        gb, kcg = cg[n % 2]
        yy, kcy = cy[n % 2]
        oo, kco = co[n % 2]
        for c in range(2):
            add("dve", lambda e, c=c: e.tensor_scalar_mul(out=yy[:, c, 0:w], in0=u[:, c, 1:w + 1],
                                                          scalar1=cwv[:, l, 1, c:c + 1]),
                r=[kcu, k_cwv], w=[kcy + str(c)])
            add("dve", lambda e, c=c: e.scalar_tensor_tensor(
                out=yy[:, c, 0:w], in0=u[:, c, 0:w], scalar=cwv[:, l, 0, c:c + 1], in1=yy[:, c, 0:w],
                op0=ALU.mult, op1=ALU.add), r=[kcu, k_cwv, kcy + str(c)], w=[kcy + str(c)])
            add("dve", lambda e, c=c: e.scalar_tensor_tensor(
                out=yy[:, c, 0:w], in0=u[:, c, 2:w + 2], scalar=cwv[:, l, 2, c:c + 1], in1=yy[:, c, 0:w],
                op0=ALU.mult, op1=ALU.add), r=[kcu, k_cwv, kcy + str(c)], w=[kcy + str(c)])
            add("dve", lambda e, c=c: e.tensor_tensor(out=oo[:, c, 0:w], in0=yy[:, c, 0:w], in1=gb[:, c, 0:w], op=ALU.mult),
                r=[kcg, kcy + str(c)], w=[kco + str(c)])
        add("sp", lambda e: e.dma_start(
            out=g["MIXd"][b, 256:512, a:a + w].rearrange("(c p) t -> p c t", p=128), in_=oo[:, :, 0:w]),
            r=[kco + "0", kco + "1"], w=[("MIX", b)], dma="st_cv%d" % (n % 2))

    if ctl:
        cloads(0, *ctl[0])
    for n, tl_ in enumerate(ctl):
        if n + 1 < len(ctl):
            cloads(n + 1, *ctl[n + 1])
        ccompute(n, *tl_)


def _consts(SEQ, CTX):
    NT = SEQ + CTX
    out = {"ident": np.eye(128, dtype=np.float32)}
    cs = np.zeros((256, 512), np.float64)
    jj, cc = np.meshgrid(np.arange(64), np.arange(64), indexing="ij")
    for g in range(4):
        ang = 2 * np.pi * (jj * cc % 64) / 64.0
        cs[g * 64:(g + 1) * 64, g * 64:(g + 1) * 64] = np.cos(ang).T
        cs[g * 64:(g + 1) * 64, 256 + g * 64:256 + (g + 1) * 64] = -np.sin(ang).T
    out["cs"] = cs.astype(np.float32)
    for name, N in (("x", SEQ), ("c", CTX)):
        R = N // 64
        n1, k1 = np.meshgrid(np.arange(R), np.arange(R), indexing="ij")
        ang = 2 * np.pi * (n1 * k1 % R) / R
        Fr, Fi = np.cos(ang), -np.sin(ang)
        m1 = np.zeros((2 * R, 2 * R))
        m1[:R, :R] = Fr
        m1[R:, :R] = -Fi
        m1[:R, R:] = Fi
        m1[R:, R:] = Fr
        out["m1" + name] = m1.astype(np.float32)
        n2 = np.arange(64)[:, None]
        k = np.arange(N)[None, :]
        kk = (np.arange(N) // 64) + R * (np.arange(N) % 64)
        ang = 2 * np.pi * ((n2 * kk[None, :]) % N) / N
        sc = 1.0 / np.sqrt(N * 64.0)
        g2 = np.concatenate([np.cos(ang), np.sin(ang)], axis=0) * sc
        out["g2" + name] = g2.astype(np.float32)
    rows = SEQ // 64
    row = np.repeat(np.arange(rows), 64).astype(np.float32)
    col = np.tile(np.arange(64), rows).astype(np.float32)
    inv = (1.0 / (np.float32(10000.0) ** (np.arange(0, 16, 2, dtype=np.float32) / np.float32(16)))).astype(np.float32)
    cosT = np.ones((96, NT), np.float32)
    sinT = np.zeros((96, NT), np.float32)
    for axis, pos in enumerate((row, col)):
        ang = pos[None, :] * inv[:, None]
        for half in range(2):
            r0 = 64 + axis * 16 + half * 8
            cosT[r0:r0 + 8, :SEQ] = np.cos(ang)
            sinT[r0:r0 + 8, :SEQ] = np.sin(ang) * (-1.0 if half == 0 else 1.0)
    out["cosT"], out["sinT"] = cosT, sinT
    return out


def make_in_maps(inp, SEQ, CTX, NB, L, ncores):
    consts = _consts(SEQ, CTX)
    shared = {k: np.ascontiguousarray(inp[k], dtype=np.float32) for k in (
        "w_mod", "b_mod", "ffn1_norm", "mix_norm", "ffn2_norm", "ffn1_w_in", "ffn2_w_in", "ffn1_w_out",
        "ffn2_w_out", "w_in", "conv_w", "q_norm", "w_uq", "kv_norm", "w_ukv", "w_out", "final_norm")}
    shared.update(consts)
    maps = []
    for i in range(ncores):
        m = dict(shared)
        m["x"] = np.ascontiguousarray(inp["x"][i * NB:(i + 1) * NB])
        m["ctx"] = np.ascontiguousarray(inp["ctx"][i * NB:(i + 1) * NB])
        m["c3"] = np.ascontiguousarray(np.concatenate([inp["c"][i * NB:(i + 1) * NB], inp["c_ctx"][None, :]], axis=0))
        maps.append(m)
    return maps


def kernel(**inputs):
    SEQ, CTX, NB, L, ncores = 4096, 256, 2, 4, 8
    inp = {k: np.asarray(v) for k, v in inputs.items()}
    nc = build(SEQ, CTX, NB, L)
    maps = make_in_maps(inp, SEQ, CTX, NB, L, ncores)
    res = run_bass_kernel_spmd(nc, maps, core_ids=list(range(ncores)))
    return np.concatenate([np.asarray(r["y"]) for r in res.results], axis=0).astype(np.float32)
```

```python
import numpy as np
from contextlib import ExitStack
import concourse.bass as bass
import concourse.mybir as mybir
from concourse.bass_utils import run_bass_kernel_spmd

F32 = mybir.dt.float32
BF16 = mybir.dt.bfloat16
AF = mybir.ActivationFunctionType
ALU = mybir.AluOpType

D = 1024
DFF = 2816
NCH = 8
FCH = 22
T = 256
H = 8
OFF_CB, OFF_CC, OFF_CX, OFF_Q, OFF_KV, OFF_KR, IN_COLS = 256, 512, 768, 1024, 1408, 1664, 1696
EPS = 1e-6
ATTN_SCALE = 96.0 ** -0.5
DBG_CTX = False


class Op:
    __slots__ = ("eng", "fn", "deps", "dma", "val", "sem", "needs_inc", "idx")

    def __init__(self, eng, fn, dma):
        self.eng = eng
        self.fn = fn
        self.dma = dma
        self.deps = []
        self.val = 0
        self.sem = None
        self.needs_inc = False


class Sched:
    ENGS = ("pe", "act", "dve", "pool", "sp")

    def __init__(self):
        self.ops = {e: [] for e in self.ENGS}
        self.last_w = {}
        self.readers = {}
        self.last_dma = {}
        self.all = []
        self.n = 0

    def add(self, eng, fn, r=(), w=(), dma=None):
        op = Op(eng, fn, dma)
        op.idx = self.n
        self.n += 1
        deps = {}
        for k in r:
            p = self.last_w.get(k)
            if p is not None:
                deps[p.idx] = (p, "raw")
        for k in w:
            p = self.last_w.get(k)
            if p is not None and p.idx not in deps:
                deps[p.idx] = (p, "waw")
            for q in self.readers.get(k, ()):
                if q.idx not in deps:
                    deps[q.idx] = (q, "war")
        if dma is not None:
            p = self.last_dma.get(dma)
            if p is not None:
                deps[p.idx] = (p, "raw")
            self.last_dma[dma] = op
        for p, kind in deps.values():
            if p.dma is None and dma is None and p.eng == eng:
                if eng == "pe" or kind != "raw":
                    continue
            op.deps.append(p)
            if p.dma is None:
                p.needs_inc = True
        for k in r:
            self.readers.setdefault(k, []).append(op)
        for k in w:
            self.last_w[k] = op
            self.readers[k] = []
        self.ops[eng].append(op)
        self.all.append(op)
        return op

    def fence(self, eng, keys):
        return self.add(eng, None, r=keys)

    def barrier(self):
        lasts = []
        for e in self.ENGS:
            for op in reversed(self.ops[e]):
                if op.fn is not None and op.dma is None:
                    lasts.append(op)
                    break
        dmas = list(self.last_dma.values())
        for e in self.ENGS:
            op = Op(e, None, None)
            op.idx = self.n
            self.n += 1
            for p in lasts:
                if p.eng != e:
                    op.deps.append(p)
                    p.needs_inc = True
            op.deps.extend(dmas)
            self.ops[e].append(op)
        self.last_w = {}
        self.readers = {}

    def emit(self, nc, stack):
        esem = {e: stack.enter_context(nc.semaphore("s_" + e)) for e in self.ENGS}
        dsem = {}
        cnt = {e: 0 for e in self.ENGS}
        for op in self.all:
            if op.dma is not None:
                if op.dma not in dsem:
                    dsem[op.dma] = [stack.enter_context(nc.semaphore("d_%d" % len(dsem))), 0]
                ent = dsem[op.dma]
                ent[1] += 16
                op.sem, op.val = ent[0], ent[1]
            elif op.needs_inc:
                cnt[op.eng] += 1
                op.sem, op.val = esem[op.eng], cnt[op.eng]
        self.n_sems = len(dsem) + 5
        block = stack.enter_context(nc.Block())

        def run(e, eng):
            waited = {}
            for op in self.ops[e]:
                need = {}
                for p in op.deps:
                    key = id(p.sem)
                    if waited.get(key, 0) >= p.val:
                        continue
                    if key not in need or need[key][1] < p.val:
                        need[key] = (p.sem, p.val)
                for key, (s, v) in need.items():
                    eng.wait_ge(s, v)
                    waited[key] = v
                if op.fn is None:
                    continue
                ins = op.fn(eng)
                if op.dma is not None:
                    ins.then_inc(op.sem, 16)
                elif op.needs_inc:
                    ins.then_inc(op.sem, 1)

        block.tensor(lambda eng: run("pe", eng))
        block.scalar(lambda eng: run("act", eng))
        block.vector(lambda eng: run("dve", eng))
        block.gpsimd(lambda eng: run("pool", eng))
        block.sync(lambda eng: run("sp", eng))


class Arena:
    def __init__(self, nc, nbytes):
        self.t = nc.alloc_sbuf_tensor("arena", [128, nbytes // 4], F32)
        self.off = 0
        self.cap = nbytes
        self.n = 0

    def alloc(self, shape, dtype, key=None):
        esz = 2 if dtype == BF16 else 4
        n = 1
        for s in shape[1:]:
            n *= s
        nb = (n * esz + 63) // 64 * 64
        assert self.off + nb <= self.cap, ("SBUF arena overflow", self.off, nb, self.cap)
        ap = self.t[:, self.off // 4:(self.off + nb) // 4]
        if dtype == BF16:
            ap = ap.bitcast(BF16)
        ap = ap[:, 0:n]
        if len(shape) == 3:
            ap = ap.rearrange("p (a b) -> p a b", a=shape[1])
        elif len(shape) == 4:
            ap = ap.rearrange("p (a b c) -> p a b c", a=shape[1], b=shape[2])
        if shape[0] < 128:
            ap = ap[0:shape[0]]
        self.off += nb
        self.n += 1
        return ap, (key or "t%d" % self.n)


def build(SEQ, CTX, NB, L, debug=False):
    NT = SEQ + CTX
    NTL = NT // T
    NR = NB + 1
    segs = [(0, SEQ, SEQ // 64), (SEQ, CTX, CTX // 64)]
    nc = bass.Bass("TRN2", target_bir_lowering=False)
    S = Sched()
    add = S.add

    def din(name, shape):
        return nc.dram_tensor(name, list(shape), F32, kind="ExternalInput").ap()

    x_in = din("x", [NB, SEQ, D])
    ctx_in = din("ctx", [NB, CTX, D])
    c3 = din("c3", [NR, D])
    w_mod = din("w_mod", [L, D, 9 * D])
    b_mod = din("b_mod", [L, 9 * D])
    ffn_norm = [din("ffn1_norm", [L, D]), din("mix_norm", [L, D]), din("ffn2_norm", [L, D])]
    ffn_win = [din("ffn1_w_in", [L, D, 2 * DFF]), din("ffn2_w_in", [L, D, 2 * DFF])]
    ffn_wout = [din("ffn1_w_out", [L, DFF, D]), din("ffn2_w_out", [L, DFF, D])]
    w_in = din("w_in", [L, D, IN_COLS])
    conv_w = din("conv_w", [L, 3, 256])
    q_norm = din("q_norm", [L, 384])
    w_uq = din("w_uq", [L, 384, 768])
    kv_norm = din("kv_norm", [L, 256])
    w_ukv = din("w_ukv", [L, 256, 1024])
    w_out = din("w_out", [L, D, D])
    final_norm = din("final_norm", [D])
    ident_in = din("ident", [128, 128])
    cs_in = din("cs", [256, 512])
    m1_in = [din("m1x", [2 * segs[0][2]] * 2), din("m1c", [2 * segs[1][2]] * 2)]
    g2_in = [din("g2x", [128, SEQ]), din("g2c", [128, CTX])]
    cos_in = din("cosT", [96, NT])
    sin_in = din("sinT", [96, NT])
    y_out = nc.dram_tensor("y", [NB, SEQ, D], F32, kind="ExternalOutput").ap()

    dk = "ExternalOutput" if debug else "Internal"
    sT = nc.dram_tensor("sT", [NB, D, NT], F32, kind=dk).ap()
    Ud = nc.dram_tensor("Ud", [NB, 2, NT, 256], BF16, kind=dk).ap()
    Yd = nc.dram_tensor("Yd", [NB, 2, NT, 256], BF16, kind=dk).ap()
    CVd = nc.dram_tensor("CVd", [NB, 2, 256, NT], F32, kind=dk).ap()
    QTd = nc.dram_tensor("QTd", [NB, H, 96, NT], BF16, kind=dk).ap()
    KTd = nc.dram_tensor("KTd", [NB, H, 96, NT], BF16, kind=dk).ap()
    Vd = nc.dram_tensor("Vd", [NB, NT, H * 65], BF16, kind=dk).ap()
    MIXd = nc.dram_tensor("MIXd", [NB, D, NT], BF16, kind=dk).ap()

    with ExitStack() as st:
        A = Arena(nc, 207 * 1024)
        ps = nc.alloc_psum_tensor("ps", [128, 8, 512], F32)
        nc_allow = st.enter_context(nc.allow_non_contiguous_dma("small per-partition vectors"))

        def PK(b):
            return ("ps", b)

        ident, k_ident = A.alloc([128, 128], F32, "ident")
        ones_d, k_ones = A.alloc([128, 3, 128], BF16, "ones")
        onesf, k_onesf = A.alloc([128, 64], F32, "onesf")
        epst, k_eps = A.alloc([128, 1], F32, "eps")
        scT, k_scT = A.alloc([128, NCH, NR], F32, "scT")
        normv, k_normv = A.alloc([128, 3, L, NCH], F32, "normv")
        fnormv, k_fnormv = A.alloc([128, NCH], F32, "fnormv")
        qnv, k_qnv = A.alloc([128, L, 3], F32, "qnv")
        kvnv, k_kvnv = A.alloc([128, L, 2], F32, "kvnv")
        cwv, k_cwv = A.alloc([128, L, 3, 2], F32, "cwv")
        modT, k_modT = A.alloc([128, 72, NR], F32, "modT")
        Gv, k_Gv = A.alloc([128, 3, NCH, NR], F32, "Gv")
        gatev, k_gatev = A.alloc([128, 3, NCH, NR], F32, "gatev")
        rstd_l = [A.alloc([128, T], F32, "rstd0")] * 2
        sqb_l = [A.alloc([128, NCH, T], BF16, "sqb0")] * 2
        tn_l = [A.alloc([128, NCH, T], F32, "tn0")] * 2
        hT_l = [A.alloc([128, NCH, T], BF16, "hT%d" % i) for i in range(2)]
        hT, k_hT = hT_l[0]
        stile = [A.alloc([128, NCH, T], F32, "stile%d" % i) for i in range(2)]
        base_mark = A.off

        add("sp", lambda e: e.dma_start(out=ident, in_=ident_in), w=[k_ident], dma="c0")
        for i, v in enumerate((1.0 / 1024, 1.0 / 384, 1.0 / 256)):
            add("pool", lambda e, i=i, v=v: e.memset(ones_d[:, i, :], v), w=[k_ones])
        add("pool", lambda e: e.memset(onesf, 1.0), w=[k_onesf])
        add("pool", lambda e: e.memset(epst, EPS), w=[k_eps])
        for i in range(3):
            for l in range(L):
                add("sp", lambda e, i=i, l=l: e.dma_start(
                    out=normv[:, i, l, :], in_=ffn_norm[i][l].rearrange("(c p) -> p c", p=128)),
                    w=[k_normv], dma="c1")
        add("sp", lambda e: e.dma_start(out=fnormv, in_=final_norm.rearrange("(c p) -> p c", p=128)),
            w=[k_fnormv], dma="c2")
        for l in range(L):
            add("sp", lambda e, l=l: e.dma_start(out=qnv[:, l, :], in_=q_norm[l].rearrange("(c p) -> p c", p=128)),
                w=[k_qnv], dma="c3")
            add("sp", lambda e, l=l: e.dma_start(out=kvnv[:, l, :], in_=kv_norm[l].rearrange("(c p) -> p c", p=128)),
                w=[k_kvnv], dma="c4")
            for k in range(3):
                add("sp", lambda e, l=l, k=k: e.dma_start(
                    out=cwv[:, l, k, :], in_=conv_w[l, k].rearrange("(c p) -> p c", p=128)),
                    w=[k_cwv], dma="c5")

        xin = [A.alloc([128, D], F32, "xin%d" % i) for i in range(2)]
        xo = [A.alloc([128, NCH, 128], F32, "xo%d" % i) for i in range(2)]
        it = 0
        for b in range(NB):
            for (t0, n, _), src in zip(segs, (x_in, ctx_in)):
                for blk in range(n // 128):
                    xi, kxi = xin[it % 2]
                    xoo, kxo = xo[it % 2]
                    add("sp", lambda e, xi=xi, src=src, b=b, blk=blk: e.dma_start(
                        out=xi, in_=src[b, blk * 128:(blk + 1) * 128, :]), w=[kxi], dma="xin%d" % (it % 2))
                    for half in range(2):
                        bank = 2 * (it % 2) + half
                        for q in range(4):
                            c = half * 4 + q
                            add("pe", lambda e, xi=xi, bank=bank, q=q, c=c: e.transpose(
                                ps[:, bank, q * 128:(q + 1) * 128], xi[:, c * 128:(c + 1) * 128], ident),
                                r=[kxi, k_ident], w=[PK(bank)])
                        eng = "act" if half == 0 else "dve"
                        if eng == "act":
                            add("act", lambda e, xoo=xoo, bank=bank, half=half: e.copy(
                                out=xoo[:, half * 4:(half + 1) * 4, :],
                                in_=ps[:, bank, :].rearrange("p (a b) -> p a b", a=4)),
                                r=[PK(bank)], w=[kxo + "h%d" % half])
                        else:
                            add("dve", lambda e, xoo=xoo, bank=bank, half=half: e.tensor_copy(
                                out=xoo[:, half * 4:(half + 1) * 4, :],
                                in_=ps[:, bank, :].rearrange("p (a b) -> p a b", a=4)),
                                r=[PK(bank)], w=[kxo + "h%d" % half])
                    tok = t0 + blk * 128
                    add("sp", lambda e, xoo=xoo, b=b, tok=tok: e.dma_start(
                        out=sT[b].rearrange("(c p) t -> p c t", p=128)[:, :, tok:tok + 128], in_=xoo),
                        r=[kxo + "h0", kxo + "h1"], w=[("sT", b, tok // T)], dma="xo%d" % (it % 2))
                    it += 1
        crow, k_crow = A.alloc([NR, D], F32, "crow")
        add("sp", lambda e: e.dma_start(out=crow, in_=c3), w=[k_crow], dma="c6")
        add("act", lambda e: e.activation(out=crow, in_=crow, func=AF.Silu), r=[k_crow], w=[k_crow])
        for c in range(NCH):
            add("pe", lambda e, c=c: e.transpose(ps[:, 7, c * NR:(c + 1) * NR], crow[:, c * 128:(c + 1) * 128],
                                                  ident[0:NR, 0:NR]), r=[k_crow, k_ident], w=[PK(7)])
        add("dve", lambda e: e.tensor_copy(out=scT, in_=ps[:, 7, 0:NCH * NR].rearrange("p (a b) -> p a b", a=NCH)),
            r=[PK(7)], w=[k_scT])
        S.barrier()

        def norm_mod_g(stl, kst, i, r, dst, kdst, nchunk=NCH, ones_i=0, gain=None, bias=None, bank=7, bs=0, scratch=None):
            (rstd, k_rstd), (sqb, k_sqb), (tn, k_tn) = scratch if scratch is not None else (rstd_l[bs], sqb_l[bs], tn_l[bs])
            add("act", lambda e: e.activation(out=sqb[:, 0:nchunk, :], in_=stl[:, 0:nchunk, :], func=AF.Square),
                r=[kst], w=[k_sqb])
            yield
            for c in range(nchunk):
                add("pe", lambda e, c=c: e.matmul(ps[:, bank, 0:T], ones_d[:, ones_i, :], sqb[:, c, :],
                                                  start=(c == 0), stop=(c == nchunk - 1)),
                    r=[k_sqb, k_ones], w=[PK(bank)])
            yield
            add("act", lambda e: e.activation(out=rstd, in_=ps[:, bank, 0:T], func=AF.Sqrt, bias=epst[:, 0:1], scale=1.0),
                r=[PK(bank), k_eps], w=[k_rstd])
            yield
            add("dve", lambda e: e.reciprocal(out=rstd, in_=rstd), r=[k_rstd], w=[k_rstd])
            yield
            for c in range(nchunk):
                add("dve", lambda e, c=c: e.tensor_tensor(out=tn[:, c, :], in0=stl[:, c, :], in1=rstd, op=ALU.mult),
                    r=[kst, k_rstd], w=[k_tn + str(c)])
                g = gain(c)
                bb = bias(c) if bias is not None else 0.0
                add("act", lambda e, c=c, g=g, bb=bb: e.activation(out=dst[:, c, :], in_=tn[:, c, :], func=AF.Identity,
                                                                     scale=g, bias=bb),
                    r=[k_tn + str(c), k_Gv, k_modT, k_qnv, k_kvnv], w=[kdst])
                if c % 2 == 1:
                    yield

        def norm_mod(*a_, **k_):
            for _ in norm_mod_g(*a_, **k_):
                pass

        def load_ffn_weights(l, f, wi, kwi, wo, kwo):
            n = 0
            for q in (0, 2, 1, 3):
                for c in range(NCH):
                    add("pool", lambda e, q=q, c=c: e.dma_start(
                        out=wi[:, c, q * 1408:(q + 1) * 1408],
                        in_=ffn_win[f][l, c * 128:(c + 1) * 128, q * 1408:(q + 1) * 1408]),
                        w=[(kwi, c, q)], dma="w%d" % (n % 6))
                    n += 1
            for j in range(FCH):
                add("pool", lambda e, j=j: e.dma_start(out=wo[:, j, :], in_=ffn_wout[f][l, j * 128:(j + 1) * 128, :]),
                    w=[(kwo, j)], dma="w%d" % (n % 6))
                n += 1

        def ffn_pro(stl, kst, i, r, bs):
            hb, khb = hT_l[bs]
            norm_mod(stl, kst, i, r, hb, khb, bs=bs, bank=7,
                     gain=lambda c: Gv[:, i, c, r:r + 1], bias=lambda c: modT[:, (3 * i) * NCH + c, r:r + 1])

        def ffn_main(stl, kst, i, r, bs, wi, kwi, wo, kwo, actb, hook=None):
            hb, khb = hT_l[bs]

            def gu(j):
                bank = 4 + j % 3
                for half, col0 in ((0, j * 128), (1, DFF + j * 128)):
                    q = col0 // 1408
                    q2 = (col0 + 127) // 1408
                    for c in range(NCH):
                        add("pe", lambda e, c=c, half=half, col0=col0, bank=bank: e.matmul(
                            ps[:, bank, half * T:(half + 1) * T], wi[:, c, col0:col0 + 128], hb[:, c, :],
                            start=(c == 0), stop=(c == NCH - 1)),
                            r=[(kwi, c, q), (kwi, c, q2), khb], w=[PK(bank)])
                ab, kab = actb[j % 4]
                sg, ksg = actb[4 + j % 4]
                add("act", lambda e, bank=bank, sg=sg: e.activation(out=sg, in_=ps[:, bank, 0:T], func=AF.Silu),
                    r=[PK(bank)], w=[ksg])
                add("dve", lambda e, bank=bank, sg=sg, ab=ab: e.tensor_tensor(out=ab, in0=sg, in1=ps[:, bank, T:2 * T],
                                                                               op=ALU.mult),
                    r=[PK(bank), ksg], w=[kab])

            def yacc(j):
                ab, kab = actb[j % 4]
                for c in range(NCH):
                    add("pe", lambda e, c=c, j=j, ab=ab: e.matmul(
                        ps[:, c // 2, (c % 2) * T:(c % 2 + 1) * T], wo[:, j, c * 128:(c + 1) * 128], ab,
                        start=(j == 0 and c % 2 == 0), stop=(j == FCH - 1), skip_group_check=True),
                        r=[(kwo, j), kab], w=[PK(c // 2)])

            gu(0)
            gu(1)
            for j in range(FCH):
                yacc(j)
                if j + 2 < FCH:
                    gu(j + 2)
                if j == 5 and hook is not None:
                    hook()
            for c in range(NCH):
                add("dve", lambda e, c=c: e.scalar_tensor_tensor(
                    out=stl[:, c, :], in0=ps[:, c // 2, (c % 2) * T:(c % 2 + 1) * T], scalar=gatev[:, i, c, r:r + 1],
                    in1=stl[:, c, :], op0=ALU.mult, op1=ALU.add),
                    r=[PK(c // 2), k_gatev, kst], w=[kst])

        def sT_tile(b, t):
            return sT[b].rearrange("(c p) t -> p c t", p=128)[:, :, t * T:(t + 1) * T]

        for l in range(L):
            last = l == L - 1
            A.off = base_mark
            wm = [A.alloc([128, NCH, 512], F32, "wm%d" % i) for i in range(2)]
            bm = [A.alloc([1, 512], F32, "bm%d" % i) for i in range(2)]
            mrow, k_mrow = A.alloc([NR, 9 * D], F32, "mrow")
            for ct in range(18):
                wmt, kwm = wm[ct % 2]
                bmt, kbm = bm[ct % 2]
                add("sp", lambda e, wmt=wmt, ct=ct, l=l: e.dma_start(
                    out=wmt, in_=w_mod[l].rearrange("(c p) n -> p c n", p=128)[:, :, ct * 512:(ct + 1) * 512]),
                    w=[kwm], dma="wm%d" % (ct % 2))
                add("sp", lambda e, bmt=bmt, ct=ct, l=l: e.dma_start(out=bmt, in_=b_mod[l:l + 1, ct * 512:(ct + 1) * 512]),
                    w=[kbm], dma="bm%d" % (ct % 2))
                bank = ct % 2
                for c in range(NCH):
                    add("pe", lambda e, c=c, wmt=wmt, bank=bank: e.matmul(
                        ps[0:NR, bank, :], scT[:, c, :], wmt[:, c, :], start=(c == 0), stop=False),
                        r=[kwm, k_scT], w=[PK(bank)])
                add("pe", lambda e, bmt=bmt, bank=bank: e.matmul(ps[0:NR, bank, :], onesf[0:1, 0:NR], bmt,
                                                                    start=False, stop=True),
                    r=[kbm, k_onesf], w=[PK(bank)])
                add("act", lambda e, ct=ct, bank=bank: e.copy(out=mrow[:, ct * 512:(ct + 1) * 512], in_=ps[0:NR, bank, :]),
                    r=[PK(bank)], w=[k_mrow])
            for ch in range(72):
                add("pe", lambda e, ch=ch: e.transpose(ps[:, 2, ch * NR:(ch + 1) * NR], mrow[:, ch * 128:(ch + 1) * 128],
                                                        ident[0:NR, 0:NR]), r=[k_mrow, k_ident], w=[PK(2)])
            add("dve", lambda e: e.tensor_copy(out=modT, in_=ps[:, 2, 0:72 * NR].rearrange("p (a b) -> p a b", a=72)),
                r=[PK(2)], w=[k_modT])
            for i in range(3):
                add("dve", lambda e, i=i: e.tensor_scalar_add(out=Gv[:, i, :, :],
                                                             in0=modT[:, (3 * i + 1) * NCH:(3 * i + 2) * NCH, :], scalar1=1.0),
                    r=[k_modT], w=[k_Gv])
                for r in range(NR):
                    add("dve", lambda e, i=i, r=r, l=l: e.tensor_tensor(out=Gv[:, i, :, r], in0=Gv[:, i, :, r],
                                                                  in1=normv[:, i, l, :], op=ALU.mult),
                        r=[k_Gv, k_normv], w=[k_Gv])
                add("dve", lambda e, i=i: e.tensor_scalar_mul(out=gatev[:, i, :, :],
                                                             in0=modT[:, (3 * i + 2) * NCH:(3 * i + 3) * NCH, :],
                                                             scalar1=(1.0 if i == 1 else 0.5)),
                    r=[k_modT], w=[k_gatev])
            S.barrier()

            A.off = base_mark
            wi, kwi = A.alloc([128, NCH, 2 * DFF], BF16, "wi")
            wo, kwo = A.alloc([128, FCH, D], BF16, "wo")
            actb = [A.alloc([128, T], BF16, "actb%d" % i) for i in range(8)]
            ffn_mark = A.off
            load_ffn_weights(l, 0, wi, kwi, wo, kwo)
            tiles = [(b, t) for b in range(NB) for t in range(NTL)]

            def ffn_phase(i, tl, epilogue):
                def pro(n):
                    b, t = tl[n]
                    stl, kst = stile[n % 2]
                    r = NB if t == NTL - 1 else b
                    add("sp", lambda e, stl=stl, b=b, t=t: e.dma_start(out=stl, in_=sT_tile(b, t)),
                        r=[("sT", b, t)], w=[kst], dma="ld_s%d" % (n % 2))
                    ffn_pro(stl, kst, i, r, n % 2)
                pro(0)
                for n, (b, t) in enumerate(tl):
                    stl, kst = stile[n % 2]
                    r = NB if t == NTL - 1 else b
                    ffn_main(stl, kst, i, r, n % 2, wi, kwi, wo, kwo, actb,
                             hook=(lambda n=n: pro(n + 1)) if n + 1 < len(tl) else None)
                    epilogue(n, b, t, stl, kst)

            def store_s(n, b, t, stl, kst):
                add("sp", lambda e, stl=stl, b=b, t=t: e.dma_start(out=sT_tile(b, t), in_=stl),
                    r=[kst], w=[("sT", b, t)], dma="st_s%d" % (n % 2))

            ffn_phase(0, tiles, store_s)
            S.barrier()

            p2(nc, S, A, base_mark, l, locals())
            S.barrier()
            p3(nc, S, A, base_mark, l, locals())
            S.barrier()
            p4(nc, S, A, base_mark, l, locals())
            S.barrier()

            A.off = ffn_mark
            wom, kwom = A.alloc([128, NCH, D], BF16, "wom")
            if last:
                _m = A.off
                yo = [A.alloc([128, D], F32, "yo0")] * 2
                A.off = _m
            mixt = [A.alloc([128, NCH, T], BF16, "mixt%d" % i) for i in range(2)]
            for c in range(NCH):
                add("pool", lambda e, c=c, l=l: e.dma_start(out=wom[:, c, :], in_=w_out[l, c * 128:(c + 1) * 128, :]),
                    w=[(kwom, c)], dma="w%d" % (c % 6))
            load_ffn_weights(l, 1, wi, kwi, wo, kwo)
            tl5 = [(b, t) for (b, t) in tiles if not (last and t == NTL - 1)]
            def loads5(n, b, t):
                stl, kst = stile[n % 2]
                mx, kmx = mixt[n % 2]
                add("sp", lambda e: e.dma_start(out=stl, in_=sT_tile(b, t)),
                    r=[("sT", b, t)], w=[kst], dma="ld_s%d" % (n % 2))
                add("sp", lambda e: e.dma_start(
                    out=mx, in_=MIXd[b].rearrange("(c p) t -> p c t", p=128)[:, :, t * T:(t + 1) * T]),
                    r=[("MIX", b)], w=[kmx], dma="ld_m%d" % (n % 2))

            loads5(0, *tl5[0])
            for n, (b, t) in enumerate(tl5):
                stl, kst = stile[n % 2]
                mx, kmx = mixt[n % 2]
                r = NB if t == NTL - 1 else b
                bk0 = 4 * (n % 2)
                if n + 1 < len(tl5):
                    loads5(n + 1, *tl5[n + 1])
                for c in range(NCH):
                    for kc in range(NCH):
                        add("pe", lambda e, c=c, kc=kc, mx=mx, bk0=bk0: e.matmul(
                            ps[:, bk0 + c // 2, (c % 2) * T:(c % 2 + 1) * T], wom[:, kc, c * 128:(c + 1) * 128], mx[:, kc, :],
                            start=(kc == 0), stop=(kc == NCH - 1), skip_group_check=True),
                            r=[(kwom, kc), kmx], w=[PK(bk0 + c // 2)])
                for c in range(NCH):
                    add("dve", lambda e, c=c, stl=stl, r=r, bk0=bk0: e.scalar_tensor_tensor(
                        out=stl[:, c, :], in0=ps[:, bk0 + c // 2, (c % 2) * T:(c % 2 + 1) * T], scalar=gatev[:, 1, c, r:r + 1],
                        in1=stl[:, c, :], op0=ALU.mult, op1=ALU.add),
                        r=[PK(bk0 + c // 2), k_gatev, kst], w=[kst])
                add("sp", lambda e, stl=stl, b=b, t=t: e.dma_start(out=sT_tile(b, t), in_=stl),
                    r=[kst], w=[("sT", b, t)], dma="st_s%d" % (n % 2))

            if last:
                S.barrier()

            def final_out(n, b, t, stl, kst):
                rstd, k_rstd = rstd_l[n % 2]
                sqb, k_sqb = sqb_l[n % 2]
                tn, k_tn = tn_l[n % 2]
                add("act", lambda e: e.activation(out=sqb, in_=stl, func=AF.Square), r=[kst], w=[k_sqb])
                for c in range(NCH):
                    add("pe", lambda e, c=c: e.matmul(ps[:, 7, 0:T], ones_d[:, 0, :], sqb[:, c, :],
                                                      start=(c == 0), stop=(c == NCH - 1)),
                        r=[k_sqb, k_ones], w=[PK(7)])
                add("act", lambda e: e.activation(out=rstd, in_=ps[:, 7, 0:T], func=AF.Sqrt, bias=epst[:, 0:1], scale=1.0),
                    r=[PK(7), k_eps], w=[k_rstd])
                add("dve", lambda e: e.reciprocal(out=rstd, in_=rstd), r=[k_rstd], w=[k_rstd])
                for c in range(NCH):
                    add("dve", lambda e, c=c: e.scalar_tensor_tensor(
                        out=tn[:, c, :], in0=stl[:, c, :], scalar=fnormv[:, c:c + 1], in1=rstd,
                        op0=ALU.mult, op1=ALU.mult), r=[kst, k_rstd, k_fnormv], w=[k_tn + str(c)])
                for sub in range(T // 128):
                    yt, kyt = yo[sub]
                    for half in range(2):
                        bank = 5 + half
                        for q in range(4):
                            c = half * 4 + q
                            add("pe", lambda e, c=c, q=q, bank=bank, sub=sub: e.transpose(
                                ps[:, bank, q * 128:(q + 1) * 128], tn[:, c, sub * 128:(sub + 1) * 128], ident),
                                r=[k_tn + str(c), k_ident], w=[PK(bank)])
                        if half == 0:
                            add("act", lambda e, yt=yt, bank=bank: e.copy(out=yt[:, 0:512], in_=ps[:, bank, :]),
                                r=[PK(bank)], w=[kyt + "a"])
                        else:
                            add("dve", lambda e, yt=yt, bank=bank: e.tensor_copy(out=yt[:, 512:1024], in_=ps[:, bank, :]),
                                r=[PK(bank)], w=[kyt + "b"])
                    tok = t * T + sub * 128
                    add("sp", lambda e, yt=yt, b=b, tok=tok: e.dma_start(out=y_out[b, tok:tok + 128, :], in_=yt),
                        r=[kyt + "a", kyt + "b"], w=[("y", b, tok)], dma="yo%d" % sub)

            ffn_phase(2, tl5, final_out if last else store_s)
            S.barrier()
        S.fence("sp", [("y", b, tok) for b in range(NB) for tok in range(0, SEQ, 128)])
        S.emit(nc, st)
    return nc


def p2(nc, S, A, base_mark, l, env):
    g = env
    add = S.add
    NB, NT, NTL, NR, L = g["NB"], g["NT"], g["NTL"], g["NR"], g["L"]
    ps, PK = g["ps"], g["PK"]
    w_in, w_uq, w_ukv = g["w_in"], g["w_uq"], g["w_ukv"]
    stile, hT, k_hT = g["stile"], g["hT"], g["k_hT"]
    norm_mod, norm_mod_g, sT_tile = g["norm_mod"], g["norm_mod_g"], g["sT_tile"]
    Gv, modT, qnv, kvnv = g["Gv"], g["modT"], g["qnv"], g["kvnv"]
    A.off = base_mark
    wmx, kwmx = A.alloc([128, NCH, IN_COLS], BF16, "wmx")
    wkr, kwkr = A.alloc([128, NCH, 2, 96], BF16, "wkr")
    wuq, kwuq = A.alloc([128, 3, 768], BF16, "wuq")
    wuqr, kwuqr = A.alloc([128, 3, H, 96], BF16, "wuqr")
    wukv, kwukv = A.alloc([128, 2, 1024], BF16, "wukv")
    csm, kcsm = A.alloc([128, 2, 512], BF16, "csm")
    cosS = [A.alloc([96, T], F32, "cosS%d" % i) for i in range(2)]
    sinS = [A.alloc([96, T], F32, "sinS%d" % i) for i in range(2)]
    def alloc_set(i):
        d = {}
        d['zT'] = A.alloc([128, 2, T], BF16, "zT%d" % i)
        d['Usb'] = A.alloc([128, 2, 512], BF16, "Usb%d" % i)
        d['cb_sb'] = A.alloc([128, 2, T], F32, "cb_sb%d" % i)
        d['cc_sb'] = A.alloc([128, 2, T], F32, "cc_sb%d" % i)
        d['u_sb'] = A.alloc([128, 2, T], F32, "u_sb%d" % i)
        d['ql'] = A.alloc([128, 3, T], F32, "ql%d" % i)
        d['qn'] = A.alloc([128, 3, T], BF16, "qn%d" % i)
        d['kvl'] = A.alloc([128, 2, T], F32, "kvl%d" % i)
        d['kvn'] = A.alloc([128, 2, T], BF16, "kvn%d" % i)
        d['t1'] = A.alloc([96, 2, T], F32, "t1%d" % i)
        d['t2'] = A.alloc([96, 2, T], F32, "t2%d" % i)
        d['krf'] = A.alloc([96, T], BF16, "krf%d" % i)
        d['Qsb'] = A.alloc([96, H, T], BF16, "Qsb%d" % i)
        d['Ksb'] = A.alloc([96, H, T], BF16, "Ksb%d" % i)
        d['Vsb'] = A.alloc([128, 2, H, 65], BF16, "Vsb%d" % i)
        return d
    sets = [alloc_set(0), alloc_set(1)]
    scr = [(A.alloc([128, T], F32, "p2rstd%d" % i), A.alloc([128, NCH, T], BF16, "p2sqb%d" % i),
            A.alloc([128, NCH, T], F32, "p2tn%d" % i)) for i in range(2)]

    for c in range(NCH):
        add("pool", lambda e, c=c: e.dma_start(out=wmx[:, c, :], in_=w_in[l, c * 128:(c + 1) * 128, :]),
            w=[kwmx], dma="w%d" % (c % 6))
    add("pool", lambda e: e.memset(wkr, 0.0), w=[kwkr])
    add("pool", lambda e: e.memset(wuqr, 0.0), w=[kwuqr])
    for _d in sets:
        add("pool", lambda e, _d=_d: e.memset(_d["Vsb"][0], 1.0), w=[_d["Vsb"][1]])
    win_v = w_in[l].rearrange("(c p) n -> p c n", p=128)
    for c in range(NCH):
        add("pool", lambda e, c=c: e.dma_start(out=wkr[:, c, 0, 64:96], in_=win_v[:, c, OFF_KR:OFF_KR + 32]),
            w=[kwkr], dma="w0")
        for ax in range(2):
            for half in range(2):
                d0 = ax * 16 + half * 8
                p0 = ax * 16 + (1 - half) * 8
                add("pool", lambda e, c=c, d0=d0, p0=p0: e.dma_start(
                    out=wkr[:, c, 1, 64 + d0:64 + d0 + 8], in_=win_v[:, c, OFF_KR + p0:OFF_KR + p0 + 8]),
                    w=[kwkr], dma="w1")
    wuq_v = w_uq[l].rearrange("(c p) n -> p c n", p=128)
    for c in range(3):
        add("pool", lambda e, c=c: e.dma_start(out=wuq[:, c, :], in_=wuq_v[:, c, :]), w=[kwuq], dma="w2")
        for ax in range(2):
            for half in range(2):
                d0 = ax * 16 + half * 8
                p0 = ax * 16 + (1 - half) * 8
                add("pool", lambda e, c=c, d0=d0, p0=p0: e.dma_start(
                    out=wuqr[:, c, :, 64 + d0:64 + d0 + 8],
                    in_=wuq_v[:, c, :].rearrange("p (h d) -> p h d", h=H)[:, :, 64 + p0:64 + p0 + 8]),
                    w=[kwuqr], dma="w3")
    for c in range(2):
        add("pool", lambda e, c=c: e.dma_start(out=wukv[:, c, :], in_=w_ukv[l, c * 128:(c + 1) * 128, :]),
            w=[kwukv], dma="w4")
        add("pool", lambda e, c=c: e.dma_start(out=csm[:, c, :], in_=g["cs_in"][c * 128:(c + 1) * 128, :]),
            w=[kcsm], dma="w5")

    def do_tile(n, b, t, BS):
        zT, kzT = BS['zT']
        Usb, kUsb = BS['Usb']
        cb_sb, kcb = BS['cb_sb']
        cc_sb, kcc = BS['cc_sb']
        u_sb, ku = BS['u_sb']
        ql, kql = BS['ql']
        qn, kqn = BS['qn']
        kvl, kkvl = BS['kvl']
        kvn, kkvn = BS['kvn']
        t1, kt1 = BS['t1']
        t2, kt2 = BS['t2']
        krf, kkrf = BS['krf']
        Qsb, kQ = BS['Qsb']
        Ksb, kK = BS['Ksb']
        Vsb, kV = BS['Vsb']
        hT, k_hT = g["hT_l"][n % 2]
        SC = scr[n % 2]

        def BK(k):
            return (k + 4 * (n % 2)) % 8

        def proj(lhs_fn, nout, bank, M=128, keys=()):
            for oc in range(nout):
                for kc in range(NCH):
                    add("pe", lambda e, oc=oc, kc=kc: e.matmul(ps[0:M, bank, oc * T:(oc + 1) * T], lhs_fn(kc, oc), hT[:, kc, :],
                                                               start=(kc == 0), stop=(kc == NCH - 1), skip_group_check=True),
                        r=[k_hT] + list(keys), w=[PK(bank)])


        stl, kst = stile[n % 2]
        cs_t, kcs = cosS[n % 2]
        sn_t, ksn = sinS[n % 2]
        r = NB if t == NTL - 1 else b
        tok = t * T
        yield from norm_mod_g(stl, kst, 1, r, hT, k_hT, gain=lambda c: Gv[:, 1, c, r:r + 1],
                               bias=lambda c: modT[:, 3 * NCH + c, r:r + 1], bank=BK(7), scratch=SC)
        proj(lambda kc, oc: wmx[:, kc, oc * 128:(oc + 1) * 128], 2, BK(0), keys=[kwmx])
        add("act", lambda e: e.copy(out=zT, in_=ps[:, BK(0), :].rearrange("p (a b) -> p a b", a=2)), r=[PK(BK(0))], w=[kzT])
        yield
        for sub in range(2):
            for kc in range(2):
                add("pe", lambda e, sub=sub, kc=kc: e.matmul(ps[:, BK(1 + sub), :], zT[:, kc, sub * 128:(sub + 1) * 128],
                                                             csm[:, kc, :], start=(kc == 0), stop=(kc == 1)),
                    r=[kzT, kcsm], w=[PK(BK(1 + sub))])
            if sub == 0:
                add("act", lambda e: e.copy(out=Usb[:, 0, :], in_=ps[:, BK(1), :]), r=[PK(BK(1))], w=[kUsb + "0"])
            else:
                add("dve", lambda e: e.tensor_copy(out=Usb[:, 1, :], in_=ps[:, BK(2), :]), r=[PK(BK(2))], w=[kUsb + "1"])
            add("sp", lambda e, sub=sub, b=b, tok=tok: e.dma_start(
                out=g["Ud"][b, :, tok + sub * 128:tok + (sub + 1) * 128, :].rearrange("r t c -> t r c"),
                in_=Usb[:, sub, :].rearrange("p (r c) -> p r c", r=2)),
                r=[kUsb + str(sub)], w=[("Ud", b)], dma="st_u%d" % sub)
        yield
        proj(lambda kc, oc: wmx[:, kc, OFF_CB + oc * 128:OFF_CB + (oc + 1) * 128], 2, BK(3), keys=[kwmx])
        add("act", lambda e: e.copy(out=cb_sb, in_=ps[:, BK(3), :].rearrange("p (a b) -> p a b", a=2)), r=[PK(BK(3))], w=[kcb])
        add("sp", lambda e, b=b, tok=tok: e.dma_start(
            out=g["CVd"][b, 0].rearrange("(c p) t -> p c t", p=128)[:, :, tok:tok + T], in_=cb_sb),
            r=[kcb], w=[("CV", b)], dma="st_cb")
        proj(lambda kc, oc: wmx[:, kc, OFF_CC + oc * 128:OFF_CC + (oc + 1) * 128], 2, BK(4), keys=[kwmx])
        add("act", lambda e: e.copy(out=cc_sb, in_=ps[:, BK(4), :].rearrange("p (a b) -> p a b", a=2)), r=[PK(BK(4))], w=[kcc])
        yield
        proj(lambda kc, oc: wmx[:, kc, OFF_CX + oc * 128:OFF_CX + (oc + 1) * 128], 2, BK(5), keys=[kwmx])
        add("dve", lambda e: e.tensor_tensor(out=u_sb, in0=cc_sb, in1=ps[:, BK(5), :].rearrange("p (a b) -> p a b", a=2),
                                             op=ALU.mult), r=[PK(BK(5)), kcc], w=[ku])
        add("sp", lambda e, b=b, tok=tok: e.dma_start(
            out=g["CVd"][b, 1].rearrange("(c p) t -> p c t", p=128)[:, :, tok:tok + T], in_=u_sb),
            r=[ku], w=[("CV", b)], dma="st_u")
        yield
        proj(lambda kc, oc: wmx[:, kc, OFF_Q + oc * 128:OFF_Q + (oc + 1) * 128], 2, BK(6), keys=[kwmx])
        add("act", lambda e: e.copy(out=ql[:, 0:2, :], in_=ps[:, BK(6), :].rearrange("p (a b) -> p a b", a=2)),
            r=[PK(BK(6))], w=[kql])
        proj(lambda kc, oc: wmx[:, kc, OFF_Q + 256:OFF_Q + 384], 1, BK(7), keys=[kwmx])
        add("act", lambda e: e.copy(out=ql[:, 2, :], in_=ps[:, BK(7), 0:T]), r=[PK(BK(7))], w=[kql])
        yield
        proj(lambda kc, oc: wmx[:, kc, OFF_KV + oc * 128:OFF_KV + (oc + 1) * 128], 2, BK(0), keys=[kwmx])
        add("act", lambda e: e.copy(out=kvl, in_=ps[:, BK(0), :].rearrange("p (a b) -> p a b", a=2)), r=[PK(BK(0))], w=[kkvl])
        proj(lambda kc, oc: wkr[:, kc, oc, :], 2, BK(1), M=96, keys=[kwkr])
        add("dve", lambda e, cs_t=cs_t: e.tensor_tensor(out=t1[:, 0, :], in0=ps[0:96, BK(1), 0:T], in1=cs_t, op=ALU.mult),
            r=[PK(BK(1)), kcs], w=[kt1])
        add("dve", lambda e, sn_t=sn_t: e.tensor_tensor(out=t2[:, 0, :], in0=ps[0:96, BK(1), T:2 * T], in1=sn_t, op=ALU.mult),
            r=[PK(BK(1)), ksn], w=[kt2])
        add("pool", lambda e: e.tensor_tensor(out=krf, in0=t1[:, 0, :], in1=t2[:, 0, :], op=ALU.add),
            r=[kt1, kt2], w=[kkrf])
        yield
        yield from norm_mod_g(kvl, kkvl, 1, r, kvn, kkvn, nchunk=2, ones_i=2, gain=lambda c: kvnv[:, l, c:c + 1], bank=BK(2), scratch=SC)
        yield from norm_mod_g(ql, kql, 1, r, qn, kqn, nchunk=3, ones_i=1, gain=lambda c: qnv[:, l, c:c + 1], bank=BK(2), scratch=SC)
        for hp in range(4):
            bank = BK(3 + (hp % 2) * 2)
            for hh in range(2):
                h = 2 * hp + hh
                for kc in range(2):
                    add("pe", lambda e, h=h, hh=hh, kc=kc, bank=bank: e.matmul(
                        ps[0:64, bank, hh * T:(hh + 1) * T], wukv[:, kc, h * 128:h * 128 + 64], kvn[:, kc, :],
                        start=(kc == 0), stop=(kc == 1), skip_group_check=True), r=[kwukv, kkvn], w=[PK(bank)])
            add("act", lambda e, hp=hp, bank=bank: e.copy(
                out=Ksb[0:64, 2 * hp:2 * hp + 2, :], in_=ps[0:64, bank, :].rearrange("p (a b) -> p a b", a=2)),
                r=[PK(bank)], w=[kK + "n"])
            yield
        add("act", lambda e: e.copy(out=Ksb[64:96, :, :], in_=krf[64:96, :].unsqueeze(1).to_broadcast([32, H, T])),
            r=[kkrf], w=[kK + "r"])
        add("sp", lambda e, b=b, tok=tok: e.dma_start(
            out=g["KTd"][b, :, :, tok:tok + T].rearrange("h d t -> d h t"), in_=Ksb),
            r=[kK + "n", kK + "r"], w=[("KT", b)], dma="st_k")
        for sub in range(2):
            bank = BK(6 + sub)
            for kc in range(2):
                add("pe", lambda e, sub=sub, kc=kc, bank=bank: e.matmul(
                    ps[:, bank, :], kvn[:, kc, sub * 128:(sub + 1) * 128],
                    wukv[:, kc, :].rearrange("p (h d) -> p h d", h=H)[:, :, 64:128],
                    start=(kc == 0), stop=(kc == 1)), r=[kwukv, kkvn], w=[PK(bank)])
            add("act", lambda e, sub=sub, bank=bank: e.copy(
                out=Vsb[:, sub, :, 0:64], in_=ps[:, bank, :].rearrange("p (h d) -> p h d", h=H)),
                r=[PK(bank)], w=[kV + str(sub)])
            add("sp", lambda e, sub=sub, b=b, tok=tok: e.dma_start(
                out=g["Vd"][b, tok + sub * 128:tok + (sub + 1) * 128, :],
                in_=Vsb[:, sub, :, :].rearrange("p h d -> p (h d)")),
                r=[kV, kV + str(sub)], w=[("V", b)], dma="st_v%d" % sub)
            yield
        for hp in range(4):
            ba, bb = BK(3 + 2 * (hp % 2)), BK(4 + 2 * (hp % 2))
            for hh in range(2):
                h = 2 * hp + hh
                for kc in range(3):
                    add("pe", lambda e, h=h, hh=hh, kc=kc, ba=ba: e.matmul(
                        ps[0:96, ba, hh * T:(hh + 1) * T], wuq[:, kc, h * 96:(h + 1) * 96], qn[:, kc, :],
                        start=(kc == 0), stop=(kc == 2), skip_group_check=True), r=[kwuq, kqn], w=[PK(ba)])
                for kc in range(3):
                    add("pe", lambda e, h=h, hh=hh, kc=kc, bb=bb: e.matmul(
                        ps[0:96, bb, hh * T:(hh + 1) * T], wuqr[:, kc, h, :], qn[:, kc, :],
                        start=(kc == 0), stop=(kc == 2), skip_group_check=True), r=[kwuqr, kqn], w=[PK(bb)])
            add("dve", lambda e, ba=ba, cs_t=cs_t: e.tensor_tensor(
                out=t1, in0=ps[0:96, ba, :].rearrange("p (a b) -> p a b", a=2),
                in1=cs_t.unsqueeze(1).to_broadcast([96, 2, T]), op=ALU.mult), r=[PK(ba), kcs], w=[kt1])
            add("dve", lambda e, bb=bb, sn_t=sn_t: e.tensor_tensor(
                out=t2, in0=ps[0:96, bb, :].rearrange("p (a b) -> p a b", a=2),
                in1=sn_t.unsqueeze(1).to_broadcast([96, 2, T]), op=ALU.mult), r=[PK(bb), ksn], w=[kt2])
            add("pool", lambda e, hp=hp: e.tensor_tensor(out=Qsb[:, 2 * hp:2 * hp + 2, :], in0=t1, in1=t2, op=ALU.add),
                r=[kt1, kt2], w=[kQ])
            yield
        add("sp", lambda e, b=b, tok=tok: e.dma_start(
            out=g["QTd"][b, :, :, tok:tok + T].rearrange("h d t -> d h t"), in_=Qsb),
            r=[kQ], w=[("QT", b)], dma="st_q")


    def loads(n, b, t, what="sc"):
        stl, kst = stile[n % 2]
        cs_t, kcs = cosS[n % 2]
        sn_t, ksn = sinS[n % 2]
        tok = t * T
        if "s" in what:
            add("sp", lambda e: e.dma_start(out=stl, in_=sT_tile(b, t)),
                r=[("sT", b, t)], w=[kst], dma="ld_s%d" % (n % 2))
        if "c" in what:
            add("sp", lambda e: e.dma_start(out=cs_t, in_=g["cos_in"][:, tok:tok + T]),
                w=[kcs], dma="ld_c%d" % (n % 2))
            add("sp", lambda e: e.dma_start(out=sn_t, in_=g["sin_in"][:, tok:tok + T]),
                w=[ksn], dma="ld_n%d" % (n % 2))

    tl = [(b, t) for b in range(NB) for t in range(NTL)]
    loads(0, *tl[0])
    if len(tl) > 1:
        loads(1, *tl[1])
    for n0 in range(0, len(tl), 2):
        pair = list(range(n0, min(n0 + 2, len(tl))))
        gens = [do_tile(n, tl[n][0], tl[n][1], sets[n % 2]) for n in pair]
        rounds = 0
        while gens:
            for gi_ in list(gens):
                try:
                    next(gi_)
                except StopIteration:
                    gens.remove(gi_)
            rounds += 1
            if rounds == 10:
                for n in (n0 + 2, n0 + 3):
                    if n < len(tl):
                        loads(n, *tl[n], what="s")
        for n in (n0 + 2, n0 + 3):
            if n < len(tl):
                loads(n, *tl[n], what="c")


def p3(nc, S, A, base_mark, l, env):
    g = env
    add = S.add
    NB, NT, NTL, SEQ, CTX = g["NB"], g["NT"], g["NTL"], g["SEQ"], g["CTX"]
    ps, PK, onesf, k_onesf = g["ps"], g["PK"], g["onesf"], g["k_onesf"]
    A.off = base_mark
    NKC = NT // 128
    TQ = 512 if SEQ % 512 == 0 else 256
    Kall, kKall = A.alloc([96, H, NT], BF16, "Kall")
    Vall, kVall = A.alloc([128, NKC, H * 65], BF16, "Vall")
    Qt = [A.alloc([96, H, TQ], BF16, "Qt%d" % i) for i in range(2)]
    Pb = [A.alloc([128, 2, TQ], BF16, "Pb%d" % i) for i in range(2)]
    rd, krd = A.alloc([128, TQ], F32, "rd")
    bcs = [A.alloc([64, TQ], F32, "bcs%d" % i) for i in range(2)]
    Osb = [A.alloc([64, H, TQ], BF16, "Osb%d" % i) for i in range(2)]
    n = 0
    hc = 0
    gcount = 0
    for b in range(NB):
        for h in range(H):
            add("sp", lambda e, b=b, h=h: e.dma_start(out=Kall[:, h, :], in_=g["KTd"][b, h]),
                r=[("KT", b)], w=[kKall], dma="ld_k%d" % (h % 4))
        for c0 in range(0, NKC, 6):
            c1 = min(NKC, c0 + 6)
            add("sp", lambda e, b=b, c0=c0, c1=c1: e.dma_start(
                out=Vall[:, c0:c1, :], in_=g["Vd"][b].rearrange("(c p) n -> p c n", p=128)[:, c0:c1, :]),
                r=[("V", b)], w=[kVall], dma="ld_v")
        qtiles = [(tok, TQ, list(range(NKC))) for tok in range(0, SEQ, TQ)]
        if not (g["last"] and not DBG_CTX):
            qtiles.append((SEQ, CTX, list(range(SEQ // 128, NKC))))
        def loadq(nn, tok, W, b=b):
            qt, kqt = Qt[nn % 2]
            add("sp", lambda e: e.dma_start(
                out=qt[:, :, 0:W], in_=g["QTd"][b, :, :, tok:tok + W].rearrange("h d t -> d h t")),
                r=[("QT", b)], w=[kqt], dma="ld_q%d" % (nn % 2))

        loadq(n, qtiles[0][0], qtiles[0][1])
        for qi, (tok, W, kcs) in enumerate(qtiles):
            qt, kqt = Qt[n % 2]
            ob, kob = Osb[n % 2]
            if qi + 1 < len(qtiles):
                loadq(n + 1, qtiles[qi + 1][0], qtiles[qi + 1][1])
            groups = [kcs[i:i + 2] for i in range(0, len(kcs), 2)]
            work = [(h, gi) for h in range(H) for gi in range(len(groups))]
            slot_of = {}
            deferred = []

            def emitS(idx):
                nonlocal gcount
                h, gi = work[idx]
                slot = gcount % 2
                gcount += 1
                slot_of[idx] = slot
                for q, kc in enumerate(groups[gi]):
                    add("pe", lambda e, h=h, kc=kc, slot=slot, q=q, qt=qt, W=W: e.matmul(
                        ps[:, 2 * slot + q, 0:W], Kall[:, h, kc * 128:(kc + 1) * 128], qt[:, h, 0:W],
                        start=True, stop=True), r=[kKall, kqt], w=[PK(2 * slot + q)])

            def emitExpPV(idx):
                h, gi = work[idx]
                slot = slot_of[idx]
                ng = len(groups[gi])
                pb, kpb = Pb[slot]
                obank = 4 + (hc + h) % 2
                add("act", lambda e, slot=slot, pb=pb, ng=ng, W=W: e.activation(
                    out=pb[:, 0:ng, 0:W], in_=ps[:, 2 * slot:2 * slot + ng, 0:W], func=AF.Exp, scale=ATTN_SCALE),
                    r=[PK(2 * slot + q) for q in range(ng)], w=[kpb])
                for q, kc in enumerate(groups[gi]):
                    first = gi == 0 and q == 0
                    lastmm = gi == len(groups) - 1 and q == ng - 1
                    add("pe", lambda e, h=h, kc=kc, pb=pb, q=q, obank=obank, first=first, lastmm=lastmm, W=W: e.matmul(
                        ps[0:65, obank, 0:W], Vall[:, kc, h * 65:(h + 1) * 65], pb[:, q, 0:W], start=first, stop=lastmm),
                        r=[kVall, kpb], w=[PK(obank)])

            def epilogue_a(h):
                obank = 4 + (hc + h) % 2
                add("dve", lambda e, obank=obank, W=W: e.reciprocal(out=rd[64:65, 0:W], in_=ps[64:65, obank, 0:W]),
                    r=[PK(obank)], w=[krd])

            def epilogue_b(h):
                obank = 4 + (hc + h) % 2
                bb = 6 + (hc + h) % 2
                bc, kbc = bcs[(hc + h) % 2]
                add("pe", lambda e, bb=bb, W=W: e.matmul(ps[0:64, bb, 0:W], onesf[64:65, 0:64], rd[64:65, 0:W],
                                                         start=True, stop=True), r=[krd, k_onesf], w=[PK(bb)])
                add("dve", lambda e, bb=bb, bc=bc, W=W: e.tensor_copy(out=bc[:, 0:W], in_=ps[0:64, bb, 0:W]),
                    r=[PK(bb)], w=[kbc])
                add("dve", lambda e, h=h, ob=ob, obank=obank, bc=bc, W=W: e.tensor_tensor(
                    out=ob[:, h, 0:W], in0=ps[0:64, obank, 0:W], in1=bc[:, 0:W], op=ALU.mult),
                    r=[PK(obank), kbc], w=[kob])

            emitS(0)
            for idx in range(len(work)):
                if idx + 1 < len(work):
                    emitS(idx + 1)
                emitExpPV(idx)
                h, gi = work[idx]
                if deferred and gi == min(5, len(groups) - 1):
                    epilogue_b(deferred.pop(0))
                if gi == len(groups) - 1:
                    epilogue_a(h)
                    deferred.append(h)
            while deferred:
                epilogue_b(deferred.pop(0))
            hc += H
            add("sp", lambda e, ob=ob, b=b, tok=tok, W=W: e.dma_start(
                out=g["MIXd"][b, 512:1024, tok:tok + W].rearrange("(h d) t -> d h t", h=H), in_=ob[:, :, 0:W]),
                r=[kob], w=[("MIX", b)], dma="st_o%d" % (n % 2))
            n += 1


def p4(nc, S, A, base_mark, l, env):
    g = env
    add = S.add
    NB, NT, NTL = g["NB"], g["NT"], g["NTL"]
    ps, PK, cwv, k_cwv = g["ps"], g["PK"], g["cwv"], g["k_cwv"]
    it = 0
    for si, (t0, N, R) in enumerate(g["segs"]):
        if g["last"] and not DBG_CTX and si == 1:
            continue
        if si > 0:
            S.barrier()
        A.off = base_mark
        m1, km1 = A.alloc([2 * R, 2 * R], BF16, "m1")
        G2, kG2 = A.alloc([128, N], BF16, "G2")
        D2, kD2 = A.alloc([2 * R, 64 * 256], BF16, "D2")
        Y1, kY1 = A.alloc([2 * R, 64 * 256], BF16, "Y1")
        Y2, kY2 = A.alloc([128, R, 256], BF16, "Y2")
        Fo, kFo = A.alloc([128, 2, N], BF16, "Fo")
        add("pool", lambda e, m1=m1, si=si: e.dma_start(out=m1, in_=g["m1_in"][si]), w=[km1], dma="w0")
        add("pool", lambda e, G2=G2, si=si: e.dma_start(out=G2, in_=g["g2_in"][si]), w=[kG2], dma="w1")
        for b in range(NB):
            for ri in range(2):
                add("sp", lambda e, b=b, ri=ri, D2=D2, R=R, t0=t0, N=N: e.dma_start(
                    out=D2[ri * R:(ri + 1) * R, :],
                    in_=g["Ud"][b, ri, t0:t0 + N, :].rearrange("(a n) c -> a (n c)", a=R)),
                    r=[("Ud", b)], w=[kD2], dma="ld_d%d" % ri)
            for j in range(32):
                bank = j % 4
                add("pe", lambda e, j=j, bank=bank, m1=m1, D2=D2, R=R: e.matmul(
                    ps[0:2 * R, bank, :], m1, D2[:, j * 512:(j + 1) * 512], start=True, stop=True),
                    r=[km1, kD2], w=[PK(bank)])
                if j % 2 == 0:
                    add("act", lambda e, j=j, bank=bank, Y1=Y1, R=R: e.copy(out=Y1[:, j * 512:(j + 1) * 512],
                                                                            in_=ps[0:2 * R, bank, :]),
                        r=[PK(bank)], w=[kY1 + "a"])
                else:
                    add("dve", lambda e, j=j, bank=bank, Y1=Y1, R=R: e.tensor_copy(out=Y1[:, j * 512:(j + 1) * 512],
                                                                                   in_=ps[0:2 * R, bank, :]),
                        r=[PK(bank)], w=[kY1 + "d"])
            for ri in range(2):
                add("sp", lambda e, b=b, ri=ri, Y1=Y1, R=R, t0=t0, N=N: e.dma_start(
                    out=g["Yd"][b, ri, t0:t0 + N, :].rearrange("(a n) c -> a (n c)", a=R),
                    in_=Y1[ri * R:(ri + 1) * R, :]),
                    r=[kY1 + "a", kY1 + "d"], w=[("Yd", b, si)], dma="st_y%d" % ri)
            for ri in range(2):
                for a0 in range(0, R, 16):
                    a1 = min(R, a0 + 16)
                    add("sp", lambda e, b=b, ri=ri, Y2=Y2, R=R, t0=t0, N=N, a0=a0, a1=a1: e.dma_start(
                        out=Y2[ri * 64:(ri + 1) * 64, a0:a1, :],
                        in_=g["Yd"][b, ri, t0:t0 + N, :].rearrange("(a n) c -> n a c", a=R)[:, a0:a1, :]),
                        r=[("Yd", b, si)], w=[kY2], dma="ld_y%d" % ri)
            ng = (R + 7) // 8
            for cc in range(2):
                for gi in range(ng):
                    nk = min(8, R - gi * 8)
                    bank = 4 + (it % 4)
                    it += 1
                    for kl in range(nk):
                        k1 = gi * 8 + kl
                        add("pe", lambda e, k1=k1, kl=kl, cc=cc, bank=bank, Y2=Y2, G2=G2: e.matmul(
                            ps[:, bank, kl * 64:(kl + 1) * 64], Y2[:, k1, cc * 128:(cc + 1) * 128],
                            G2[:, k1 * 64:(k1 + 1) * 64], start=True, stop=True, skip_group_check=True),
                            r=[kY2, kG2], w=[PK(bank)])
                    add("dve", lambda e, cc=cc, gi=gi, nk=nk, bank=bank, Fo=Fo, R=R: e.tensor_copy(
                        out=Fo[:, cc, :].rearrange("p (k2 k1) -> p k1 k2", k1=R)[:, gi * 8:gi * 8 + nk, :],
                        in_=ps[:, bank, 0:nk * 64].rearrange("p (a b) -> p a b", a=nk)),
                        r=[PK(bank)], w=[kFo])
            add("sp", lambda e, b=b, Fo=Fo, t0=t0, N=N: e.dma_start(
                out=g["MIXd"][b, 0:256, t0:t0 + N].rearrange("(c p) t -> p c t", p=128), in_=Fo),
                r=[kFo], w=[("MIX", b)], dma="st_f")
    S.barrier()
    A.off = base_mark
    CW = 1024
    cu = [A.alloc([128, 2, CW + 2], F32, "cu%d" % i) for i in range(2)]
    cg = [A.alloc([128, 2, CW], F32, "cg%d" % i) for i in range(2)]
    cy = [A.alloc([128, 2, CW], F32, "cy%d" % i) for i in range(2)]
    co = [A.alloc([128, 2, CW], BF16, "co%d" % i) for i in range(2)]
    ctl = []
    for b in range(NB):
        for si, (t0, N, R) in enumerate(g["segs"]):
            if g["last"] and not DBG_CTX and si == 1:
                continue
            w = min(CW, N)
            for a in range(t0, t0 + N, w):
                ctl.append((b, t0, N, a, w))

    def cloads(n, b, t0, N, a, w):
        u, kcu = cu[n % 2]
        gb, kcg = cg[n % 2]
        cvu = g["CVd"][b, 1].rearrange("(c p) t -> p c t", p=128)
        cvg = g["CVd"][b, 0].rearrange("(c p) t -> p c t", p=128)
        lo = a - 1 if a > t0 else a
        hi = a + w + 1 if a + w < t0 + N else a + w
        if a == t0:
            add("pool", lambda e: e.memset(u[:, :, 0:1], 0.0), w=[kcu])
        if a + w == t0 + N:
            add("pool", lambda e: e.memset(u[:, :, w + 1:w + 2], 0.0), w=[kcu])
        add("sp", lambda e: e.dma_start(out=u[:, :, 1 - (a - lo):1 + (hi - a)], in_=cvu[:, :, lo:hi]),
            r=[("CV", b)], w=[kcu], dma="ld_cu%d" % (n % 2))
        add("sp", lambda e: e.dma_start(out=gb[:, :, 0:w], in_=cvg[:, :, a:a + w]),
            r=[("CV", b)], w=[kcg], dma="ld_cg%d" % (n % 2))

    def ccompute(n, b, t0, N, a, w):
        u, kcu = cu[n % 2]
        gb, kcg = cg[n % 2]
        yy, kcy = cy[n % 2]
        oo, kco = co[n % 2]
        for c in range(2):
            add("dve", lambda e, c=c: e.tensor_scalar_mul(out=yy[:, c, 0:w], in0=u[:, c, 1:w + 1],
                                                          scalar1=cwv[:, l, 1, c:c + 1]),
                r=[kcu, k_cwv], w=[kcy + str(c)])
            add("dve", lambda e, c=c: e.scalar_tensor_tensor(
                out=yy[:, c, 0:w], in0=u[:, c, 0:w], scalar=cwv[:, l, 0, c:c + 1], in1=yy[:, c, 0:w],
                op0=ALU.mult, op1=ALU.add), r=[kcu, k_cwv, kcy + str(c)], w=[kcy + str(c)])
            add("dve", lambda e, c=c: e.scalar_tensor_tensor(
                out=yy[:, c, 0:w], in0=u[:, c, 2:w + 2], scalar=cwv[:, l, 2, c:c + 1], in1=yy[:, c, 0:w],
                op0=ALU.mult, op1=ALU.add), r=[kcu, k_cwv, kcy + str(c)], w=[kcy + str(c)])
            add("dve", lambda e, c=c: e.tensor_tensor(out=oo[:, c, 0:w], in0=yy[:, c, 0:w], in1=gb[:, c, 0:w], op=ALU.mult),
                r=[kcg, kcy + str(c)], w=[kco + str(c)])
        add("sp", lambda e: e.dma_start(
            out=g["MIXd"][b, 256:512, a:a + w].rearrange("(c p) t -> p c t", p=128), in_=oo[:, :, 0:w]),
            r=[kco + "0", kco + "1"], w=[("MIX", b)], dma="st_cv%d" % (n % 2))

    if ctl:
        cloads(0, *ctl[0])
    for n, tl_ in enumerate(ctl):
        if n + 1 < len(ctl):
            cloads(n + 1, *ctl[n + 1])
        ccompute(n, *tl_)


def _consts(SEQ, CTX):
    NT = SEQ + CTX
    out = {"ident": np.eye(128, dtype=np.float32)}
    cs = np.zeros((256, 512), np.float64)
    jj, cc = np.meshgrid(np.arange(64), np.arange(64), indexing="ij")
    for g in range(4):
        ang = 2 * np.pi * (jj * cc % 64) / 64.0
        cs[g * 64:(g + 1) * 64, g * 64:(g + 1) * 64] = np.cos(ang).T
        cs[g * 64:(g + 1) * 64, 256 + g * 64:256 + (g + 1) * 64] = -np.sin(ang).T
    out["cs"] = cs.astype(np.float32)
    for name, N in (("x", SEQ), ("c", CTX)):
        R = N // 64
        n1, k1 = np.meshgrid(np.arange(R), np.arange(R), indexing="ij")
        ang = 2 * np.pi * (n1 * k1 % R) / R
        Fr, Fi = np.cos(ang), -np.sin(ang)
        m1 = np.zeros((2 * R, 2 * R))
        m1[:R, :R] = Fr
        m1[R:, :R] = -Fi
        m1[:R, R:] = Fi
        m1[R:, R:] = Fr
        out["m1" + name] = m1.astype(np.float32)
        n2 = np.arange(64)[:, None]
        k = np.arange(N)[None, :]
        kk = (np.arange(N) // 64) + R * (np.arange(N) % 64)
        ang = 2 * np.pi * ((n2 * kk[None, :]) % N) / N
        sc = 1.0 / np.sqrt(N * 64.0)
        g2 = np.concatenate([np.cos(ang), np.sin(ang)], axis=0) * sc
        out["g2" + name] = g2.astype(np.float32)
    rows = SEQ // 64
    row = np.repeat(np.arange(rows), 64).astype(np.float32)
    col = np.tile(np.arange(64), rows).astype(np.float32)
    inv = (1.0 / (np.float32(10000.0) ** (np.arange(0, 16, 2, dtype=np.float32) / np.float32(16)))).astype(np.float32)
    cosT = np.ones((96, NT), np.float32)
    sinT = np.zeros((96, NT), np.float32)
    for axis, pos in enumerate((row, col)):
        ang = pos[None, :] * inv[:, None]
        for half in range(2):
            r0 = 64 + axis * 16 + half * 8
            cosT[r0:r0 + 8, :SEQ] = np.cos(ang)
            sinT[r0:r0 + 8, :SEQ] = np.sin(ang) * (-1.0 if half == 0 else 1.0)
    out["cosT"], out["sinT"] = cosT, sinT
    return out


def make_in_maps(inp, SEQ, CTX, NB, L, ncores):
    consts = _consts(SEQ, CTX)
    shared = {k: np.ascontiguousarray(inp[k], dtype=np.float32) for k in (
        "w_mod", "b_mod", "ffn1_norm", "mix_norm", "ffn2_norm", "ffn1_w_in", "ffn2_w_in", "ffn1_w_out",
        "ffn2_w_out", "w_in", "conv_w", "q_norm", "w_uq", "kv_norm", "w_ukv", "w_out", "final_norm")}
    shared.update(consts)
    maps = []
    for i in range(ncores):
        m = dict(shared)
        m["x"] = np.ascontiguousarray(inp["x"][i * NB:(i + 1) * NB])
        m["ctx"] = np.ascontiguousarray(inp["ctx"][i * NB:(i + 1) * NB])
        m["c3"] = np.ascontiguousarray(np.concatenate([inp["c"][i * NB:(i + 1) * NB], inp["c_ctx"][None, :]], axis=0))
        maps.append(m)
    return maps


def kernel(**inputs):
    SEQ, CTX, NB, L, ncores = 4096, 256, 2, 4, 8
    inp = {k: np.asarray(v) for k, v in inputs.items()}
    nc = build(SEQ, CTX, NB, L)
    maps = make_in_maps(inp, SEQ, CTX, NB, L, ncores)
    res = run_bass_kernel_spmd(nc, maps, core_ids=list(range(ncores)))
    return np.concatenate([np.asarray(r["y"]) for r in res.results], axis=0).astype(np.float32)
```

```python
import numpy as np
from contextlib import ExitStack
import concourse.bass as bass
import concourse.mybir as mybir
from concourse.bass_utils import run_bass_kernel_spmd

F32 = mybir.dt.float32
BF16 = mybir.dt.bfloat16
AF = mybir.ActivationFunctionType
ALU = mybir.AluOpType

D = 1024
DFF = 2816
NCH = 8
FCH = 22
T = 256
H = 8
OFF_CB, OFF_CC, OFF_CX, OFF_Q, OFF_KV, OFF_KR, IN_COLS = 256, 512, 768, 1024, 1408, 1664, 1696
EPS = 1e-6
ATTN_SCALE = 96.0 ** -0.5
DBG_CTX = False


class Op:
    __slots__ = ("eng", "fn", "deps", "dma", "val", "sem", "needs_inc", "idx")

    def __init__(self, eng, fn, dma):
        self.eng = eng
        self.fn = fn
        self.dma = dma
        self.deps = []
        self.val = 0
        self.sem = None
        self.needs_inc = False


class Sched:
    ENGS = ("pe", "act", "dve", "pool", "sp")

    def __init__(self):
        self.ops = {e: [] for e in self.ENGS}
        self.last_w = {}
        self.readers = {}
        self.last_dma = {}
        self.all = []
        self.n = 0

    def add(self, eng, fn, r=(), w=(), dma=None):
        op = Op(eng, fn, dma)
        op.idx = self.n
        self.n += 1
        deps = {}
        for k in r:
            p = self.last_w.get(k)
            if p is not None:
                deps[p.idx] = (p, "raw")
        for k in w:
            p = self.last_w.get(k)
            if p is not None and p.idx not in deps:
                deps[p.idx] = (p, "waw")
            for q in self.readers.get(k, ()):
                if q.idx not in deps:
                    deps[q.idx] = (q, "war")
        if dma is not None:
            p = self.last_dma.get(dma)
            if p is not None:
                deps[p.idx] = (p, "raw")
            self.last_dma[dma] = op
        for p, kind in deps.values():
            if p.dma is None and dma is None and p.eng == eng:
                if eng == "pe" or kind != "raw":
                    continue
            op.deps.append(p)
            if p.dma is None:
                p.needs_inc = True
        for k in r:
            self.readers.setdefault(k, []).append(op)
        for k in w:
            self.last_w[k] = op
            self.readers[k] = []
        self.ops[eng].append(op)
        self.all.append(op)
        return op

    def fence(self, eng, keys):
        return self.add(eng, None, r=keys)

    def barrier(self):
        lasts = []
        for e in self.ENGS:
            for op in reversed(self.ops[e]):
                if op.fn is not None and op.dma is None:
                    lasts.append(op)
                    break
        dmas = list(self.last_dma.values())
        for e in self.ENGS:
            op = Op(e, None, None)
            op.idx = self.n
            self.n += 1
            for p in lasts:
                if p.eng != e:
                    op.deps.append(p)
                    p.needs_inc = True
            op.deps.extend(dmas)
            self.ops[e].append(op)
        self.last_w = {}
        self.readers = {}

    def emit(self, nc, stack):
        esem = {e: stack.enter_context(nc.semaphore("s_" + e)) for e in self.ENGS}
        dsem = {}
        cnt = {e: 0 for e in self.ENGS}
        for op in self.all:
            if op.dma is not None:
                if op.dma not in dsem:
                    dsem[op.dma] = [stack.enter_context(nc.semaphore("d_%d" % len(dsem))), 0]
                ent = dsem[op.dma]
                ent[1] += 16
                op.sem, op.val = ent[0], ent[1]
            elif op.needs_inc:
                cnt[op.eng] += 1
                op.sem, op.val = esem[op.eng], cnt[op.eng]
        self.n_sems = len(dsem) + 5
        block = stack.enter_context(nc.Block())

        def run(e, eng):
            waited = {}
            for op in self.ops[e]:
                need = {}
                for p in op.deps:
                    key = id(p.sem)
                    if waited.get(key, 0) >= p.val:
                        continue
                    if key not in need or need[key][1] < p.val:
                        need[key] = (p.sem, p.val)
                for key, (s, v) in need.items():
                    eng.wait_ge(s, v)
                    waited[key] = v
                if op.fn is None:
                    continue
                ins = op.fn(eng)
                if op.dma is not None:
                    ins.then_inc(op.sem, 16)
                elif op.needs_inc:
                    ins.then_inc(op.sem, 1)

        block.tensor(lambda eng: run("pe", eng))
        block.scalar(lambda eng: run("act", eng))
        block.vector(lambda eng: run("dve", eng))
        block.gpsimd(lambda eng: run("pool", eng))
        block.sync(lambda eng: run("sp", eng))


class Arena:
    def __init__(self, nc, nbytes):
        self.t = nc.alloc_sbuf_tensor("arena", [128, nbytes // 4], F32)
        self.off = 0
        self.cap = nbytes
        self.n = 0

    def alloc(self, shape, dtype, key=None):
        esz = 2 if dtype == BF16 else 4
        n = 1
        for s in shape[1:]:
            n *= s
        nb = (n * esz + 63) // 64 * 64
        assert self.off + nb <= self.cap, ("SBUF arena overflow", self.off, nb, self.cap)
        ap = self.t[:, self.off // 4:(self.off + nb) // 4]
        if dtype == BF16:
            ap = ap.bitcast(BF16)
        ap = ap[:, 0:n]
        if len(shape) == 3:
            ap = ap.rearrange("p (a b) -> p a b", a=shape[1])
        elif len(shape) == 4:
            ap = ap.rearrange("p (a b c) -> p a b c", a=shape[1], b=shape[2])
        if shape[0] < 128:
            ap = ap[0:shape[0]]
        self.off += nb
        self.n += 1
        return ap, (key or "t%d" % self.n)


def build(SEQ, CTX, NB, L, debug=False):
    NT = SEQ + CTX
    NTL = NT // T
    NR = NB + 1
    segs = [(0, SEQ, SEQ // 64), (SEQ, CTX, CTX // 64)]
    nc = bass.Bass("TRN2", target_bir_lowering=False)
    S = Sched()
    add = S.add

    def din(name, shape):
        return nc.dram_tensor(name, list(shape), F32, kind="ExternalInput").ap()

    x_in = din("x", [NB, SEQ, D])
    ctx_in = din("ctx", [NB, CTX, D])
    c3 = din("c3", [NR, D])
    w_mod = din("w_mod", [L, D, 9 * D])
    b_mod = din("b_mod", [L, 9 * D])
    ffn_norm = [din("ffn1_norm", [L, D]), din("mix_norm", [L, D]), din("ffn2_norm", [L, D])]
    ffn_win = [din("ffn1_w_in", [L, D, 2 * DFF]), din("ffn2_w_in", [L, D, 2 * DFF])]
    ffn_wout = [din("ffn1_w_out", [L, DFF, D]), din("ffn2_w_out", [L, DFF, D])]
    w_in = din("w_in", [L, D, IN_COLS])
    conv_w = din("conv_w", [L, 3, 256])
    q_norm = din("q_norm", [L, 384])
    w_uq = din("w_uq", [L, 384, 768])
    kv_norm = din("kv_norm", [L, 256])
    w_ukv = din("w_ukv", [L, 256, 1024])
    w_out = din("w_out", [L, D, D])
    final_norm = din("final_norm", [D])
    ident_in = din("ident", [128, 128])
    cs_in = din("cs", [256, 512])
    m1_in = [din("m1x", [2 * segs[0][2]] * 2), din("m1c", [2 * segs[1][2]] * 2)]
    g2_in = [din("g2x", [128, SEQ]), din("g2c", [128, CTX])]
    cos_in = din("cosT", [96, NT])
    sin_in = din("sinT", [96, NT])
    y_out = nc.dram_tensor("y", [NB, SEQ, D], F32, kind="ExternalOutput").ap()

    dk = "ExternalOutput" if debug else "Internal"
    sT = nc.dram_tensor("sT", [NB, D, NT], F32, kind=dk).ap()
    Ud = nc.dram_tensor("Ud", [NB, 2, NT, 256], BF16, kind=dk).ap()
    Yd = nc.dram_tensor("Yd", [NB, 2, NT, 256], BF16, kind=dk).ap()
    CVd = nc.dram_tensor("CVd", [NB, 2, 256, NT], F32, kind=dk).ap()
    QTd = nc.dram_tensor("QTd", [NB, H, 96, NT], BF16, kind=dk).ap()
    KTd = nc.dram_tensor("KTd", [NB, H, 96, NT], BF16, kind=dk).ap()
    Vd = nc.dram_tensor("Vd", [NB, NT, H * 65], BF16, kind=dk).ap()
    MIXd = nc.dram_tensor("MIXd", [NB, D, NT], BF16, kind=dk).ap()

    with ExitStack() as st:
        A = Arena(nc, 207 * 1024)
        ps = nc.alloc_psum_tensor("ps", [128, 8, 512], F32)
        nc_allow = st.enter_context(nc.allow_non_contiguous_dma("small per-partition vectors"))

        def PK(b):
            return ("ps", b)

        ident, k_ident = A.alloc([128, 128], F32, "ident")
        ones_d, k_ones = A.alloc([128, 3, 128], BF16, "ones")
        onesf, k_onesf = A.alloc([128, 64], F32, "onesf")
        epst, k_eps = A.alloc([128, 1], F32, "eps")
        scT, k_scT = A.alloc([128, NCH, NR], F32, "scT")
        normv, k_normv = A.alloc([128, 3, L, NCH], F32, "normv")
        fnormv, k_fnormv = A.alloc([128, NCH], F32, "fnormv")
        qnv, k_qnv = A.alloc([128, L, 3], F32, "qnv")
        kvnv, k_kvnv = A.alloc([128, L, 2], F32, "kvnv")
        cwv, k_cwv = A.alloc([128, L, 3, 2], F32, "cwv")
        modT, k_modT = A.alloc([128, 72, NR], F32, "modT")
        Gv, k_Gv = A.alloc([128, 3, NCH, NR], F32, "Gv")
        gatev, k_gatev = A.alloc([128, 3, NCH, NR], F32, "gatev")
        rstd_l = [A.alloc([128, T], F32, "rstd0")] * 2
        sqb_l = [A.alloc([128, NCH, T], BF16, "sqb0")] * 2
        tn_l = [A.alloc([128, NCH, T], F32, "tn0")] * 2
        hT_l = [A.alloc([128, NCH, T], BF16, "hT%d" % i) for i in range(2)]
        hT, k_hT = hT_l[0]
        stile = [A.alloc([128, NCH, T], F32, "stile%d" % i) for i in range(2)]
        base_mark = A.off

        add("sp", lambda e: e.dma_start(out=ident, in_=ident_in), w=[k_ident], dma="c0")
        for i, v in enumerate((1.0 / 1024, 1.0 / 384, 1.0 / 256)):
            add("pool", lambda e, i=i, v=v: e.memset(ones_d[:, i, :], v), w=[k_ones])
        add("pool", lambda e: e.memset(onesf, 1.0), w=[k_onesf])
        add("pool", lambda e: e.memset(epst, EPS), w=[k_eps])
        for i in range(3):
            for l in range(L):
                add("sp", lambda e, i=i, l=l: e.dma_start(
                    out=normv[:, i, l, :], in_=ffn_norm[i][l].rearrange("(c p) -> p c", p=128)),
                    w=[k_normv], dma="c1")
        add("sp", lambda e: e.dma_start(out=fnormv, in_=final_norm.rearrange("(c p) -> p c", p=128)),
            w=[k_fnormv], dma="c2")
        for l in range(L):
            add("sp", lambda e, l=l: e.dma_start(out=qnv[:, l, :], in_=q_norm[l].rearrange("(c p) -> p c", p=128)),
                w=[k_qnv], dma="c3")
            add("sp", lambda e, l=l: e.dma_start(out=kvnv[:, l, :], in_=kv_norm[l].rearrange("(c p) -> p c", p=128)),
                w=[k_kvnv], dma="c4")
            for k in range(3):
                add("sp", lambda e, l=l, k=k: e.dma_start(
                    out=cwv[:, l, k, :], in_=conv_w[l, k].rearrange("(c p) -> p c", p=128)),
                    w=[k_cwv], dma="c5")

        xin = [A.alloc([128, D], F32, "xin%d" % i) for i in range(2)]
        xo = [A.alloc([128, NCH, 128], F32, "xo%d" % i) for i in range(2)]
        it = 0
        for b in range(NB):
            for (t0, n, _), src in zip(segs, (x_in, ctx_in)):
                for blk in range(n // 128):
                    xi, kxi = xin[it % 2]
                    xoo, kxo = xo[it % 2]
                    add("sp", lambda e, xi=xi, src=src, b=b, blk=blk: e.dma_start(
                        out=xi, in_=src[b, blk * 128:(blk + 1) * 128, :]), w=[kxi], dma="xin%d" % (it % 2))
                    for half in range(2):
                        bank = 2 * (it % 2) + half
                        for q in range(4):
                            c = half * 4 + q
                            add("pe", lambda e, xi=xi, bank=bank, q=q, c=c: e.transpose(
                                ps[:, bank, q * 128:(q + 1) * 128], xi[:, c * 128:(c + 1) * 128], ident),
                                r=[kxi, k_ident], w=[PK(bank)])
                        eng = "act" if half == 0 else "dve"
                        if eng == "act":
                            add("act", lambda e, xoo=xoo, bank=bank, half=half: e.copy(
                                out=xoo[:, half * 4:(half + 1) * 4, :],
                                in_=ps[:, bank, :].rearrange("p (a b) -> p a b", a=4)),
                                r=[PK(bank)], w=[kxo + "h%d" % half])
                        else:
                            add("dve", lambda e, xoo=xoo, bank=bank, half=half: e.tensor_copy(
                                out=xoo[:, half * 4:(half + 1) * 4, :],
                                in_=ps[:, bank, :].rearrange("p (a b) -> p a b", a=4)),
                                r=[PK(bank)], w=[kxo + "h%d" % half])
                    tok = t0 + blk * 128
                    add("sp", lambda e, xoo=xoo, b=b, tok=tok: e.dma_start(
                        out=sT[b].rearrange("(c p) t -> p c t", p=128)[:, :, tok:tok + 128], in_=xoo),
                        r=[kxo + "h0", kxo + "h1"], w=[("sT", b, tok // T)], dma="xo%d" % (it % 2))
                    it += 1
        crow, k_crow = A.alloc([NR, D], F32, "crow")
        add("sp", lambda e: e.dma_start(out=crow, in_=c3), w=[k_crow], dma="c6")
        add("act", lambda e: e.activation(out=crow, in_=crow, func=AF.Silu), r=[k_crow], w=[k_crow])
        for c in range(NCH):
            add("pe", lambda e, c=c: e.transpose(ps[:, 7, c * NR:(c + 1) * NR], crow[:, c * 128:(c + 1) * 128],
                                                  ident[0:NR, 0:NR]), r=[k_crow, k_ident], w=[PK(7)])
        add("dve", lambda e: e.tensor_copy(out=scT, in_=ps[:, 7, 0:NCH * NR].rearrange("p (a b) -> p a b", a=NCH)),
            r=[PK(7)], w=[k_scT])
        S.barrier()

        def norm_mod_g(stl, kst, i, r, dst, kdst, nchunk=NCH, ones_i=0, gain=None, bias=None, bank=7, bs=0, scratch=None):
            (rstd, k_rstd), (sqb, k_sqb), (tn, k_tn) = scratch if scratch is not None else (rstd_l[bs], sqb_l[bs], tn_l[bs])
            add("act", lambda e: e.activation(out=sqb[:, 0:nchunk, :], in_=stl[:, 0:nchunk, :], func=AF.Square),
                r=[kst], w=[k_sqb])
            yield
            for c in range(nchunk):
                add("pe", lambda e, c=c: e.matmul(ps[:, bank, 0:T], ones_d[:, ones_i, :], sqb[:, c, :],
                                                  start=(c == 0), stop=(c == nchunk - 1)),
                    r=[k_sqb, k_ones], w=[PK(bank)])
            yield
            add("act", lambda e: e.activation(out=rstd, in_=ps[:, bank, 0:T], func=AF.Sqrt, bias=epst[:, 0:1], scale=1.0),
                r=[PK(bank), k_eps], w=[k_rstd])
            yield
            add("dve", lambda e: e.reciprocal(out=rstd, in_=rstd), r=[k_rstd], w=[k_rstd])
            yield
            for c in range(nchunk):
                add("dve", lambda e, c=c: e.tensor_tensor(out=tn[:, c, :], in0=stl[:, c, :], in1=rstd, op=ALU.mult),
                    r=[kst, k_rstd], w=[k_tn + str(c)])
                g = gain(c)
                bb = bias(c) if bias is not None else 0.0
                add("act", lambda e, c=c, g=g, bb=bb: e.activation(out=dst[:, c, :], in_=tn[:, c, :], func=AF.Identity,
                                                                     scale=g, bias=bb),
                    r=[k_tn + str(c), k_Gv, k_modT, k_qnv, k_kvnv], w=[kdst])
                if c % 2 == 1:
                    yield

        def norm_mod(*a_, **k_):
            for _ in norm_mod_g(*a_, **k_):
                pass

        def load_ffn_weights(l, f, wi, kwi, wo, kwo):
            n = 0
            for q in (0, 2, 1, 3):
                for c in range(NCH):
                    add("pool", lambda e, q=q, c=c: e.dma_start(
                        out=wi[:, c, q * 1408:(q + 1) * 1408],
                        in_=ffn_win[f][l, c * 128:(c + 1) * 128, q * 1408:(q + 1) * 1408]),
                        w=[(kwi, c, q)], dma="w%d" % (n % 6))
                    n += 1
            for j in range(FCH):
                add("pool", lambda e, j=j: e.dma_start(out=wo[:, j, :], in_=ffn_wout[f][l, j * 128:(j + 1) * 128, :]),
                    w=[(kwo, j)], dma="w%d" % (n % 6))
                n += 1

        def ffn_pro(stl, kst, i, r, bs):
            hb, khb = hT_l[bs]
            norm_mod(stl, kst, i, r, hb, khb, bs=bs, bank=7,
                     gain=lambda c: Gv[:, i, c, r:r + 1], bias=lambda c: modT[:, (3 * i) * NCH + c, r:r + 1])

        def ffn_main(stl, kst, i, r, bs, wi, kwi, wo, kwo, actb, hook=None):
            hb, khb = hT_l[bs]

            def gu(j):
                bank = 4 + j % 3
                for half, col0 in ((0, j * 128), (1, DFF + j * 128)):
                    q = col0 // 1408
                    q2 = (col0 + 127) // 1408
                    for c in range(NCH):
                        add("pe", lambda e, c=c, half=half, col0=col0, bank=bank: e.matmul(
                            ps[:, bank, half * T:(half + 1) * T], wi[:, c, col0:col0 + 128], hb[:, c, :],
                            start=(c == 0), stop=(c == NCH - 1)),
                            r=[(kwi, c, q), (kwi, c, q2), khb], w=[PK(bank)])
                ab, kab = actb[j % 4]
                sg, ksg = actb[4 + j % 4]
                add("act", lambda e, bank=bank, sg=sg: e.activation(out=sg, in_=ps[:, bank, 0:T], func=AF.Silu),
                    r=[PK(bank)], w=[ksg])
                add("dve", lambda e, bank=bank, sg=sg, ab=ab: e.tensor_tensor(out=ab, in0=sg, in1=ps[:, bank, T:2 * T],
                                                                               op=ALU.mult),
                    r=[PK(bank), ksg], w=[kab])

            def yacc(j):
                ab, kab = actb[j % 4]
                for c in range(NCH):
                    add("pe", lambda e, c=c, j=j, ab=ab: e.matmul(
                        ps[:, c // 2, (c % 2) * T:(c % 2 + 1) * T], wo[:, j, c * 128:(c + 1) * 128], ab,
                        start=(j == 0 and c % 2 == 0), stop=(j == FCH - 1), skip_group_check=True),
                        r=[(kwo, j), kab], w=[PK(c // 2)])

            gu(0)
            gu(1)
            for j in range(FCH):
                yacc(j)
                if j + 2 < FCH:
                    gu(j + 2)
                if hook is not None and j >= 2:
                    next(hook, None)
            if hook is not None:
                for _ in hook:
                    pass
            for c in range(NCH):
                add("dve", lambda e, c=c: e.scalar_tensor_tensor(
                    out=stl[:, c, :], in0=ps[:, c // 2, (c % 2) * T:(c % 2 + 1) * T], scalar=gatev[:, i, c, r:r + 1],
                    in1=stl[:, c, :], op0=ALU.mult, op1=ALU.add),
                    r=[PK(c // 2), k_gatev, kst], w=[kst])

        def sT_tile(b, t):
            return sT[b].rearrange("(c p) t -> p c t", p=128)[:, :, t * T:(t + 1) * T]

        for l in range(L):
            last = l == L - 1
            A.off = base_mark
            wm = [A.alloc([128, NCH, 512], F32, "wm%d" % i) for i in range(2)]
            bm = [A.alloc([1, 512], F32, "bm%d" % i) for i in range(2)]
            mrow, k_mrow = A.alloc([NR, 9 * D], F32, "mrow")
            for ct in range(18):
                wmt, kwm = wm[ct % 2]
                bmt, kbm = bm[ct % 2]
                add("sp", lambda e, wmt=wmt, ct=ct, l=l: e.dma_start(
                    out=wmt, in_=w_mod[l].rearrange("(c p) n -> p c n", p=128)[:, :, ct * 512:(ct + 1) * 512]),
                    w=[kwm], dma="wm%d" % (ct % 2))
                add("sp", lambda e, bmt=bmt, ct=ct, l=l: e.dma_start(out=bmt, in_=b_mod[l:l + 1, ct * 512:(ct + 1) * 512]),
                    w=[kbm], dma="bm%d" % (ct % 2))
                bank = ct % 2
                for c in range(NCH):
                    add("pe", lambda e, c=c, wmt=wmt, bank=bank: e.matmul(
                        ps[0:NR, bank, :], scT[:, c, :], wmt[:, c, :], start=(c == 0), stop=False),
                        r=[kwm, k_scT], w=[PK(bank)])
                add("pe", lambda e, bmt=bmt, bank=bank: e.matmul(ps[0:NR, bank, :], onesf[0:1, 0:NR], bmt,
                                                                    start=False, stop=True),
                    r=[kbm, k_onesf], w=[PK(bank)])
                add("act", lambda e, ct=ct, bank=bank: e.copy(out=mrow[:, ct * 512:(ct + 1) * 512], in_=ps[0:NR, bank, :]),
                    r=[PK(bank)], w=[k_mrow])
            for ch in range(72):
                add("pe", lambda e, ch=ch: e.transpose(ps[:, 2, ch * NR:(ch + 1) * NR], mrow[:, ch * 128:(ch + 1) * 128],
                                                        ident[0:NR, 0:NR]), r=[k_mrow, k_ident], w=[PK(2)])
            add("dve", lambda e: e.tensor_copy(out=modT, in_=ps[:, 2, 0:72 * NR].rearrange("p (a b) -> p a b", a=72)),
                r=[PK(2)], w=[k_modT])
            for i in range(3):
                add("dve", lambda e, i=i: e.tensor_scalar_add(out=Gv[:, i, :, :],
                                                             in0=modT[:, (3 * i + 1) * NCH:(3 * i + 2) * NCH, :], scalar1=1.0),
                    r=[k_modT], w=[k_Gv])
                for r in range(NR):
                    add("dve", lambda e, i=i, r=r, l=l: e.tensor_tensor(out=Gv[:, i, :, r], in0=Gv[:, i, :, r],
                                                                  in1=normv[:, i, l, :], op=ALU.mult),
                        r=[k_Gv, k_normv], w=[k_Gv])
                add("dve", lambda e, i=i: e.tensor_scalar_mul(out=gatev[:, i, :, :],
                                                             in0=modT[:, (3 * i + 2) * NCH:(3 * i + 3) * NCH, :],
                                                             scalar1=(1.0 if i == 1 else 0.5)),
                    r=[k_modT], w=[k_gatev])
            S.barrier()

            A.off = base_mark
            wi, kwi = A.alloc([128, NCH, 2 * DFF], BF16, "wi")
            wo, kwo = A.alloc([128, FCH, D], BF16, "wo")
            actb = [A.alloc([128, T], BF16, "actb%d" % i) for i in range(8)]
            ffn_mark = A.off
            load_ffn_weights(l, 0, wi, kwi, wo, kwo)
            tiles = [(b, t) for b in range(NB) for t in range(NTL)]

            def ffn_phase(i, tl, epilogue):
                def pro(n):
                    b, t = tl[n]
                    stl, kst = stile[n % 2]
                    r = NB if t == NTL - 1 else b
                    add("sp", lambda e, stl=stl, b=b, t=t: e.dma_start(out=stl, in_=sT_tile(b, t)),
                        r=[("sT", b, t)], w=[kst], dma="ld_s%d" % (n % 2))
                    hb, khb = hT_l[n % 2]
                    yield from norm_mod_g(stl, kst, i, r, hb, khb, bs=n % 2, bank=7,
                                          gain=lambda c: Gv[:, i, c, r:r + 1],
                                          bias=lambda c: modT[:, (3 * i) * NCH + c, r:r + 1])
                for _ in pro(0):
                    pass
                for n, (b, t) in enumerate(tl):
                    stl, kst = stile[n % 2]
                    r = NB if t == NTL - 1 else b
                    ffn_main(stl, kst, i, r, n % 2, wi, kwi, wo, kwo, actb,
                             hook=pro(n + 1) if n + 1 < len(tl) else None)
                    epilogue(n, b, t, stl, kst)

            def store_s(n, b, t, stl, kst):
                add("sp", lambda e, stl=stl, b=b, t=t: e.dma_start(out=sT_tile(b, t), in_=stl),
                    r=[kst], w=[("sT", b, t)], dma="st_s%d" % (n % 2))

            ffn_phase(0, tiles, store_s)
            S.barrier()

            p2(nc, S, A, base_mark, l, locals())
            S.barrier()
            p3(nc, S, A, base_mark, l, locals())
            S.barrier()
            p4(nc, S, A, base_mark, l, locals())
            S.barrier()

            A.off = ffn_mark
            wom, kwom = A.alloc([128, NCH, D], BF16, "wom")
            if last:
                _m = A.off
                yo = [A.alloc([128, D], F32, "yo0")] * 2
                A.off = _m
            mixt = [A.alloc([128, NCH, T], BF16, "mixt%d" % i) for i in range(2)]
            for c in range(NCH):
                add("pool", lambda e, c=c, l=l: e.dma_start(out=wom[:, c, :], in_=w_out[l, c * 128:(c + 1) * 128, :]),
                    w=[(kwom, c)], dma="w%d" % (c % 6))
            load_ffn_weights(l, 1, wi, kwi, wo, kwo)
            tl5 = [(b, t) for (b, t) in tiles if not (last and t == NTL - 1)]
            def loads5(n, b, t):
                stl, kst = stile[n % 2]
                mx, kmx = mixt[n % 2]
                add("sp", lambda e: e.dma_start(out=stl, in_=sT_tile(b, t)),
                    r=[("sT", b, t)], w=[kst], dma="ld_s%d" % (n % 2))
                add("sp", lambda e: e.dma_start(
                    out=mx, in_=MIXd[b].rearrange("(c p) t -> p c t", p=128)[:, :, t * T:(t + 1) * T]),
                    r=[("MIX", b)], w=[kmx], dma="ld_m%d" % (n % 2))

            loads5(0, *tl5[0])
            for n, (b, t) in enumerate(tl5):
                stl, kst = stile[n % 2]
                mx, kmx = mixt[n % 2]
                r = NB if t == NTL - 1 else b
                bk0 = 4 * (n % 2)
                if n + 1 < len(tl5):
                    loads5(n + 1, *tl5[n + 1])
                for c in range(NCH):
                    for kc in range(NCH):
                        add("pe", lambda e, c=c, kc=kc, mx=mx, bk0=bk0: e.matmul(
                            ps[:, bk0 + c // 2, (c % 2) * T:(c % 2 + 1) * T], wom[:, kc, c * 128:(c + 1) * 128], mx[:, kc, :],
                            start=(kc == 0), stop=(kc == NCH - 1), skip_group_check=True),
                            r=[(kwom, kc), kmx], w=[PK(bk0 + c // 2)])
                for c in range(NCH):
                    add("dve", lambda e, c=c, stl=stl, r=r, bk0=bk0: e.scalar_tensor_tensor(
                        out=stl[:, c, :], in0=ps[:, bk0 + c // 2, (c % 2) * T:(c % 2 + 1) * T], scalar=gatev[:, 1, c, r:r + 1],
                        in1=stl[:, c, :], op0=ALU.mult, op1=ALU.add),
                        r=[PK(bk0 + c // 2), k_gatev, kst], w=[kst])
                add("sp", lambda e, stl=stl, b=b, t=t: e.dma_start(out=sT_tile(b, t), in_=stl),
                    r=[kst], w=[("sT", b, t)], dma="st_s%d" % (n % 2))

            if last:
                S.barrier()

            def final_out(n, b, t, stl, kst):
                rstd, k_rstd = rstd_l[n % 2]
                sqb, k_sqb = sqb_l[n % 2]
                tn, k_tn = tn_l[n % 2]
                add("act", lambda e: e.activation(out=sqb, in_=stl, func=AF.Square), r=[kst], w=[k_sqb])
                for c in range(NCH):
                    add("pe", lambda e, c=c: e.matmul(ps[:, 7, 0:T], ones_d[:, 0, :], sqb[:, c, :],
                                                      start=(c == 0), stop=(c == NCH - 1)),
                        r=[k_sqb, k_ones], w=[PK(7)])
                add("act", lambda e: e.activation(out=rstd, in_=ps[:, 7, 0:T], func=AF.Sqrt, bias=epst[:, 0:1], scale=1.0),
                    r=[PK(7), k_eps], w=[k_rstd])
                add("dve", lambda e: e.reciprocal(out=rstd, in_=rstd), r=[k_rstd], w=[k_rstd])
                for c in range(NCH):
                    add("dve", lambda e, c=c: e.scalar_tensor_tensor(
                        out=tn[:, c, :], in0=stl[:, c, :], scalar=fnormv[:, c:c + 1], in1=rstd,
                        op0=ALU.mult, op1=ALU.mult), r=[kst, k_rstd, k_fnormv], w=[k_tn + str(c)])
                for sub in range(T // 128):
                    yt, kyt = yo[sub]
                    for half in range(2):
                        bank = 5 + half
                        for q in range(4):
                            c = half * 4 + q
                            add("pe", lambda e, c=c, q=q, bank=bank, sub=sub: e.transpose(
                                ps[:, bank, q * 128:(q + 1) * 128], tn[:, c, sub * 128:(sub + 1) * 128], ident),
                                r=[k_tn + str(c), k_ident], w=[PK(bank)])
                        if half == 0:
                            add("act", lambda e, yt=yt, bank=bank: e.copy(out=yt[:, 0:512], in_=ps[:, bank, :]),
                                r=[PK(bank)], w=[kyt + "a"])
                        else:
                            add("dve", lambda e, yt=yt, bank=bank: e.tensor_copy(out=yt[:, 512:1024], in_=ps[:, bank, :]),
                                r=[PK(bank)], w=[kyt + "b"])
                    tok = t * T + sub * 128
                    add("sp", lambda e, yt=yt, b=b, tok=tok: e.dma_start(out=y_out[b, tok:tok + 128, :], in_=yt),
                        r=[kyt + "a", kyt + "b"], w=[("y", b, tok)], dma="yo%d" % sub)

            ffn_phase(2, tl5, final_out if last else store_s)
            S.barrier()
        S.fence("sp", [("y", b, tok) for b in range(NB) for tok in range(0, SEQ, 128)])
        S.emit(nc, st)
    return nc


def p2(nc, S, A, base_mark, l, env):
    g = env
    add = S.add
    NB, NT, NTL, NR, L = g["NB"], g["NT"], g["NTL"], g["NR"], g["L"]
    ps, PK = g["ps"], g["PK"]
    w_in, w_uq, w_ukv = g["w_in"], g["w_uq"], g["w_ukv"]
    stile, hT, k_hT = g["stile"], g["hT"], g["k_hT"]
    norm_mod, norm_mod_g, sT_tile = g["norm_mod"], g["norm_mod_g"], g["sT_tile"]
    Gv, modT, qnv, kvnv = g["Gv"], g["modT"], g["qnv"], g["kvnv"]
    A.off = base_mark
    wmx, kwmx = A.alloc([128, NCH, IN_COLS], BF16, "wmx")
    wkr, kwkr = A.alloc([128, NCH, 2, 96], BF16, "wkr")
    wuq, kwuq = A.alloc([128, 3, 768], BF16, "wuq")
    wuqr, kwuqr = A.alloc([128, 3, H, 96], BF16, "wuqr")
    wukv, kwukv = A.alloc([128, 2, 1024], BF16, "wukv")
    csm, kcsm = A.alloc([128, 2, 512], BF16, "csm")
    cosS = [A.alloc([96, T], F32, "cosS%d" % i) for i in range(2)]
    sinS = [A.alloc([96, T], F32, "sinS%d" % i) for i in range(2)]
    def alloc_set(i):
        d = {}
        d['zT'] = A.alloc([128, 2, T], BF16, "zT%d" % i)
        d['Usb'] = A.alloc([128, 2, 512], BF16, "Usb%d" % i)
        d['cb_sb'] = A.alloc([128, 2, T], F32, "cb_sb%d" % i)
        d['cc_sb'] = A.alloc([128, 2, T], F32, "cc_sb%d" % i)
        d['u_sb'] = A.alloc([128, 2, T], F32, "u_sb%d" % i)
        d['ql'] = A.alloc([128, 3, T], F32, "ql%d" % i)
        d['qn'] = A.alloc([128, 3, T], BF16, "qn%d" % i)
        d['kvl'] = A.alloc([128, 2, T], F32, "kvl%d" % i)
        d['kvn'] = A.alloc([128, 2, T], BF16, "kvn%d" % i)
        d['t1'] = A.alloc([96, 2, T], F32, "t1%d" % i)
        d['t2'] = A.alloc([96, 2, T], F32, "t2%d" % i)
        d['krf'] = A.alloc([96, T], BF16, "krf%d" % i)
        d['Qsb'] = A.alloc([96, H, T], BF16, "Qsb%d" % i)
        d['Ksb'] = A.alloc([96, H, T], BF16, "Ksb%d" % i)
        d['Vsb'] = A.alloc([128, 2, H, 65], BF16, "Vsb%d" % i)
        return d
    sets = [alloc_set(0), alloc_set(1)]
    scr = [(A.alloc([128, T], F32, "p2rstd%d" % i), A.alloc([128, NCH, T], BF16, "p2sqb%d" % i),
            A.alloc([128, NCH, T], F32, "p2tn%d" % i)) for i in range(2)]

    for c in range(NCH):
        add("pool", lambda e, c=c: e.dma_start(out=wmx[:, c, :], in_=w_in[l, c * 128:(c + 1) * 128, :]),
            w=[kwmx], dma="w%d" % (c % 6))
    add("pool", lambda e: e.memset(wkr, 0.0), w=[kwkr])
    add("pool", lambda e: e.memset(wuqr, 0.0), w=[kwuqr])
    for _d in sets:
        add("pool", lambda e, _d=_d: e.memset(_d["Vsb"][0], 1.0), w=[_d["Vsb"][1]])
    win_v = w_in[l].rearrange("(c p) n -> p c n", p=128)
    for c in range(NCH):
        add("pool", lambda e, c=c: e.dma_start(out=wkr[:, c, 0, 64:96], in_=win_v[:, c, OFF_KR:OFF_KR + 32]),
            w=[kwkr], dma="w0")
        for ax in range(2):
            for half in range(2):
                d0 = ax * 16 + half * 8
                p0 = ax * 16 + (1 - half) * 8
                add("pool", lambda e, c=c, d0=d0, p0=p0: e.dma_start(
                    out=wkr[:, c, 1, 64 + d0:64 + d0 + 8], in_=win_v[:, c, OFF_KR + p0:OFF_KR + p0 + 8]),
                    w=[kwkr], dma="w1")
    wuq_v = w_uq[l].rearrange("(c p) n -> p c n", p=128)
    for c in range(3):
        add("pool", lambda e, c=c: e.dma_start(out=wuq[:, c, :], in_=wuq_v[:, c, :]), w=[kwuq], dma="w2")
        for ax in range(2):
            for half in range(2):
                d0 = ax * 16 + half * 8
                p0 = ax * 16 + (1 - half) * 8
                add("pool", lambda e, c=c, d0=d0, p0=p0: e.dma_start(
                    out=wuqr[:, c, :, 64 + d0:64 + d0 + 8],
                    in_=wuq_v[:, c, :].rearrange("p (h d) -> p h d", h=H)[:, :, 64 + p0:64 + p0 + 8]),
                    w=[kwuqr], dma="w3")
    for c in range(2):
        add("pool", lambda e, c=c: e.dma_start(out=wukv[:, c, :], in_=w_ukv[l, c * 128:(c + 1) * 128, :]),
            w=[kwukv], dma="w4")
        add("pool", lambda e, c=c: e.dma_start(out=csm[:, c, :], in_=g["cs_in"][c * 128:(c + 1) * 128, :]),
            w=[kcsm], dma="w5")

    def do_tile(n, b, t, BS):
        zT, kzT = BS['zT']
        Usb, kUsb = BS['Usb']
        cb_sb, kcb = BS['cb_sb']
        cc_sb, kcc = BS['cc_sb']
        u_sb, ku = BS['u_sb']
        ql, kql = BS['ql']
        qn, kqn = BS['qn']
        kvl, kkvl = BS['kvl']
        kvn, kkvn = BS['kvn']
        t1, kt1 = BS['t1']
        t2, kt2 = BS['t2']
        krf, kkrf = BS['krf']
        Qsb, kQ = BS['Qsb']
        Ksb, kK = BS['Ksb']
        Vsb, kV = BS['Vsb']
        hT, k_hT = g["hT_l"][n % 2]
        SC = scr[n % 2]

        def BK(k):
            return (k + 4 * (n % 2)) % 8

        def proj(lhs_fn, nout, bank, M=128, keys=()):
            for oc in range(nout):
                for kc in range(NCH):
                    add("pe", lambda e, oc=oc, kc=kc: e.matmul(ps[0:M, bank, oc * T:(oc + 1) * T], lhs_fn(kc, oc), hT[:, kc, :],
                                                               start=(kc == 0), stop=(kc == NCH - 1), skip_group_check=True),
                        r=[k_hT] + list(keys), w=[PK(bank)])


        stl, kst = stile[n % 2]
        cs_t, kcs = cosS[n % 2]
        sn_t, ksn = sinS[n % 2]
        r = NB if t == NTL - 1 else b
        tok = t * T
        yield from norm_mod_g(stl, kst, 1, r, hT, k_hT, gain=lambda c: Gv[:, 1, c, r:r + 1],
                               bias=lambda c: modT[:, 3 * NCH + c, r:r + 1], bank=BK(7), scratch=SC)
        proj(lambda kc, oc: wmx[:, kc, oc * 128:(oc + 1) * 128], 2, BK(0), keys=[kwmx])
        add("act", lambda e: e.copy(out=zT, in_=ps[:, BK(0), :].rearrange("p (a b) -> p a b", a=2)), r=[PK(BK(0))], w=[kzT])
        yield
        for sub in range(2):
            for kc in range(2):
                add("pe", lambda e, sub=sub, kc=kc: e.matmul(ps[:, BK(1 + sub), :], zT[:, kc, sub * 128:(sub + 1) * 128],
                                                             csm[:, kc, :], start=(kc == 0), stop=(kc == 1)),
                    r=[kzT, kcsm], w=[PK(BK(1 + sub))])
            if sub == 0:
                add("act", lambda e: e.copy(out=Usb[:, 0, :], in_=ps[:, BK(1), :]), r=[PK(BK(1))], w=[kUsb + "0"])
            else:
                add("dve", lambda e: e.tensor_copy(out=Usb[:, 1, :], in_=ps[:, BK(2), :]), r=[PK(BK(2))], w=[kUsb + "1"])
            add("sp", lambda e, sub=sub, b=b, tok=tok: e.dma_start(
                out=g["Ud"][b, :, tok + sub * 128:tok + (sub + 1) * 128, :].rearrange("r t c -> t r c"),
                in_=Usb[:, sub, :].rearrange("p (r c) -> p r c", r=2)),
                r=[kUsb + str(sub)], w=[("Ud", b)], dma="st_u%d" % sub)
        yield
        proj(lambda kc, oc: wmx[:, kc, OFF_CB + oc * 128:OFF_CB + (oc + 1) * 128], 2, BK(3), keys=[kwmx])
        add("act", lambda e: e.copy(out=cb_sb, in_=ps[:, BK(3), :].rearrange("p (a b) -> p a b", a=2)), r=[PK(BK(3))], w=[kcb])
        add("sp", lambda e, b=b, tok=tok: e.dma_start(
            out=g["CVd"][b, 0].rearrange("(c p) t -> p c t", p=128)[:, :, tok:tok + T], in_=cb_sb),
            r=[kcb], w=[("CV", b)], dma="st_cb")
        proj(lambda kc, oc: wmx[:, kc, OFF_CC + oc * 128:OFF_CC + (oc + 1) * 128], 2, BK(4), keys=[kwmx])
        add("act", lambda e: e.copy(out=cc_sb, in_=ps[:, BK(4), :].rearrange("p (a b) -> p a b", a=2)), r=[PK(BK(4))], w=[kcc])
        yield
        proj(lambda kc, oc: wmx[:, kc, OFF_CX + oc * 128:OFF_CX + (oc + 1) * 128], 2, BK(5), keys=[kwmx])
        add("dve", lambda e: e.tensor_tensor(out=u_sb, in0=cc_sb, in1=ps[:, BK(5), :].rearrange("p (a b) -> p a b", a=2),
                                             op=ALU.mult), r=[PK(BK(5)), kcc], w=[ku])
        add("sp", lambda e, b=b, tok=tok: e.dma_start(
            out=g["CVd"][b, 1].rearrange("(c p) t -> p c t", p=128)[:, :, tok:tok + T], in_=u_sb),
            r=[ku], w=[("CV", b)], dma="st_u")
        yield
        proj(lambda kc, oc: wmx[:, kc, OFF_Q + oc * 128:OFF_Q + (oc + 1) * 128], 2, BK(6), keys=[kwmx])
        add("act", lambda e: e.copy(out=ql[:, 0:2, :], in_=ps[:, BK(6), :].rearrange("p (a b) -> p a b", a=2)),
            r=[PK(BK(6))], w=[kql])
        proj(lambda kc, oc: wmx[:, kc, OFF_Q + 256:OFF_Q + 384], 1, BK(7), keys=[kwmx])
        add("act", lambda e: e.copy(out=ql[:, 2, :], in_=ps[:, BK(7), 0:T]), r=[PK(BK(7))], w=[kql])
        yield
        proj(lambda kc, oc: wmx[:, kc, OFF_KV + oc * 128:OFF_KV + (oc + 1) * 128], 2, BK(0), keys=[kwmx])
        add("act", lambda e: e.copy(out=kvl, in_=ps[:, BK(0), :].rearrange("p (a b) -> p a b", a=2)), r=[PK(BK(0))], w=[kkvl])
        proj(lambda kc, oc: wkr[:, kc, oc, :], 2, BK(1), M=96, keys=[kwkr])
        add("dve", lambda e, cs_t=cs_t: e.tensor_tensor(out=t1[:, 0, :], in0=ps[0:96, BK(1), 0:T], in1=cs_t, op=ALU.mult),
            r=[PK(BK(1)), kcs], w=[kt1])
        add("dve", lambda e, sn_t=sn_t: e.tensor_tensor(out=t2[:, 0, :], in0=ps[0:96, BK(1), T:2 * T], in1=sn_t, op=ALU.mult),
            r=[PK(BK(1)), ksn], w=[kt2])
        add("pool", lambda e: e.tensor_tensor(out=krf, in0=t1[:, 0, :], in1=t2[:, 0, :], op=ALU.add),
            r=[kt1, kt2], w=[kkrf])
        yield
        yield from norm_mod_g(kvl, kkvl, 1, r, kvn, kkvn, nchunk=2, ones_i=2, gain=lambda c: kvnv[:, l, c:c + 1], bank=BK(2), scratch=SC)
        yield from norm_mod_g(ql, kql, 1, r, qn, kqn, nchunk=3, ones_i=1, gain=lambda c: qnv[:, l, c:c + 1], bank=BK(2), scratch=SC)
        for hp in range(4):
            bank = BK(3 + (hp % 2) * 2)
            for hh in range(2):
                h = 2 * hp + hh
                for kc in range(2):
                    add("pe", lambda e, h=h, hh=hh, kc=kc, bank=bank: e.matmul(
                        ps[0:64, bank, hh * T:(hh + 1) * T], wukv[:, kc, h * 128:h * 128 + 64], kvn[:, kc, :],
                        start=(kc == 0), stop=(kc == 1), skip_group_check=True), r=[kwukv, kkvn], w=[PK(bank)])
            add("act", lambda e, hp=hp, bank=bank: e.copy(
                out=Ksb[0:64, 2 * hp:2 * hp + 2, :], in_=ps[0:64, bank, :].rearrange("p (a b) -> p a b", a=2)),
                r=[PK(bank)], w=[kK + "n"])
            yield
        add("act", lambda e: e.copy(out=Ksb[64:96, :, :], in_=krf[64:96, :].unsqueeze(1).to_broadcast([32, H, T])),
            r=[kkrf], w=[kK + "r"])
        add("sp", lambda e, b=b, tok=tok: e.dma_start(
            out=g["KTd"][b, :, :, tok:tok + T].rearrange("h d t -> d h t"), in_=Ksb),
            r=[kK + "n", kK + "r"], w=[("KT", b)], dma="st_k")
        for sub in range(2):
            bank = BK(6 + sub)
            for kc in range(2):
                add("pe", lambda e, sub=sub, kc=kc, bank=bank: e.matmul(
                    ps[:, bank, :], kvn[:, kc, sub * 128:(sub + 1) * 128],
                    wukv[:, kc, :].rearrange("p (h d) -> p h d", h=H)[:, :, 64:128],
                    start=(kc == 0), stop=(kc == 1)), r=[kwukv, kkvn], w=[PK(bank)])
            add("act", lambda e, sub=sub, bank=bank: e.copy(
                out=Vsb[:, sub, :, 0:64], in_=ps[:, bank, :].rearrange("p (h d) -> p h d", h=H)),
                r=[PK(bank)], w=[kV + str(sub)])
            add("sp", lambda e, sub=sub, b=b, tok=tok: e.dma_start(
                out=g["Vd"][b, tok + sub * 128:tok + (sub + 1) * 128, :],
                in_=Vsb[:, sub, :, :].rearrange("p h d -> p (h d)")),
                r=[kV, kV + str(sub)], w=[("V", b)], dma="st_v%d" % sub)
            yield
        for hp in range(4):
            ba, bb = BK(3 + 2 * (hp % 2)), BK(4 + 2 * (hp % 2))
            for hh in range(2):
                h = 2 * hp + hh
                for kc in range(3):
                    add("pe", lambda e, h=h, hh=hh, kc=kc, ba=ba: e.matmul(
                        ps[0:96, ba, hh * T:(hh + 1) * T], wuq[:, kc, h * 96:(h + 1) * 96], qn[:, kc, :],
                        start=(kc == 0), stop=(kc == 2), skip_group_check=True), r=[kwuq, kqn], w=[PK(ba)])
                for kc in range(3):
                    add("pe", lambda e, h=h, hh=hh, kc=kc, bb=bb: e.matmul(
                        ps[0:96, bb, hh * T:(hh + 1) * T], wuqr[:, kc, h, :], qn[:, kc, :],
                        start=(kc == 0), stop=(kc == 2), skip_group_check=True), r=[kwuqr, kqn], w=[PK(bb)])
            add("dve", lambda e, ba=ba, cs_t=cs_t: e.tensor_tensor(
                out=t1, in0=ps[0:96, ba, :].rearrange("p (a b) -> p a b", a=2),
                in1=cs_t.unsqueeze(1).to_broadcast([96, 2, T]), op=ALU.mult), r=[PK(ba), kcs], w=[kt1])
            add("dve", lambda e, bb=bb, sn_t=sn_t: e.tensor_tensor(
                out=t2, in0=ps[0:96, bb, :].rearrange("p (a b) -> p a b", a=2),
                in1=sn_t.unsqueeze(1).to_broadcast([96, 2, T]), op=ALU.mult), r=[PK(bb), ksn], w=[kt2])
            add("pool", lambda e, hp=hp: e.tensor_tensor(out=Qsb[:, 2 * hp:2 * hp + 2, :], in0=t1, in1=t2, op=ALU.add),
                r=[kt1, kt2], w=[kQ])
            yield
        add("sp", lambda e, b=b, tok=tok: e.dma_start(
            out=g["QTd"][b, :, :, tok:tok + T].rearrange("h d t -> d h t"), in_=Qsb),
            r=[kQ], w=[("QT", b)], dma="st_q")


    def loads(n, b, t, what="sc"):
        stl, kst = stile[n % 2]
        cs_t, kcs = cosS[n % 2]
        sn_t, ksn = sinS[n % 2]
        tok = t * T
        if "s" in what:
            add("sp", lambda e: e.dma_start(out=stl, in_=sT_tile(b, t)),
                r=[("sT", b, t)], w=[kst], dma="ld_s%d" % (n % 2))
        if "c" in what:
            add("sp", lambda e: e.dma_start(out=cs_t, in_=g["cos_in"][:, tok:tok + T]),
                w=[kcs], dma="ld_c%d" % (n % 2))
            add("sp", lambda e: e.dma_start(out=sn_t, in_=g["sin_in"][:, tok:tok + T]),
                w=[ksn], dma="ld_n%d" % (n % 2))

    tl = [(b, t) for b in range(NB) for t in range(NTL)]
    loads(0, *tl[0])
    if len(tl) > 1:
        loads(1, *tl[1])
    for n0 in range(0, len(tl), 2):
        pair = list(range(n0, min(n0 + 2, len(tl))))
        gens = [do_tile(n, tl[n][0], tl[n][1], sets[n % 2]) for n in pair]
        rounds = 0
        while gens:
            for gi_ in list(gens):
                try:
                    next(gi_)
                except StopIteration:
                    gens.remove(gi_)
            rounds += 1
            if rounds == 10:
                for n in (n0 + 2, n0 + 3):
                    if n < len(tl):
                        loads(n, *tl[n], what="s")
        for n in (n0 + 2, n0 + 3):
            if n < len(tl):
                loads(n, *tl[n], what="c")


def p3(nc, S, A, base_mark, l, env):
    g = env
    add = S.add
    NB, NT, NTL, SEQ, CTX = g["NB"], g["NT"], g["NTL"], g["SEQ"], g["CTX"]
    ps, PK, onesf, k_onesf = g["ps"], g["PK"], g["onesf"], g["k_onesf"]
    A.off = base_mark
    NKC = NT // 128
    TQ = 512 if SEQ % 512 == 0 else 256
    Kall, kKall = A.alloc([96, H, NT], BF16, "Kall")
    Vall, kVall = A.alloc([128, NKC, H * 65], BF16, "Vall")
    Qt = [A.alloc([96, H, TQ], BF16, "Qt%d" % i) for i in range(2)]
    Pb = [A.alloc([128, 2, TQ], BF16, "Pb%d" % i) for i in range(2)]
    rd, krd = A.alloc([128, TQ], F32, "rd")
    bcs = [A.alloc([64, TQ], F32, "bcs%d" % i) for i in range(2)]
    Osb = [A.alloc([64, H, TQ], BF16, "Osb%d" % i) for i in range(2)]
    n = 0
    hc = 0
    gcount = 0
    for b in range(NB):
        for h in range(H):
            add("sp", lambda e, b=b, h=h: e.dma_start(out=Kall[:, h, :], in_=g["KTd"][b, h]),
                r=[("KT", b)], w=[kKall], dma="ld_k%d" % (h % 4))
        for c0 in range(0, NKC, 6):
            c1 = min(NKC, c0 + 6)
            add("sp", lambda e, b=b, c0=c0, c1=c1: e.dma_start(
                out=Vall[:, c0:c1, :], in_=g["Vd"][b].rearrange("(c p) n -> p c n", p=128)[:, c0:c1, :]),
                r=[("V", b)], w=[kVall], dma="ld_v")
        qtiles = [(tok, TQ, list(range(NKC))) for tok in range(0, SEQ, TQ)]
        if not (g["last"] and not DBG_CTX):
            qtiles.append((SEQ, CTX, list(range(SEQ // 128, NKC))))
        def loadq(nn, tok, W, b=b):
            qt, kqt = Qt[nn % 2]
            add("sp", lambda e: e.dma_start(
                out=qt[:, :, 0:W], in_=g["QTd"][b, :, :, tok:tok + W].rearrange("h d t -> d h t")),
                r=[("QT", b)], w=[kqt], dma="ld_q%d" % (nn % 2))

        loadq(n, qtiles[0][0], qtiles[0][1])
        for qi, (tok, W, kcs) in enumerate(qtiles):
            qt, kqt = Qt[n % 2]
            ob, kob = Osb[n % 2]
            if qi + 1 < len(qtiles):
                loadq(n + 1, qtiles[qi + 1][0], qtiles[qi + 1][1])
            groups = [kcs[i:i + 2] for i in range(0, len(kcs), 2)]
            work = [(h, gi) for h in range(H) for gi in range(len(groups))]
            slot_of = {}
            deferred = []

            def emitS(idx):
                nonlocal gcount
                h, gi = work[idx]
                slot = gcount % 2
                gcount += 1
                slot_of[idx] = slot
                for q, kc in enumerate(groups[gi]):
                    add("pe", lambda e, h=h, kc=kc, slot=slot, q=q, qt=qt, W=W: e.matmul(
                        ps[:, 2 * slot + q, 0:W], Kall[:, h, kc * 128:(kc + 1) * 128], qt[:, h, 0:W],
                        start=True, stop=True), r=[kKall, kqt], w=[PK(2 * slot + q)])

            def emitExpPV(idx):
                h, gi = work[idx]
                slot = slot_of[idx]
                ng = len(groups[gi])
                pb, kpb = Pb[slot]
                obank = 4 + (hc + h) % 2
                add("act", lambda e, slot=slot, pb=pb, ng=ng, W=W: e.activation(
                    out=pb[:, 0:ng, 0:W], in_=ps[:, 2 * slot:2 * slot + ng, 0:W], func=AF.Exp, scale=ATTN_SCALE),
                    r=[PK(2 * slot + q) for q in range(ng)], w=[kpb])
                for q, kc in enumerate(groups[gi]):
                    first = gi == 0 and q == 0
                    lastmm = gi == len(groups) - 1 and q == ng - 1
                    add("pe", lambda e, h=h, kc=kc, pb=pb, q=q, obank=obank, first=first, lastmm=lastmm, W=W: e.matmul(
                        ps[0:65, obank, 0:W], Vall[:, kc, h * 65:(h + 1) * 65], pb[:, q, 0:W], start=first, stop=lastmm),
                        r=[kVall, kpb], w=[PK(obank)])

            def epilogue_a(h):
                obank = 4 + (hc + h) % 2
                add("dve", lambda e, obank=obank, W=W: e.reciprocal(out=rd[64:65, 0:W], in_=ps[64:65, obank, 0:W]),
                    r=[PK(obank)], w=[krd])

            def epilogue_b(h):
                obank = 4 + (hc + h) % 2
                bb = 6 + (hc + h) % 2
                bc, kbc = bcs[(hc + h) % 2]
                add("pe", lambda e, bb=bb, W=W: e.matmul(ps[0:64, bb, 0:W], onesf[64:65, 0:64], rd[64:65, 0:W],
                                                         start=True, stop=True), r=[krd, k_onesf], w=[PK(bb)])
                add("dve", lambda e, bb=bb, bc=bc, W=W: e.tensor_copy(out=bc[:, 0:W], in_=ps[0:64, bb, 0:W]),
                    r=[PK(bb)], w=[kbc])
                add("dve", lambda e, h=h, ob=ob, obank=obank, bc=bc, W=W: e.tensor_tensor(
                    out=ob[:, h, 0:W], in0=ps[0:64, obank, 0:W], in1=bc[:, 0:W], op=ALU.mult),
                    r=[PK(obank), kbc], w=[kob])

            emitS(0)
            for idx in range(len(work)):
                if idx + 1 < len(work):
                    emitS(idx + 1)
                emitExpPV(idx)
                h, gi = work[idx]
                if deferred and gi == min(5, len(groups) - 1):
                    epilogue_b(deferred.pop(0))
                if gi == len(groups) - 1:
                    epilogue_a(h)
                    deferred.append(h)
            while deferred:
                epilogue_b(deferred.pop(0))
            hc += H
            add("sp", lambda e, ob=ob, b=b, tok=tok, W=W: e.dma_start(
                out=g["MIXd"][b, 512:1024, tok:tok + W].rearrange("(h d) t -> d h t", h=H), in_=ob[:, :, 0:W]),
                r=[kob], w=[("MIX", b)], dma="st_o%d" % (n % 2))
            n += 1


def p4(nc, S, A, base_mark, l, env):
    g = env
    add = S.add
    NB, NT, NTL = g["NB"], g["NT"], g["NTL"]
    ps, PK, cwv, k_cwv = g["ps"], g["PK"], g["cwv"], g["k_cwv"]
    it = 0
    for si, (t0, N, R) in enumerate(g["segs"]):
        if g["last"] and not DBG_CTX and si == 1:
            continue
        if si > 0:
            S.barrier()
        A.off = base_mark
        m1, km1 = A.alloc([2 * R, 2 * R], BF16, "m1")
        G2, kG2 = A.alloc([128, N], BF16, "G2")
        D2, kD2 = A.alloc([2 * R, 64 * 256], BF16, "D2")
        Y1, kY1 = A.alloc([2 * R, 64 * 256], BF16, "Y1")
        Y2, kY2 = A.alloc([128, R, 256], BF16, "Y2")
        Fo, kFo = A.alloc([128, 2, N], BF16, "Fo")
        add("pool", lambda e, m1=m1, si=si: e.dma_start(out=m1, in_=g["m1_in"][si]), w=[km1], dma="w0")
        add("pool", lambda e, G2=G2, si=si: e.dma_start(out=G2, in_=g["g2_in"][si]), w=[kG2], dma="w1")
        for b in range(NB):
            for ri in range(2):
                add("sp", lambda e, b=b, ri=ri, D2=D2, R=R, t0=t0, N=N: e.dma_start(
                    out=D2[ri * R:(ri + 1) * R, :],
                    in_=g["Ud"][b, ri, t0:t0 + N, :].rearrange("(a n) c -> a (n c)", a=R)),
                    r=[("Ud", b)], w=[kD2], dma="ld_d%d" % ri)
            for j in range(32):
                bank = j % 4
                add("pe", lambda e, j=j, bank=bank, m1=m1, D2=D2, R=R: e.matmul(
                    ps[0:2 * R, bank, :], m1, D2[:, j * 512:(j + 1) * 512], start=True, stop=True),
                    r=[km1, kD2], w=[PK(bank)])
                if j % 2 == 0:
                    add("act", lambda e, j=j, bank=bank, Y1=Y1, R=R: e.copy(out=Y1[:, j * 512:(j + 1) * 512],
                                                                            in_=ps[0:2 * R, bank, :]),
                        r=[PK(bank)], w=[kY1 + "a"])
                else:
                    add("dve", lambda e, j=j, bank=bank, Y1=Y1, R=R: e.tensor_copy(out=Y1[:, j * 512:(j + 1) * 512],
                                                                                   in_=ps[0:2 * R, bank, :]),
                        r=[PK(bank)], w=[kY1 + "d"])
            for ri in range(2):
                add("sp", lambda e, b=b, ri=ri, Y1=Y1, R=R, t0=t0, N=N: e.dma_start(
                    out=g["Yd"][b, ri, t0:t0 + N, :].rearrange("(a n) c -> a (n c)", a=R),
                    in_=Y1[ri * R:(ri + 1) * R, :]),
                    r=[kY1 + "a", kY1 + "d"], w=[("Yd", b, si)], dma="st_y%d" % ri)
            for ri in range(2):
                for a0 in range(0, R, 16):
                    a1 = min(R, a0 + 16)
                    add("sp", lambda e, b=b, ri=ri, Y2=Y2, R=R, t0=t0, N=N, a0=a0, a1=a1: e.dma_start(
                        out=Y2[ri * 64:(ri + 1) * 64, a0:a1, :],
                        in_=g["Yd"][b, ri, t0:t0 + N, :].rearrange("(a n) c -> n a c", a=R)[:, a0:a1, :]),
                        r=[("Yd", b, si)], w=[kY2], dma="ld_y%d" % ri)
            ng = (R + 7) // 8
            for cc in range(2):
                for gi in range(ng):
                    nk = min(8, R - gi * 8)
                    bank = 4 + (it % 4)
                    it += 1
                    for kl in range(nk):
                        k1 = gi * 8 + kl
                        add("pe", lambda e, k1=k1, kl=kl, cc=cc, bank=bank, Y2=Y2, G2=G2: e.matmul(
                            ps[:, bank, kl * 64:(kl + 1) * 64], Y2[:, k1, cc * 128:(cc + 1) * 128],
                            G2[:, k1 * 64:(k1 + 1) * 64], start=True, stop=True, skip_group_check=True),
                            r=[kY2, kG2], w=[PK(bank)])
                    add("dve", lambda e, cc=cc, gi=gi, nk=nk, bank=bank, Fo=Fo, R=R: e.tensor_copy(
                        out=Fo[:, cc, :].rearrange("p (k2 k1) -> p k1 k2", k1=R)[:, gi * 8:gi * 8 + nk, :],
                        in_=ps[:, bank, 0:nk * 64].rearrange("p (a b) -> p a b", a=nk)),
                        r=[PK(bank)], w=[kFo])
            add("sp", lambda e, b=b, Fo=Fo, t0=t0, N=N: e.dma_start(
                out=g["MIXd"][b, 0:256, t0:t0 + N].rearrange("(c p) t -> p c t", p=128), in_=Fo),
                r=[kFo], w=[("MIX", b)], dma="st_f")
    S.barrier()
    A.off = base_mark
    CW = 1024
    cu = [A.alloc([128, 2, CW + 2], F32, "cu%d" % i) for i in range(2)]
    cg = [A.alloc([128, 2, CW], F32, "cg%d" % i) for i in range(2)]
    cy = [A.alloc([128, 2, CW], F32, "cy%d" % i) for i in range(2)]
    co = [A.alloc([128, 2, CW], BF16, "co%d" % i) for i in range(2)]
    ctl = []
    for b in range(NB):
        for si, (t0, N, R) in enumerate(g["segs"]):
            if g["last"] and not DBG_CTX and si == 1:
                continue
            w = min(CW, N)
            for a in range(t0, t0 + N, w):
                ctl.append((b, t0, N, a, w))

    def cloads(n, b, t0, N, a, w):
        u, kcu = cu[n % 2]
        gb, kcg = cg[n % 2]
        cvu = g["CVd"][b, 1].rearrange("(c p) t -> p c t", p=128)
        cvg = g["CVd"][b, 0].rearrange("(c p) t -> p c t", p=128)
        lo = a - 1 if a > t0 else a
        hi = a + w + 1 if a + w < t0 + N else a + w
        if a == t0:
            add("pool", lambda e: e.memset(u[:, :, 0:1], 0.0), w=[kcu])
        if a + w == t0 + N:
            add("pool", lambda e: e.memset(u[:, :, w + 1:w + 2], 0.0), w=[kcu])
        add("sp", lambda e: e.dma_start(out=u[:, :, 1 - (a - lo):1 + (hi - a)], in_=cvu[:, :, lo:hi]),
            r=[("CV", b)], w=[kcu], dma="ld_cu%d" % (n % 2))
        add("sp", lambda e: e.dma_start(out=gb[:, :, 0:w], in_=cvg[:, :, a:a + w]),
            r=[("CV", b)], w=[kcg], dma="ld_cg%d" % (n % 2))

    def ccompute(n, b, t0, N, a, w):
        u, kcu = cu[n % 2]
        gb, kcg = cg[n % 2]
        yy, kcy = cy[n % 2]
        oo, kco = co[n % 2]
        for c in range(2):
            add("dve", lambda e, c=c: e.tensor_scalar_mul(out=yy[:, c, 0:w], in0=u[:, c, 1:w + 1],
                                                          scalar1=cwv[:, l, 1, c:c + 1]),
                r=[kcu, k_cwv], w=[kcy + str(c)])
            add("dve", lambda e, c=c: e.scalar_tensor_tensor(
                out=yy[:, c, 0:w], in0=u[:, c, 0:w], scalar=cwv[:, l, 0, c:c + 1], in1=yy[:, c, 0:w],
                op0=ALU.mult, op1=ALU.add), r=[kcu, k_cwv, kcy + str(c)], w=[kcy + str(c)])
            add("dve", lambda e, c=c: e.scalar_tensor_tensor(
                out=yy[:, c, 0:w], in0=u[:, c, 2:w + 2], scalar=cwv[:, l, 2, c:c + 1], in1=yy[:, c, 0:w],
                op0=ALU.mult, op1=ALU.add), r=[kcu, k_cwv, kcy + str(c)], w=[kcy + str(c)])
            add("dve", lambda e, c=c: e.tensor_tensor(out=oo[:, c, 0:w], in0=yy[:, c, 0:w], in1=gb[:, c, 0:w], op=ALU.mult),
                r=[kcg, kcy + str(c)], w=[kco + str(c)])
        add("sp", lambda e: e.dma_start(
            out=g["MIXd"][b, 256:512, a:a + w].rearrange("(c p) t -> p c t", p=128), in_=oo[:, :, 0:w]),
            r=[kco + "0", kco + "1"], w=[("MIX", b)], dma="st_cv%d" % (n % 2))

    if ctl:
        cloads(0, *ctl[0])
    for n, tl_ in enumerate(ctl):
        if n + 1 < len(ctl):
            cloads(n + 1, *ctl[n + 1])
        ccompute(n, *tl_)


def _consts(SEQ, CTX):
    NT = SEQ + CTX
    out = {"ident": np.eye(128, dtype=np.float32)}
    cs = np.zeros((256, 512), np.float64)
    jj, cc = np.meshgrid(np.arange(64), np.arange(64), indexing="ij")
    for g in range(4):
        ang = 2 * np.pi * (jj * cc % 64) / 64.0
        cs[g * 64:(g + 1) * 64, g * 64:(g + 1) * 64] = np.cos(ang).T
        cs[g * 64:(g + 1) * 64, 256 + g * 64:256 + (g + 1) * 64] = -np.sin(ang).T
    out["cs"] = cs.astype(np.float32)
    for name, N in (("x", SEQ), ("c", CTX)):
        R = N // 64
        n1, k1 = np.meshgrid(np.arange(R), np.arange(R), indexing="ij")
        ang = 2 * np.pi * (n1 * k1 % R) / R
        Fr, Fi = np.cos(ang), -np.sin(ang)
        m1 = np.zeros((2 * R, 2 * R))
        m1[:R, :R] = Fr
        m1[R:, :R] = -Fi
        m1[:R, R:] = Fi
        m1[R:, R:] = Fr
        out["m1" + name] = m1.astype(np.float32)
        n2 = np.arange(64)[:, None]
        k = np.arange(N)[None, :]
        kk = (np.arange(N) // 64) + R * (np.arange(N) % 64)
        ang = 2 * np.pi * ((n2 * kk[None, :]) % N) / N
        sc = 1.0 / np.sqrt(N * 64.0)
        g2 = np.concatenate([np.cos(ang), np.sin(ang)], axis=0) * sc
        out["g2" + name] = g2.astype(np.float32)
    rows = SEQ // 64
    row = np.repeat(np.arange(rows), 64).astype(np.float32)
    col = np.tile(np.arange(64), rows).astype(np.float32)
    inv = (1.0 / (np.float32(10000.0) ** (np.arange(0, 16, 2, dtype=np.float32) / np.float32(16)))).astype(np.float32)
    cosT = np.ones((96, NT), np.float32)
    sinT = np.zeros((96, NT), np.float32)
    for axis, pos in enumerate((row, col)):
        ang = pos[None, :] * inv[:, None]
        for half in range(2):
            r0 = 64 + axis * 16 + half * 8
            cosT[r0:r0 + 8, :SEQ] = np.cos(ang)
            sinT[r0:r0 + 8, :SEQ] = np.sin(ang) * (-1.0 if half == 0 else 1.0)
    out["cosT"], out["sinT"] = cosT, sinT
    return out


def make_in_maps(inp, SEQ, CTX, NB, L, ncores):
    consts = _consts(SEQ, CTX)
    shared = {k: np.ascontiguousarray(inp[k], dtype=np.float32) for k in (
        "w_mod", "b_mod", "ffn1_norm", "mix_norm", "ffn2_norm", "ffn1_w_in", "ffn2_w_in", "ffn1_w_out",
        "ffn2_w_out", "w_in", "conv_w", "q_norm", "w_uq", "kv_norm", "w_ukv", "w_out", "final_norm")}
    shared.update(consts)
    maps = []
    for i in range(ncores):
        m = dict(shared)
        m["x"] = np.ascontiguousarray(inp["x"][i * NB:(i + 1) * NB])
        m["ctx"] = np.ascontiguousarray(inp["ctx"][i * NB:(i + 1) * NB])
        m["c3"] = np.ascontiguousarray(np.concatenate([inp["c"][i * NB:(i + 1) * NB], inp["c_ctx"][None, :]], axis=0))
        maps.append(m)
    return maps


def kernel(**inputs):
    SEQ, CTX, NB, L, ncores = 4096, 256, 2, 4, 8
    inp = {k: np.asarray(v) for k, v in inputs.items()}
    nc = build(SEQ, CTX, NB, L)
    maps = make_in_maps(inp, SEQ, CTX, NB, L, ncores)
    res = run_bass_kernel_spmd(nc, maps, core_ids=list(range(ncores)))
    return np.concatenate([np.asarray(r["y"]) for r in res.results], axis=0).astype(np.float32)
```
